# Optimizing a Trainium2 kernel written in Bass

```python
import math
import jax, jax.numpy as jnp
from jax import lax
import numpy as np

D_MODEL = 1024
BATCH = 8
SEQ = 4096
DEPTH = 2
DEC_BATCH = 16
DEC_SEQ = 64
PAST_LEN = 4096

CHUNK = 64
N_HEADS = D_MODEL // 128
HEAD_DIM = 64
ATT_DIM = N_HEADS * HEAD_DIM
SSM_DIM = D_MODEL // 4
SSM_GROUP = 16
SSM_GROUPS = SSM_DIM // SSM_GROUP
SSM_STATE = 64
CONV_DIM = D_MODEL // 4
CONV_WIDTH = 3
MIX_DIM = ATT_DIM + SSM_DIM + CONV_DIM
D_FF = 4 * D_MODEL
Q_BLOCK = 128
RMS_EPS = 1e-6
NEG_INF = -1e30
DT_MIN = 0.001
DT_MAX = 0.1
SPLITS = (ATT_DIM, 2 * ATT_DIM, 3 * ATT_DIM, 3 * ATT_DIM + N_HEADS,
          3 * ATT_DIM + N_HEADS + SSM_DIM, 3 * ATT_DIM + N_HEADS + SSM_DIM + CONV_DIM,
          3 * ATT_DIM + N_HEADS + SSM_DIM + 2 * CONV_DIM)
PROJ_DIM = 3 * ATT_DIM + N_HEADS + SSM_DIM + 3 * CONV_DIM

kernel_name = 'hybrid_fox_s5_shortconv_stream_step'


def _rms(x, w):
    xf = x.astype(jnp.float32)
    y = xf * lax.rsqrt(jnp.mean(xf * xf, axis=-1, keepdims=True) + RMS_EPS)
    return (y * w.astype(jnp.float32)).astype(x.dtype)


def _fox_block(q, k, v, c_q, c_k, q_pos, k_pos):
    s = jnp.einsum('bqhd,bkhd->bhqk', q, k).astype(jnp.float32) * (HEAD_DIM ** -0.5)
    s = s + jnp.swapaxes(c_q, 1, 2)[..., :, None] - jnp.swapaxes(c_k, 1, 2)[..., None, :]
    s = jnp.where(k_pos[None, :] <= q_pos[:, None], s, NEG_INF)
    p = jax.nn.softmax(s, axis=-1).astype(v.dtype)
    return jnp.einsum('bhqk,bkhd->bqhd', p, v)


def _fox_prompt(q, k, v, logf):
    bt, L = q.shape[0], q.shape[1]
    c = jnp.cumsum(logf, axis=1)
    pos = jnp.arange(L)
    nb = L // Q_BLOCK
    qb = jnp.swapaxes(q.reshape(bt, nb, Q_BLOCK, N_HEADS, HEAD_DIM), 0, 1)
    cb = jnp.swapaxes(c.reshape(bt, nb, Q_BLOCK, N_HEADS), 0, 1)
    pb = pos.reshape(nb, Q_BLOCK)
    out = lax.map(lambda a: _fox_block(a[0], k, v, a[1], c, a[2], pos), (qb, cb, pb))
    return jnp.swapaxes(out, 0, 1).reshape(bt, L, ATT_DIM)


def _fox_sample(q, k, v, logf, cache_k, cache_v, cache_logf):
    bt, L = q.shape[0], q.shape[1]
    past = cache_k.shape[1]
    k_all = jnp.concatenate([cache_k.astype(k.dtype), k], axis=1)
    v_all = jnp.concatenate([cache_v.astype(v.dtype), v], axis=1)
    c_all = jnp.cumsum(jnp.concatenate([cache_logf.astype(jnp.float32), logf], axis=1), axis=1)
    k_pos = jnp.arange(past + L)
    q_pos = past + jnp.arange(L)
    out = _fox_block(q, k_all, v_all, c_all[:, past:], c_all, q_pos, k_pos)
    return out.reshape(bt, L, ATT_DIM)


def _s5(u, lam_re, lam_im, log_dt, b_re, b_im, c_re, c_im, d_skip, h0):
    f32 = jnp.float32
    bt, L = u.shape[0], u.shape[1]
    ug = u.astype(f32).reshape(bt, L, SSM_GROUPS, SSM_GROUP)
    lr = jnp.minimum(lam_re.astype(f32), -1e-4)
    li = lam_im.astype(f32)
    dt = jnp.exp(log_dt.astype(f32))[:, None]
    ldr, ldi = lr * dt, li * dt
    mag = jnp.exp(ldr)
    ar, ai = mag * jnp.cos(ldi), mag * jnp.sin(ldi)
    den = lr * lr + li * li
    nr = ar - 1.0
    qr = (nr * lr + ai * li) / den
    qi = (ai * lr - nr * li) / den
    br, bi = b_re.astype(f32), b_im.astype(f32)
    bbar_re = qr[..., None] * br - qi[..., None] * bi
    bbar_im = qr[..., None] * bi + qi[..., None] * br
    bu_re = jnp.einsum('blgc,gpc->blgp', ug, bbar_re)
    bu_im = jnp.einsum('blgc,gpc->blgp', ug, bbar_im)
    a_re = jnp.broadcast_to(ar, bu_re.shape)
    a_im = jnp.broadcast_to(ai, bu_im.shape)

    def combine(e1, e2):
        a1r, a1i, b1r, b1i = e1
        a2r, a2i, b2r, b2i = e2
        return (a1r * a2r - a1i * a2i, a1r * a2i + a1i * a2r,
                a2r * b1r - a2i * b1i + b2r, a2r * b1i + a2i * b1r + b2i)

    _, _, hr, hi = lax.associative_scan(combine, (a_re, a_im, bu_re, bu_im), axis=1)
    if h0 is not None:
        h0r, h0i = h0[0].astype(f32)[:, None], h0[1].astype(f32)[:, None]
        t = jnp.arange(1, L + 1, dtype=f32)[:, None, None]
        pm = jnp.exp(ldr * t)
        pr, pi = pm * jnp.cos(ldi * t), pm * jnp.sin(ldi * t)
        hr = hr + pr * h0r - pi * h0i
        hi = hi + pr * h0i + pi * h0r
    y = jnp.einsum('blgp,gcp->blgc', hr, c_re.astype(f32)) - jnp.einsum('blgp,gcp->blgc', hi, c_im.astype(f32))
    y = y.reshape(bt, L, SSM_DIM) + d_skip.astype(f32) * u.astype(f32)
    return y.astype(u.dtype), hr[:, -1], hi[:, -1]


def _short_conv(h, gate_b, gate_c, w, prev):
    z = gate_c * h
    bt, L = z.shape[0], z.shape[1]
    if prev is None:
        pad = jnp.zeros((bt, CONV_WIDTH - 1, CONV_DIM), z.dtype)
    else:
        pad = prev.astype(z.dtype)
    zp = jnp.concatenate([pad, z], axis=1)
    y = sum(w[j] * zp[:, j:j + L] for j in range(CONV_WIDTH))
    return gate_b * y, zp[:, -(CONV_WIDTH - 1):]


def _layer(x, lp, past):
    bt, L = x.shape[0], x.shape[1]
    h = _rms(x, lp['ln1'])
    proj = h @ lp['w_in']
    q, k, v, fg, u, hc, gb, gc = jnp.split(proj, SPLITS, axis=-1)
    q = _rms(q.reshape(bt, L, N_HEADS, HEAD_DIM), lp['qn'])
    k = _rms(k.reshape(bt, L, N_HEADS, HEAD_DIM), lp['kn'])
    v = v.reshape(bt, L, N_HEADS, HEAD_DIM)
    logf = jax.nn.log_sigmoid((fg + lp['bf']).astype(jnp.float32))
    if past is None:
        att = _fox_prompt(q, k, v, logf)
        h0, conv_prev = None, None
    else:
        ck, cv, clf, s_re, s_im, s_conv = past
        att = _fox_sample(q, k, v, logf, ck, cv, clf)
        h0, conv_prev = (s_re, s_im), s_conv
    y_ssm, hr, hi = _s5(u, lp['lam_re'], lp['lam_im'], lp['log_dt'], lp['b_re'], lp['b_im'],
                        lp['c_re'], lp['c_im'], lp['d'], h0)
    g = jax.nn.gelu(y_ssm)
    ssm = g * jax.nn.sigmoid(g @ lp['w_glu'] + lp['b_glu'])
    conv, conv_state = _short_conv(hc, gb, gc, lp['conv_w'], conv_prev)
    bn = lp['bn']
    mix = jnp.concatenate([_rms(att, bn[:ATT_DIM]),
                           _rms(ssm, bn[ATT_DIM:ATT_DIM + SSM_DIM]),
                           _rms(conv, bn[ATT_DIM + SSM_DIM:])], axis=-1)
    x = x + mix @ lp['w_out']
    f = jax.nn.relu(_rms(x, lp['ln2']) @ lp['w_up'])
    x = x + (f * f) @ lp['w_down']
    return x, (k, v, logf, hr, hi, conv_state)


def setup_inputs(seed: int = 0) -> dict:
    key = jax.random.key(seed)
    ks = jax.random.split(key, 32)
    f32 = jnp.float32

    def nrm(k, shape, scale):
        return jax.random.normal(k, shape, f32) * scale

    out = {}
    out['x_prompt'] = nrm(ks[0], (BATCH, SEQ, D_MODEL), 1.0)
    out['x_sample'] = nrm(ks[1], (DEC_BATCH, DEC_SEQ, D_MODEL), 1.0)
    out['cache_k'] = nrm(ks[2], (DEPTH, DEC_BATCH, PAST_LEN, N_HEADS, HEAD_DIM), 1.0)
    out['cache_v'] = nrm(ks[3], (DEPTH, DEC_BATCH, PAST_LEN, N_HEADS, HEAD_DIM), 1.0)
    out['cache_logf'] = jax.nn.log_sigmoid(nrm(ks[4], (DEPTH, DEC_BATCH, PAST_LEN, N_HEADS), 1.0) + 3.0)
    out['state_ssm_re'] = nrm(ks[5], (DEPTH, DEC_BATCH, SSM_GROUPS, SSM_STATE), 0.1)
    out['state_ssm_im'] = nrm(ks[6], (DEPTH, DEC_BATCH, SSM_GROUPS, SSM_STATE), 0.1)
    out['state_conv'] = nrm(ks[7], (DEPTH, DEC_BATCH, CONV_WIDTH - 1, CONV_DIM), 1.0)
    out['ln1_w'] = 1.0 + nrm(ks[8], (DEPTH, D_MODEL), 0.02)
    out['w_in'] = nrm(ks[9], (DEPTH, D_MODEL, PROJ_DIM), D_MODEL ** -0.5)
    out['b_forget'] = jax.random.uniform(ks[10], (DEPTH, N_HEADS), f32, 1.0, 5.0)
    out['q_norm_w'] = 1.0 + nrm(ks[11], (DEPTH, HEAD_DIM), 0.02)
    out['k_norm_w'] = 1.0 + nrm(ks[12], (DEPTH, HEAD_DIM), 0.02)
    out['conv_w'] = nrm(ks[13], (DEPTH, CONV_WIDTH, CONV_DIM), CONV_WIDTH ** -0.5)
    out['ssm_lam_re'] = -0.5 + nrm(ks[14], (DEPTH, SSM_GROUPS, SSM_STATE), 0.01)
    out['ssm_lam_im'] = math.pi * jnp.arange(SSM_STATE, dtype=f32)[None, None, :] + nrm(ks[15], (DEPTH, SSM_GROUPS, SSM_STATE), 0.01)
    out['ssm_log_dt'] = jax.random.uniform(ks[16], (DEPTH, SSM_GROUPS), f32, math.log(DT_MIN), math.log(DT_MAX))
    out['ssm_b_re'] = nrm(ks[17], (DEPTH, SSM_GROUPS, SSM_STATE, SSM_GROUP), (2 * SSM_GROUP) ** -0.5)
    out['ssm_b_im'] = nrm(ks[18], (DEPTH, SSM_GROUPS, SSM_STATE, SSM_GROUP), (2 * SSM_GROUP) ** -0.5)
    out['ssm_c_re'] = nrm(ks[19], (DEPTH, SSM_GROUPS, SSM_GROUP, SSM_STATE), SSM_STATE ** -0.5)
    out['ssm_c_im'] = nrm(ks[20], (DEPTH, SSM_GROUPS, SSM_GROUP, SSM_STATE), SSM_STATE ** -0.5)
    out['ssm_d'] = nrm(ks[21], (DEPTH, SSM_DIM), 1.0)
    out['w_glu'] = nrm(ks[22], (DEPTH, SSM_DIM, SSM_DIM), SSM_DIM ** -0.5)
    out['b_glu'] = nrm(ks[23], (DEPTH, SSM_DIM), 0.02)
    out['branch_norm_w'] = 1.0 + nrm(ks[24], (DEPTH, MIX_DIM), 0.02)
    out['w_out'] = nrm(ks[25], (DEPTH, MIX_DIM, D_MODEL), MIX_DIM ** -0.5)
    out['ln2_w'] = 1.0 + nrm(ks[26], (DEPTH, D_MODEL), 0.02)
    out['w_up'] = nrm(ks[27], (DEPTH, D_MODEL, D_FF), D_MODEL ** -0.5)
    out['w_down'] = nrm(ks[28], (DEPTH, D_FF, D_MODEL), 0.5 * D_FF ** -0.5)
    return out


def reference(x_prompt, x_sample, cache_k, cache_v, cache_logf, state_ssm_re, state_ssm_im, state_conv,
              ln1_w, w_in, b_forget, q_norm_w, k_norm_w, conv_w, ssm_lam_re, ssm_lam_im, ssm_log_dt,
              ssm_b_re, ssm_b_im, ssm_c_re, ssm_c_im, ssm_d, w_glu, b_glu, branch_norm_w, w_out,
              ln2_w, w_up, w_down):
    xp, xs = x_prompt, x_sample
    st_p, st_s = [], []
    for l in range(DEPTH):
        lp = dict(ln1=ln1_w[l], w_in=w_in[l], bf=b_forget[l], qn=q_norm_w[l], kn=k_norm_w[l],
                  conv_w=conv_w[l], lam_re=ssm_lam_re[l], lam_im=ssm_lam_im[l], log_dt=ssm_log_dt[l],
                  b_re=ssm_b_re[l], b_im=ssm_b_im[l], c_re=ssm_c_re[l], c_im=ssm_c_im[l], d=ssm_d[l],
                  w_glu=w_glu[l], b_glu=b_glu[l], bn=branch_norm_w[l], w_out=w_out[l],
                  ln2=ln2_w[l], w_up=w_up[l], w_down=w_down[l])
        xp, sp = _layer(xp, lp, None)
        xs, ss = _layer(xs, lp, (cache_k[l], cache_v[l], cache_logf[l],
                                 state_ssm_re[l], state_ssm_im[l], state_conv[l]))
        st_p.append(sp)
        st_s.append(ss)

    def stk(lst, i):
        return jnp.stack([s[i] for s in lst])

    return (xp, xs,
            stk(st_p, 0), stk(st_p, 1), stk(st_p, 2), stk(st_p, 3), stk(st_p, 4), stk(st_p, 5),
            stk(st_s, 0), stk(st_s, 1), stk(st_s, 2), stk(st_s, 3), stk(st_s, 4), stk(st_s, 5))
```

```python
import numpy as np
import concourse.bass as bass
import concourse.mybir as mybir
from concourse.bass_utils import run_bass_kernel_spmd

F32 = mybir.dt.float32
BF16 = mybir.dt.bfloat16
I32 = mybir.dt.int32
AF = mybir.ActivationFunctionType
ALU = mybir.AluOpType
AX = mybir.AxisListType

D = 1024
NH = 8
DH = 64
ATT = 512
SSM = 256
CONV = 256
PROJ = 2568
DFF = 4096
EPS = 1e-6
NEG = -1e30
C_FG = 3 * ATT
C_U = C_FG + NH
VW = DH + 2


class Dep:
    __slots__ = ("w", "r", "excl")

    def __init__(self, excl=False):
        self.w = None
        self.r = []
        self.excl = excl


class KB:
    def __init__(self, nc, ndma=20):
        self.nc = nc
        self.eng = {"pe": nc.tensor, "dve": nc.vector, "act": nc.scalar, "pool": nc.gpsimd, "sp": nc.sync}
        self.sems = {}
        self.cnt = {}
        for e in ["pe", "dve", "act", "pool"]:
            self.sems[e] = nc.semaphore("sem_" + e).__enter__()
            self.cnt[e] = 0
        self.dma_sems = {}
        self.dma_cnt = {}
        self.dma_rr = {}
        for q in ["sp", "pool"]:
            self.dma_sems[q] = [nc.semaphore(f"dsem_{q}{i}").__enter__() for i in range(ndma)]
            self.dma_cnt[q] = [0] * ndma
            self.dma_rr[q] = 0
        self.waited = {e: {} for e in self.eng}
        self.ninst = 0
        self.sb_bytes = 0
        self.stack = []

    def sb(self, name, shape, dt):
        self.sb_bytes += 1
        t = self.nc.sbuf_tensor(f"{name}_{self.sb_bytes}", list(shape), dt)
        h = t.__enter__()
        self.stack.append(t)
        return h

    def ps(self, name, shape, dt):
        t = self.nc.psum_tensor(name, list(shape), dt)
        h = t.__enter__()
        self.stack.append(t)
        return h

    def mark(self):
        return len(self.stack)

    def release(self, mark):
        while len(self.stack) > mark:
            t = self.stack.pop()
            t.__exit__(None, None, None)

    def _semobj(self, key):
        if isinstance(key, str):
            return self.sems[key]
        q, i = key
        return self.dma_sems[q][i]

    def _wait(self, e, key, val):
        w = self.waited[e]
        if w.get(key, 0) >= val:
            return
        w[key] = val
        self.eng[e].wait_ge(self._semobj(key), val)

    def _deps(self, e, reads, writes):
        need = {}

        def add(kv):
            if kv is None:
                return
            k, v = kv
            if k == "pe" and e == "pe":
                return
            if need.get(k, 0) < v:
                need[k] = v
        for d in reads:
            add(d.w)
        for d in writes:
            add(d.w)
            for r in d.r:
                add(r)
        for k, v in need.items():
            self._wait(e, k, v)

    def _record(self, tag, reads, writes):
        for d in reads:
            d.r.append(tag)
            if len(d.r) > 48:
                m = {}
                for k, v in d.r:
                    if m.get(k, 0) < v:
                        m[k] = v
                d.r = list(m.items())
        for d in writes:
            d.w = tag
            d.r = []

    def op(self, e, fn, reads=(), writes=()):
        ex = [d for d in reads if d.excl]
        if ex:
            reads = [d for d in reads if not d.excl]
            writes = list(writes) + [d for d in ex if d not in writes]
        self._deps(e, reads, writes)
        ins = fn()
        self.cnt[e] += 1
        ins.then_inc(self.sems[e], 1)
        self._record((e, self.cnt[e]), reads, writes)
        self.ninst += 1
        return ins

    def dma(self, q, out, in_, reads=(), writes=(), **kw):
        i = self.dma_rr[q]
        self.dma_rr[q] = (i + 1) % len(self.dma_sems[q])
        key = (q, i)
        if self.dma_cnt[q][i] > 0:
            self._wait(q, key, self.dma_cnt[q][i])
        self._deps(q, reads, writes)
        ins = self.eng[q].dma_start(out=out, in_=in_, **kw)
        self.dma_cnt[q][i] += 16
        ins.then_inc(self.dma_sems[q][i], 16)
        self._record((key, self.dma_cnt[q][i]), reads, writes)
        self.ninst += 1
        return ins

    def barrier(self):
        for e in ["pe", "dve", "act", "pool", "sp"]:
            for q in self.dma_sems:
                for i, c in enumerate(self.dma_cnt[q]):
                    if c > 0:
                        self._wait(e, (q, i), c)
            for e2 in ["pe", "dve", "act", "pool"]:
                if self.cnt[e2] > 0 and e2 != e:
                    self._wait(e, e2, self.cnt[e2])

    def finish(self):
        for q in self.dma_sems:
            for i, c in enumerate(self.dma_cnt[q]):
                if c > 0:
                    self._wait("sp", (q, i), c)
        for e in ["pe", "dve", "act", "pool"]:
            if self.cnt[e] > 0:
                self._wait("sp", e, self.cnt[e])


class Rot:
    def __init__(self, bufs):
        self.bufs = bufs
        self.deps = [Dep() for _ in bufs]
        self.i = 0

    def next(self):
        b, d = self.bufs[self.i], self.deps[self.i]
        self.i = (self.i + 1) % len(self.bufs)
        return b, d


class StopBuild(Exception):
    pass


def build_program(L, P, nlayers=2, debug=False, stage=99):
    nc = bass.Bass("TRN2", target_bir_lowering=False)
    kb = KB(nc)

    def stg(n):
        if stage == n:
            raise StopBuild()
    NTP = L // 128
    NT = NTP + 1
    NTOK = L + 128
    NQ = L // 512
    NKC = P // 128

    def din(name, shape, dt=F32):
        return nc.dram_tensor(name, list(shape), dt, kind="ExternalInput").ap()

    def dout(name, shape, dt=F32):
        return nc.dram_tensor(name, list(shape), dt, kind="ExternalOutput").ap()

    def dscr(name, shape, dt=F32):
        return nc.dram_tensor(name, list(shape), dt, kind="Internal").ap()

    x_all = din("x_all", [NTOK, D])
    ck = din("ck", [nlayers, 2, P, ATT])
    cv = din("cv", [nlayers, 2, P, ATT])
    clf = din("clf", [nlayers, 2, P, NH])
    sre = din("sre", [nlayers, 2, 128, 8])
    sim = din("sim", [nlayers, 2, 128, 8])
    sconv = din("sconv", [nlayers, 2, 128, 2, 2])
    w_in = din("w_in", [nlayers, D, PROJ])
    w_out = din("w_out", [nlayers, D, D])
    w_up = din("w_up", [nlayers, D, DFF])
    w_down = din("w_down", [nlayers, DFF, D])
    w_glu = din("w_glu", [nlayers, SSM, SSM])
    ln1b = din("ln1b", [nlayers, 128, D])
    ln2b = din("ln2b", [nlayers, 128, D])
    bnatt = din("bnatt", [nlayers, 128, ATT])
    qnb = din("qnb", [nlayers, 128, ATT])
    knb = din("knb", [nlayers, 128, ATT])
    bfb = din("bfb", [nlayers, 128, NH])
    convw = din("convw", [nlayers, 128, 2, 3])
    dvec = din("dvec", [nlayers, 128, 2])
    bglu = din("bglu", [nlayers, 128, 2])
    bnssm = din("bnssm", [nlayers, 128, 2])
    bnconv = din("bnconv", [nlayers, 128, 2])
    lamre = din("lamre", [nlayers, 128, 8])
    lamim = din("lamim", [nlayers, 128, 8])
    logdt = din("logdt", [nlayers, 128, 8])
    bre_pad = din("bre_pad", [nlayers, 128, 8, 128])
    bim_pad = din("bim_pad", [nlayers, 128, 8, 128])
    cre_pad = din("cre_pad", [nlayers, 128, 8, 128])
    cim_pad = din("cim_pad", [nlayers, 128, 8, 128])
    c_ident = din("c_ident", [128, 128])
    c_utri = din("c_utri", [128, 128])
    c_utri2 = din("c_utri2", [128, 128])
    c_ones = din("c_ones", [128, 128])
    c_lstrict = din("c_lstrict", [128, 128])
    c_maskT = din("c_maskT", [128, 128])
    c_mask2 = din("c_mask2", [2, 128, 64])

    y_out = dout("y_out", [NTOK, D])
    k_out = dout("k_out", [nlayers, NTOK, ATT])
    v_out = dout("v_out", [nlayers, NTOK, ATT])
    lf_out = dout("lf_out", [nlayers, NTOK, NH])
    sre_out = dout("sre_out", [nlayers, 3, 128, 8])
    sim_out = dout("sim_out", [nlayers, 3, 128, 8])
    conv_out = dout("conv_out", [nlayers, 3, 128, 2, 2])
    if debug:
        att_dbg = dout("att_dbg", [nlayers, NTOK, ATT])
        feat_dbg = dout("feat_dbg", [nlayers, D, NTOK])
        att_s = [att_dbg[l] for l in range(nlayers)]
        featT = [feat_dbg[l] for l in range(nlayers)]
    else:
        att_sc = dscr("att_sc", [nlayers, NTOK, ATT])
        feat_sc = dscr("feat_sc", [nlayers, D, NTOK])
        att_s = [att_sc[l] for l in range(nlayers)]
        featT = [feat_sc[l] for l in range(nlayers)]
    x1 = dscr("x1", [NTOK, D])
    wup_bf = dscr("wup_bf", [nlayers, 16, 128, 8, 256], BF16)
    wdn_bf = dscr("wdn_bf", [nlayers, 2, 8, 128, 4, 512], BF16)
    d_wup_bf = [Dep() for _ in range(nlayers)]
    d_wdn_bf = [Dep() for _ in range(nlayers)]
    d_att = [[Dep() for _ in range(NT)] for _ in range(nlayers)]
    d_feat = [[Dep() for _ in range(NT)] for _ in range(nlayers)]
    d_x1 = [Dep() for _ in range(NT)]

    pbank = [kb.ps(f"pb{i}", [128, 512], F32) for i in range(8)]
    pdep = [Dep(excl=True) for _ in range(8)]

    ident_f = kb.sb("ident_f", [128, 128], F32)
    ident_b = kb.sb("ident_b", [128, 128], BF16)
    utri = kb.sb("utri", [128, 128], F32)
    utri2 = kb.sb("utri2", [128, 128], F32)
    ones_f = kb.sb("ones_f", [128, 128], F32)
    lstrict = kb.sb("lstrict", [128, 128], F32)
    maskT = kb.sb("maskT", [128, 128], F32)
    mask2 = kb.sb("mask2", [128, 2, 64], F32)
    d_const = Dep()
    kb.dma("sp", ident_f[:], c_ident[:, :], writes=[d_const])
    kb.dma("pool", ident_b[:], c_ident[:, :], writes=[d_const])
    kb.dma("sp", utri[:], c_utri[:, :], writes=[d_const])
    kb.dma("sp", utri2[:], c_utri2[:, :], writes=[d_const])
    kb.dma("sp", ones_f[:], c_ones[:, :], writes=[d_const])
    kb.dma("sp", lstrict[:], c_lstrict[:, :], writes=[d_const])
    kb.dma("sp", maskT[:], c_maskT[:, :], writes=[d_const])
    kb.dma("sp", mask2[:], c_mask2.rearrange("s p q -> p s q"), writes=[d_const])

    CAST_KW = dict(max_dma_last_dim=2048)
    stg(1)

    def phase_a(l, x_src, d_xsrc):
        m0 = kb.mark()
        w_in_sb = kb.sb("w_in_sb", [128, 8, PROJ], BF16)
        WIN_CH = [(0, 512), (512, 1024), (1024, 1536), (1536, 2048), (2048, 2560), (2560, 2568)]
        d_win_ch = [Dep() for _ in WIN_CH]
        for (c0_, c1_), dd in zip(WIN_CH, d_win_ch):
            kb.dma("pool", w_in_sb[:, :, c0_:c1_], w_in[l].rearrange("(kt p) n -> p kt n", p=128)[:, :, c0_:c1_], writes=[dd], **CAST_KW)

        def d_win_for(c0_, n_):
            return [dd for (a, b), dd in zip(WIN_CH, d_win_ch) if a < c0_ + n_ and c0_ < b]
        for fc in range(16):
            kb.dma("pool", wup_bf[l, fc], w_up[l].rearrange("(kt p) n -> p kt n", p=128)[:, :, fc * 256:(fc + 1) * 256],
                   writes=[d_wup_bf[l]], **CAST_KW)
        for c in range(2):
            for ig in range(8):
                kb.dma("pool", wdn_bf[l, c, ig],
                       w_down[l].rearrange("(i p) n -> p i n", p=128)[:, ig * 4:(ig + 1) * 4, c * 512:(c + 1) * 512],
                       writes=[d_wdn_bf[l]], **CAST_KW)
        ln1 = kb.sb("ln1", [128, D], F32)
        qn = kb.sb("qn", [128, ATT], F32)
        kn = kb.sb("kn", [128, ATT], F32)
        bf = kb.sb("bf", [128, NH], F32)
        d_vec = Dep()
        kb.dma("sp", ln1[:], ln1b[l], writes=[d_vec])
        kb.dma("sp", qn[:], qnb[l], writes=[d_vec])
        kb.dma("sp", kn[:], knb[l], writes=[d_vec])
        kb.dma("sp", bf[:], bfb[l], writes=[d_vec])
        KT = kb.sb("KT", [128, 4, L], BF16)
        d_KT = [Dep() for _ in range(NTP)]
        VA = kb.sb("VA", [128, NTP, NH, VW], BF16)
        d_VA = [Dep() for _ in range(NTP)]
        d_VA1 = Dep()
        kb.op("dve", lambda: nc.vector.memset(VA[:, :, :, DH:DH + 1], 1.0), writes=[d_VA1])
        cst = kb.sb("cst", [128, NTP, NH], F32)
        d_cst = [Dep() for _ in range(NTP)]
        carry = kb.sb("carry", [128, NTP + 1, NH], F32)
        d_carry = [Dep() for _ in range(NTP + 1)]
        kb.op("dve", lambda: nc.vector.memset(carry[:, 0, :], 0.0), writes=[d_carry[0]])

        xt_r = Rot([kb.sb(f"xt{i}", [128, D], F32) for i in range(2)])
        sqs = kb.sb("sqs", [128, D], F32)
        d_sqs = Dep()
        hb_r = Rot([kb.sb(f"hb{i}", [128, D], BF16) for i in range(2)])
        hT = kb.sb("hT", [128, 8, 512], BF16)
        d_hT = [Dep() for _ in range(4)]
        QT2 = [kb.sb(f"QT{i}", [128, 4, 2, 512], BF16) for i in range(2)]
        d_QT2 = [[Dep() for _ in range(4)] for _ in range(2)]
        d_QTz = Dep()
        for QTb in QT2:
            kb.op("dve", lambda QTb=QTb: nc.vector.memset(QTb[:], 0.0), writes=[d_QTz])
        stat_r = Rot([kb.sb(f"stat{i}", [128, 40], F32) for i in range(4)])
        qk_r = Rot([kb.sb(f"qkb{i}", [128, ATT], BF16) for i in range(4)])
        kf_r = Rot([kb.sb(f"kf{i}", [128, ATT], F32) for i in range(2)])
        vf_r = Rot([kb.sb(f"vf{i}", [128, ATT], F32) for i in range(2)])
        lf_r = Rot([kb.sb(f"lf{i}", [128, NH], F32) for i in range(2)])
        fe_r = Rot([kb.sb(f"fe{i}", [128, 512], F32) for i in range(2)])
        rc_r = Rot([kb.sb(f"rc{i}", [128, 1], F32) for i in range(4)])
        ptw_r = Rot([kb.sb(f"ptw{i}", [128, 512], BF16) for i in range(4)])
        biasF = kb.sb("biasF", [128, NTP, NH], F32)
        d_biasF = Dep()
        biasN = kb.sb("biasN", [128, 4, 4, NH], F32)
        d_biasN = Dep()
        fac = kb.sb("fac", [128, 4, NH], F32)
        d_fac = Dep()
        osum = kb.sb("osum", [128, 4, DH + 1], F32)
        d_osum = Dep()
        rc4 = kb.sb("rc4", [128, 4], F32)
        d_rc4 = Dep()
        att4 = kb.sb("att4", [128, 4, ATT], F32)
        d_att4 = Dep()
        FAR_BANKS = [3, 5]
        NEAR_BANKS = [4, 6]
        PB_PROJ = [0, 1]
        PB_S = [2, 3, 4]
        PB_O = [5, 6]
        PB_T = 7
        rr = {"proj": 0, "s": 0, "o": 0, "sw": 0}

        def nextbank(kind, lst):
            i = rr[kind]
            rr[kind] = (i + 1) % len(lst)
            return lst[i]

        def rms_h(xt, d_x, nrm_w, hb, d_hb):
            st, d_st = stat_r.next()
            kb.op("act", lambda: nc.scalar.activation(out=sqs[:], in_=xt[:], func=AF.Square, accum_out=st[:, 0:1]),
                  reads=[d_x], writes=[d_sqs, d_st])
            kb.op("act", lambda: nc.scalar.activation(out=st[:, 1:2], in_=st[:, 0:1], func=AF.Ln, scale=1.0 / D, bias=EPS),
                  reads=[d_st], writes=[d_st])
            kb.op("act", lambda: nc.scalar.activation(out=st[:, 2:3], in_=st[:, 1:2], func=AF.Exp, scale=-0.5),
                  reads=[d_st], writes=[d_st])
            kb.op("dve", lambda: nc.vector.scalar_tensor_tensor(out=hb[:], in0=xt[:], scalar=st[:, 2:3], in1=nrm_w[:],
                                                                op0=ALU.mult, op1=ALU.mult),
                  reads=[d_x, d_st, d_vec], writes=[d_hb])

        def transpose_to(hb, d_hb, dst_fn, d_dst, ncols=8):
            b = PB_T
            pb16 = pbank[b][:].bitcast(BF16)
            for c in range(ncols):
                kb.op("pe", lambda c=c: nc.tensor.transpose(out=pb16[:, c * 128:(c + 1) * 128],
                                                           in_=hb[:, c * 128:(c + 1) * 128], identity=ident_b[:]),
                      reads=[d_hb, d_const], writes=[pdep[b]])
            return pb16, b

        def token_tile(tg, j, qpar, sample=False):
            QT, d_QT = QT2[qpar], d_QT2[qpar]
            xt, d_x = xt_r.next()
            kb.dma("sp", xt[:], x_src[tg * 128:(tg + 1) * 128, :], reads=[d_xsrc[tg]], writes=[d_x])
            hb, d_hb = hb_r.next()
            rms_h(xt, d_x, ln1, hb, d_hb)
            pb16, b = transpose_to(hb, d_hb, None, None)
            kb.op("dve", lambda: nc.vector.tensor_copy(out=hT[:, :, j * 128:(j + 1) * 128],
                                                       in_=pb16[:, 0:1024].rearrange("p (c t) -> p c t", c=8)),
                  reads=[pdep[b]], writes=[d_hT[j]])
            yield
            stg(2)
            st, d_st = stat_r.next()

            def proj(c0, n):
                pbi = nextbank("proj", PB_PROJ)
                for kt in range(8):
                    kb.op("pe", lambda kt=kt: nc.tensor.matmul(pbank[pbi][:, 0:n], lhsT=hT[:, kt, j * 128:(j + 1) * 128],
                                                              rhs=w_in_sb[:, kt, c0:c0 + n], start=(kt == 0), stop=(kt == 7)),
                          reads=[d_hT[j]] + d_win_for(c0, n), writes=[pdep[pbi]])
                return pbi

            def qk_norm(pbi, w_t, off, dst_b, d_dstb, dst_f=None, d_dstf=None):
                kb.op("act", lambda: nc.scalar.activation(out=sqs[:, 0:ATT], in_=pbank[pbi][:, 0:ATT], func=AF.Square),
                      reads=[pdep[pbi]], writes=[d_sqs])
                kb.op("dve", lambda: nc.vector.tensor_reduce(out=st[:, off:off + 8],
                                                             in_=sqs[:, 0:ATT].rearrange("p (h d) -> p h d", h=NH),
                                                             axis=AX.X, op=ALU.add),
                      reads=[d_sqs], writes=[d_st])
                kb.op("act", lambda: nc.scalar.activation(out=st[:, off:off + 8], in_=st[:, off:off + 8], func=AF.Ln,
                                                          scale=1.0 / DH, bias=EPS), reads=[d_st], writes=[d_st])
                kb.op("act", lambda: nc.scalar.activation(out=st[:, off + 16:off + 24], in_=st[:, off:off + 8], func=AF.Exp,
                                                          scale=-0.5), reads=[d_st], writes=[d_st])
                kb.op("dve", lambda: nc.vector.tensor_tensor(
                    out=sqs[:, 0:ATT].rearrange("p (h d) -> p h d", h=NH),
                    in0=pbank[pbi][:, 0:ATT].rearrange("p (h d) -> p h d", h=NH),
                    in1=st[:, off + 16:off + 24].unsqueeze(2).broadcast_to([128, NH, DH]), op=ALU.mult),
                    reads=[pdep[pbi], d_st], writes=[d_sqs])
                if dst_f is not None:
                    kb.op("dve", lambda: nc.vector.tensor_tensor(out=dst_f[:], in0=sqs[:, 0:ATT], in1=w_t[:], op=ALU.mult),
                          reads=[d_sqs, d_vec], writes=[d_dstf])
                    kb.op("act", lambda: nc.scalar.copy(out=dst_b[:], in_=dst_f[:]), reads=[d_dstf], writes=[d_dstb])
                else:
                    kb.op("dve", lambda: nc.vector.tensor_tensor(out=dst_b[:], in0=sqs[:, 0:ATT], in1=w_t[:], op=ALU.mult),
                          reads=[d_sqs, d_vec], writes=[d_dstb])

            pq = proj(0, ATT)
            qb, d_qb = qk_r.next()
            qk_norm(pq, qn, 0, qb, d_qb)
            pb16, b = transpose_to(qb, d_qb, None, None, ncols=4)
            kb.op("dve", lambda: nc.vector.tensor_copy(out=QT[0:64, :, 0, j * 128:(j + 1) * 128],
                                                       in_=pb16[0:64, 0:512].rearrange("p (c t) -> p c t", c=4)),
                  reads=[pdep[b], d_QTz], writes=[d_QT[j]])
            kb.op("dve", lambda: nc.vector.tensor_copy(out=QT[64:128, :, 1, j * 128:(j + 1) * 128],
                                                       in_=pb16[64:128, 0:512].rearrange("p (c t) -> p c t", c=4)),
                  reads=[pdep[b], d_QTz], writes=[d_QT[j]])
            yield
            stg(21)
            pk = proj(ATT, ATT)
            kbb, d_kbb = qk_r.next()
            kf, d_kf = kf_r.next()
            qk_norm(pk, kn, 8, kbb, d_kbb, kf, d_kf)
            kb.dma("sp", k_out[l, tg * 128:(tg + 1) * 128, :], kf[:], reads=[d_kf])
            pb16, b = transpose_to(kbb, d_kbb, None, None, ncols=4)
            if not sample:
                kb.op("dve", lambda: nc.vector.tensor_copy(out=KT[:, :, tg * 128:(tg + 1) * 128],
                                                           in_=pb16[:, 0:512].rearrange("p (c t) -> p c t", c=4)),
                      reads=[pdep[b]], writes=[d_KT[tg]])
            else:
                kb.op("dve", lambda: nc.vector.tensor_copy(out=KTs[:, :, :],
                                                           in_=pb16[:, 0:512].rearrange("p (c t) -> p c t", c=4)),
                      reads=[pdep[b]], writes=[d_KTs])
            yield
            stg(22)
            pv = proj(2 * ATT, ATT)
            stg(221)
            vf, d_vf = vf_r.next()
            kb.op("dve", lambda: nc.vector.tensor_copy(out=vf[:], in_=pbank[pv][:, 0:ATT]), reads=[pdep[pv]], writes=[d_vf])
            stg(222)
            kb.dma("sp", v_out[l, tg * 128:(tg + 1) * 128, :], vf[:], reads=[d_vf])
            stg(223)
            if not sample:
                kb.op("dve", lambda: nc.vector.tensor_copy(out=VA[:, tg, :, 0:DH],
                                                           in_=pbank[pv][:, 0:ATT].rearrange("p (h d) -> p h d", h=NH)),
                      reads=[pdep[pv], d_VA1], writes=[d_VA[tg]])
            else:
                kb.op("dve", lambda: nc.vector.tensor_copy(out=VAs[:, :, 0:DH],
                                                           in_=pbank[pv][:, 0:ATT].rearrange("p (h d) -> p h d", h=NH)),
                      reads=[pdep[pv]], writes=[d_VAs])
            stg(23)
            yield
            pf = proj(C_FG, NH)
            lf, d_lf = lf_r.next()
            kb.op("dve", lambda: nc.vector.tensor_tensor(out=st[:, 32:40], in0=pbank[pf][:, 0:NH], in1=bf[:], op=ALU.add),
                  reads=[pdep[pf], d_vec], writes=[d_st])
            kb.op("act", lambda: nc.scalar.activation(out=st[:, 32:40], in_=st[:, 32:40], func=AF.Exp, scale=-1.0),
                  reads=[d_st], writes=[d_st])
            kb.op("act", lambda: nc.scalar.activation(out=st[:, 32:40], in_=st[:, 32:40], func=AF.Ln, bias=1.0),
                  reads=[d_st], writes=[d_st])
            kb.op("dve", lambda: nc.vector.tensor_scalar(out=lf[:], in0=st[:, 32:40], scalar1=-1.0, scalar2=None, op0=ALU.mult),
                  reads=[d_st], writes=[d_lf])
            kb.dma("sp", lf_out[l, tg * 128:(tg + 1) * 128, :], lf[:], reads=[d_lf])
            stg(24)
            pc = nextbank("proj", PB_PROJ)
            tri = utri2 if sample else utri
            kb.op("pe", lambda: nc.tensor.matmul(pbank[pc][:, 0:NH], lhsT=tri[:], rhs=lf[:], start=True, stop=True),
                  reads=[d_lf, d_const], writes=[pdep[pc]])
            stg(25)
            if not sample:
                kb.op("pe", lambda: nc.tensor.matmul(pbank[pc][:, 8:8 + NH], lhsT=ones_f[:], rhs=lf[:], start=True, stop=True),
                      reads=[d_lf, d_const], writes=[pdep[pc]])
                stg(26)
                kb.op("dve", lambda: nc.vector.tensor_tensor(out=cst[:, tg, :], in0=pbank[pc][:, 0:NH], in1=carry[:, tg, :],
                                                             op=ALU.add),
                      reads=[pdep[pc], d_carry[tg]], writes=[d_cst[tg]])
                stg(27)
                kb.op("dve", lambda: nc.vector.tensor_tensor(out=carry[:, tg + 1, :], in0=pbank[pc][:, 8:8 + NH],
                                                             in1=carry[:, tg, :], op=ALU.add),
                      reads=[pdep[pc], d_carry[tg]], writes=[d_carry[tg + 1]])
            else:
                kb.op("dve", lambda: nc.vector.tensor_scalar(out=negc_s[:], in0=pbank[pc][:, 0:NH], scalar1=-1.0, scalar2=None,
                                                             op0=ALU.mult),
                      reads=[pdep[pc]], writes=[d_negc_s])

        def feature_proj(tg0, ntok, js):
            for ft in range(8):
                pbi = nextbank("proj", PB_PROJ)
                c0 = C_U + ft * 128
                for kt in range(8):
                    kb.op("pe", lambda kt=kt: nc.tensor.matmul(pbank[pbi][:, 0:ntok], lhsT=w_in_sb[:, kt, c0:c0 + 128],
                                                              rhs=hT[:, kt, 0:ntok], start=(kt == 0), stop=(kt == 7)),
                          reads=[d_hT[jj] for jj in js] + d_win_for(c0, 128), writes=[pdep[pbi]])
                fe, d_fe = fe_r.next()
                kb.op("dve", lambda: nc.vector.tensor_copy(out=fe[:, 0:ntok], in_=pbank[pbi][:, 0:ntok]), reads=[pdep[pbi]], writes=[d_fe])
                kb.dma("sp", featT[l][ft * 128:(ft + 1) * 128, tg0 * 128:tg0 * 128 + ntok], fe[:, 0:ntok], reads=[d_fe],
                       writes=[d_feat[l][tg0 + jj] for jj in js])
                yield

        def finish_att(ob, h, att, d_att_t, ncolq=128):
            rc, d_rc = rc_r.next()
            kb.op("dve", lambda: nc.vector.reciprocal(out=rc[0:ncolq, :], in_=pbank[ob][0:ncolq, DH:DH + 1]),
                  reads=[pdep[ob]], writes=[d_rc])
            kb.op("dve", lambda: nc.vector.tensor_scalar(out=att[0:ncolq, h * DH:(h + 1) * DH], in0=pbank[ob][0:ncolq, 0:DH],
                                                         scalar1=rc[0:ncolq, 0:1], scalar2=None, op0=ALU.mult),
                  reads=[pdep[ob], d_rc], writes=[d_att_t])

        def attention_prompt(Qm):
            QT, d_QT = QT2[Qm % 2], d_QT2[Qm % 2]
            LA = 2
            S_BANKS = [0, 1, 2, 7]
            t0 = 4 * Qm
            nfar = t0
            if nfar > 0:
                kb.op("dve", lambda: nc.vector.tensor_tensor(
                    out=biasF[:, 0:nfar, :], in0=carry[:, t0:t0 + 1, :].broadcast_to([128, nfar, NH]), in1=cst[:, 0:nfar, :],
                    op=ALU.subtract), reads=[d_carry[t0]] + d_cst[0:nfar], writes=[d_biasF])
                kb.op("dve", lambda: nc.vector.tensor_tensor(
                    out=fac[:], in0=carry[:, t0:t0 + 4, :], in1=carry[:, t0:t0 + 1, :].broadcast_to([128, 4, NH]), op=ALU.subtract),
                    reads=d_carry[t0:t0 + 4], writes=[d_fac])
                kb.op("act", lambda: nc.scalar.activation(out=fac[:], in_=fac[:], func=AF.Exp), reads=[d_fac], writes=[d_fac])
            for j in range(4):
                kb.op("dve", lambda j=j: nc.vector.tensor_tensor(
                    out=biasN[:, 0:j + 1, j, :], in0=carry[:, t0 + j:t0 + j + 1, :].broadcast_to([128, j + 1, NH]),
                    in1=cst[:, t0:t0 + j + 1, :], op=ALU.subtract),
                    reads=[d_carry[t0 + j]] + d_cst[t0:t0 + j + 1], writes=[d_biasN])
            blocks = []
            for h in range(NH):
                for kt in range(nfar):
                    blocks.append(dict(h=h, kt=kt, far=True, q0=0, i=None, first=(kt == 0), last=False))
                for i in range(4):
                    blocks.append(dict(h=h, kt=t0 + i, far=False, q0=i * 128, i=i, first=(i == 0), last=(i == 3)))
            pend = []
            cur = {}

            def emit_pvs(bk):
                h = bk["h"]
                kt = bk["kt"]
                pt, d_pt = bk["pt"]
                if bk["far"]:
                    if bk["first"]:
                        cur["far"] = FAR_BANKS[h % 2]
                    ob = cur["far"]
                    for j in range(4):
                        kb.op("pe", lambda j=j: nc.tensor.matmul(
                            pbank[ob][:, j * (DH + 1):(j + 1) * (DH + 1)], lhsT=pt[:, j * 128:(j + 1) * 128],
                            rhs=VA[:, kt, h, 0:DH + 1], start=(bk["first"] and j == 0), stop=(kt == nfar - 1), skip_group_check=True),
                            reads=[d_pt, d_VA[kt], d_VA1], writes=[pdep[ob]])
                else:
                    i = bk["i"]
                    if bk["first"]:
                        cur["near"] = NEAR_BANKS[h % 2]
                    ob = cur["near"]
                    for j in range(i, 4):
                        c0 = (j - i) * 128
                        kb.op("pe", lambda j=j, c0=c0: nc.tensor.matmul(
                            pbank[ob][:, j * (DH + 1):(j + 1) * (DH + 1)], lhsT=pt[:, c0:c0 + 128],
                            rhs=VA[:, kt, h, 0:DH + 1], start=(i == 0 and j == 0), stop=(i == j), skip_group_check=True),
                            reads=[d_pt, d_VA[kt], d_VA1], writes=[pdep[ob]])
                    if bk["last"]:
                        combine(h)

            def combine(h):
                nb = cur["near"]
                nview = pbank[nb][:, 0:4 * (DH + 1)].rearrange("p (j c) -> p j c", j=4)
                if nfar > 0:
                    fb = cur["far"]
                    fview = pbank[fb][:, 0:4 * (DH + 1)].rearrange("p (j c) -> p j c", j=4)
                    kb.op("dve", lambda: nc.vector.tensor_tensor(out=osum[:], in0=fview,
                                                                 in1=fac[:, :, h:h + 1].broadcast_to([128, 4, DH + 1]), op=ALU.mult),
                          reads=[pdep[fb], d_fac], writes=[d_osum])
                    kb.op("dve", lambda: nc.vector.tensor_tensor(out=osum[:], in0=nview, in1=osum[:], op=ALU.add),
                          reads=[pdep[nb], d_osum], writes=[d_osum])
                    src, d_src = osum[:], d_osum
                else:
                    kb.op("dve", lambda: nc.vector.tensor_copy(out=osum[:], in_=nview), reads=[pdep[nb]], writes=[d_osum])
                    src, d_src = osum[:], d_osum
                kb.op("dve", lambda: nc.vector.reciprocal(out=rc4[:], in_=osum[:, :, DH]), reads=[d_osum], writes=[d_rc4])
                kb.op("dve", lambda: nc.vector.tensor_tensor(out=att4[:, :, h * DH:(h + 1) * DH], in0=osum[:, :, 0:DH],
                                                             in1=rc4[:].unsqueeze(2).broadcast_to([128, 4, DH]), op=ALU.mult),
                      reads=[d_osum, d_rc4], writes=[d_att4])
                if h == NH - 1:
                    kb.dma("sp", att_s[l][t0 * 128:(t0 + 4) * 128, :].rearrange("(j p) c -> p j c", p=128), att4[:],
                           reads=[d_att4], writes=[d_att[l][t0 + j] for j in range(4)])

            for bk in blocks:
                h, kt = bk["h"], bk["kt"]
                pr, po = h // 2, (h % 2) * 64
                sbk = nextbank("sw", S_BANKS)
                q0 = bk["q0"]
                wq = 512 - q0
                kb.op("pe", lambda: nc.tensor.matmul(
                    pbank[sbk][:, 0:wq], lhsT=KT[:, pr, kt * 128:(kt + 1) * 128],
                    rhs=QT[:, pr, h % 2, q0:512], start=True, stop=True),
                    reads=[d_KT[kt]] + d_QT, writes=[pdep[sbk]])
                pt, d_pt = ptw_r.next()
                bk["pt"] = (pt, d_pt)
                if bk["far"]:
                    kb.op("act", lambda: nc.scalar.activation(out=pt[:, 0:512], in_=pbank[sbk][:, 0:512], func=AF.Exp,
                                                              scale=DH ** -0.5, bias=biasF[:, kt, h:h + 1]),
                          reads=[pdep[sbk], d_biasF], writes=[d_pt])
                else:
                    i = bk["i"]
                    kb.op("dve", lambda: nc.vector.tensor_tensor(out=pbank[sbk][:, 0:128], in0=pbank[sbk][:, 0:128],
                                                                 in1=maskT[:], op=ALU.add),
                          reads=[pdep[sbk], d_const], writes=[pdep[sbk]])
                    for j in range(i, 4):
                        c0 = (j - i) * 128
                        kb.op("act", lambda j=j, c0=c0: nc.scalar.activation(
                            out=pt[:, c0:c0 + 128], in_=pbank[sbk][:, c0:c0 + 128], func=AF.Exp, scale=DH ** -0.5,
                            bias=biasN[:, i, j, h:h + 1]),
                            reads=[pdep[sbk], d_biasN], writes=[d_pt])
                pend.append(bk)
                if len(pend) > LA:
                    emit_pvs(pend.pop(0))
                yield
            while pend:
                emit_pvs(pend.pop(0))

        def proj_macro(Qm):
            for j0 in (0, 2):
                ga = token_tile(4 * Qm + j0, j0, Qm % 2)
                gb2 = token_tile(4 * Qm + j0 + 1, j0 + 1, Qm % 2)
                live = [ga, gb2]
                while live:
                    for g in list(live):
                        try:
                            next(g)
                        except StopIteration:
                            live.remove(g)
                    yield
            yield from feature_proj(4 * Qm, 512, [0, 1, 2, 3])

        def proj_sample():
            yield from token_tile(NTP, 0, NQ % 2, sample=True)
            yield from feature_proj(NTP, 128, [0])

        def cosched2(ga, gb_, K=1):
            while ga is not None or gb_ is not None:
                for _ in range(K):
                    if ga is not None:
                        try:
                            next(ga)
                        except StopIteration:
                            ga = None
                if gb_ is not None:
                    try:
                        next(gb_)
                    except StopIteration:
                        gb_ = None

        KTs = kb.sb("KTs", [128, 4, 128], BF16)
        d_KTs = Dep()
        VAs = kb.sb("VAs", [128, NH, VW], BF16)
        d_VAs = Dep()
        kb.op("dve", lambda: nc.vector.memset(VAs[:, :, DH:DH + 1], 1.0), writes=[d_VAs])
        negc_s = kb.sb("negc_s", [128, NH], F32)
        d_negc_s = Dep()
        cosched2(proj_macro(0), None)
        for Qm in range(NQ):
            nblk = NH * (4 * Qm + 4)
            cosched2(attention_prompt(Qm), proj_macro(Qm + 1) if Qm + 1 < NQ else proj_sample(),
                     K=max(1, int(round(nblk / (20.0 if Qm + 1 < NQ else 14.0)))))
        QT, d_QT = QT2[NQ % 2], d_QT2[NQ % 2]
        stg(6)
        clf_sb = kb.sb("clf_sb", [128, NKC, NH], F32)
        d_clf = Dep()
        suf = kb.sb("suf", [128, NKC, NH], F32)
        d_suf = Dep()
        carr_s = kb.sb("carr_s", [128, NH], F32)
        d_carr_s = Dep()
        ckb_r = Rot([kb.sb(f"ckb{i}", [128, ATT], BF16) for i in range(2)])
        ckT_r = Rot([kb.sb(f"ckT{i}", [128, 4, 128], BF16) for i in range(3)])
        cva_r = Rot([kb.sb(f"cva{i}", [128, NH, VW], BF16) for i in range(3)])
        for cva_b, cva_d in zip(cva_r.bufs, cva_r.deps):
            kb.op("dve", lambda cva_b=cva_b: nc.vector.memset(cva_b[:, :, DH:DH + 1], 1.0), writes=[cva_d])
        pts_r = Rot([kb.sb(f"pts{i}", [128, NH, 64], BF16) for i in range(3)])
        att_smp = att4[:, 0, :]
        d_att_smp = d_att4
        for s in range(2):
            kb.dma("sp", clf_sb[:], clf[l, s].rearrange("(kt p) h -> p kt h", p=128), reads=[d_suf], writes=[d_clf])
            pc = nextbank("proj", PB_PROJ)
            n8 = NKC * NH
            kb.op("pe", lambda: nc.tensor.matmul(pbank[pc][:, 0:n8], lhsT=lstrict[:], rhs=clf_sb[:].rearrange("p k h -> p (k h)"),
                                                 start=True, stop=True), reads=[d_clf, d_const], writes=[pdep[pc]])
            pc2 = nextbank("proj", PB_PROJ)
            kb.op("pe", lambda: nc.tensor.matmul(pbank[pc2][:, 0:n8], lhsT=ones_f[:], rhs=clf_sb[:].rearrange("p k h -> p (k h)"),
                                                 start=True, stop=True), reads=[d_clf, d_const], writes=[pdep[pc2]])
            kb.op("dve", lambda: nc.vector.memset(carr_s[:], 0.0), writes=[d_carr_s])
            for kt in range(NKC - 1, -1, -1):
                kb.op("dve", lambda kt=kt: nc.vector.tensor_tensor(out=suf[:, kt, :], in0=pbank[pc][:, kt * NH:(kt + 1) * NH],
                                                                   in1=carr_s[:], op=ALU.add),
                      reads=[pdep[pc], d_carr_s], writes=[d_suf])
                if kt > 0:
                    kb.op("dve", lambda kt=kt: nc.vector.tensor_tensor(out=carr_s[:], in0=pbank[pc2][:, kt * NH:(kt + 1) * NH],
                                                                       in1=carr_s[:], op=ALU.add),
                          reads=[pdep[pc2], d_carr_s], writes=[d_carr_s])
            stg(61)
            obs = PB_O
            pend_pv = []

            def emit_pv_s(item):
                kt_, last_, pts_, d_pts_, va_, d_va_ = item
                for h in range(NH):
                    ob = obs[h // 4]
                    c0 = (h % 4) * (DH + 1)
                    first = (kt_ == 0 and h % 4 == 0)
                    kb.op("pe", lambda h=h, ob=ob, c0=c0, first=first: nc.tensor.matmul(
                        pbank[ob][0:64, c0:c0 + DH + 1], lhsT=pts_[:, h, :], rhs=va_[:, h, 0:DH + 1], start=first, stop=last_,
                        skip_group_check=True),
                        reads=[d_pts_, d_va_], writes=[pdep[ob]])

            for kt in range(NKC + 1):
                last = (kt == NKC)
                if not last:
                    ckb, d_ckb = ckb_r.next()
                    kb.dma("pool", ckb[:], ck[l, s, kt * 128:(kt + 1) * 128, :], writes=[d_ckb], **CAST_KW)
                    cva, d_cva = cva_r.next()
                    kb.dma("pool", cva[:, :, 0:DH], cv[l, s, kt * 128:(kt + 1) * 128, :].rearrange("p (h d) -> p h d", h=NH),
                           writes=[d_cva], **CAST_KW)
                    pb16, b = transpose_to(ckb, d_ckb, None, None, ncols=4)
                    ckT, d_ckT = ckT_r.next()
                    kb.op("dve", lambda: nc.vector.tensor_copy(out=ckT[:], in_=pb16[:, 0:512].rearrange("p (c t) -> p c t", c=4)),
                          reads=[pdep[b]], writes=[d_ckT])
                    kT_t, d_kT_t, va_t, d_va_t = ckT, d_ckT, cva, d_cva
                else:
                    kT_t, d_kT_t, va_t, d_va_t = KTs, d_KTs, VAs, d_VAs
                sbk = nextbank("s", PB_S)
                for h in range(NH):
                    pr = h // 2
                    kb.op("pe", lambda h=h, pr=pr: nc.tensor.matmul(
                        pbank[sbk][:, h * 64:(h + 1) * 64], lhsT=kT_t[:, pr, :],
                        rhs=QT[:, pr, h % 2, s * 64:(s + 1) * 64], start=True, stop=True),
                        reads=[d_kT_t, d_QT[0]], writes=[pdep[sbk]])
                if last:
                    kb.op("dve", lambda: nc.vector.tensor_tensor(
                        out=pbank[sbk][:, :].rearrange("p (h q) -> p h q", h=NH),
                        in0=pbank[sbk][:, :].rearrange("p (h q) -> p h q", h=NH),
                        in1=mask2[:, s:s + 1, :].broadcast_to([128, NH, 64]), op=ALU.add),
                        reads=[pdep[sbk], d_const], writes=[pdep[sbk]])
                pts, d_pts = pts_r.next()
                for h in range(NH):
                    bias_ap = negc_s[:, h:h + 1] if last else suf[:, kt, h:h + 1]
                    kb.op("act", lambda h=h, bias_ap=bias_ap: nc.scalar.activation(
                        out=pts[:, h, :], in_=pbank[sbk][:, h * 64:(h + 1) * 64], func=AF.Exp, scale=DH ** -0.5, bias=bias_ap),
                        reads=[pdep[sbk], d_negc_s if last else d_suf], writes=[d_pts])
                pend_pv.append((kt, last, pts, d_pts, va_t, d_va_t))
                if len(pend_pv) > 1:
                    emit_pv_s(pend_pv.pop(0))
            while pend_pv:
                emit_pv_s(pend_pv.pop(0))
            stg(64)
            for h in range(NH):
                ob = obs[h // 4]
                c0 = (h % 4) * (DH + 1)
                rc, d_rc = rc_r.next()
                kb.op("dve", lambda: nc.vector.reciprocal(out=rc[0:64, :], in_=pbank[ob][0:64, c0 + DH:c0 + DH + 1]),
                      reads=[pdep[ob]], writes=[d_rc])
                kb.op("dve", lambda h=h: nc.vector.tensor_scalar(out=att_smp[s * 64:(s + 1) * 64, h * DH:(h + 1) * DH],
                                                                 in0=pbank[ob][0:64, c0:c0 + DH], scalar1=rc[0:64, 0:1],
                                                                 scalar2=None, op0=ALU.mult),
                      reads=[pdep[ob], d_rc], writes=[d_att_smp])
        kb.dma("sp", att_s[l][NTP * 128:(NTP + 1) * 128, :], att_smp, reads=[d_att_smp], writes=[d_att[l][NTP]])
        kb.barrier()
        kb.release(m0)


    def phase_b(l, x_src, d_xsrc, x_dst, d_xdst):
        m0 = kb.mark()
        T = 128
        TWO_PI = float(2 * np.pi)
        w_out_sb = kb.sb("w_out_sb", [128, 8, D], BF16)
        w_glu_sb = kb.sb("w_glu_sb", [128, 2, SSM], BF16)
        d_wres = Dep()
        for kt in range(8):
            kb.dma("pool", w_out_sb[:, kt, :], w_out[l, kt * 128:(kt + 1) * 128, :], writes=[d_wres], **CAST_KW)
        kb.dma("pool", w_glu_sb[:], w_glu[l].rearrange("(kt p) n -> p kt n", p=128), writes=[d_wres], **CAST_KW)
        ln2 = kb.sb("ln2", [128, D], F32)
        bna = kb.sb("bna", [128, ATT], F32)
        cw = kb.sb("cw", [128, 2, 3], F32)
        dv = kb.sb("dv", [128, 2], F32)
        bg = kb.sb("bg", [128, 2], F32)
        bns = kb.sb("bns", [128, 2], F32)
        bnc = kb.sb("bnc", [128, 2], F32)
        d_vec = Dep()
        kb.dma("sp", ln2[:], ln2b[l], writes=[d_vec])
        kb.dma("sp", bna[:], bnatt[l], writes=[d_vec])
        kb.dma("sp", cw[:], convw[l], writes=[d_vec])
        kb.dma("sp", dv[:], dvec[l], writes=[d_vec])
        kb.dma("sp", bg[:], bglu[l], writes=[d_vec])
        kb.dma("sp", bns[:], bnssm[l], writes=[d_vec])
        kb.dma("sp", bnc[:], bnconv[l], writes=[d_vec])
        WBre = kb.sb("WBre", [128, 8, 128], BF16)
        WBim = kb.sb("WBim", [128, 8, 128], BF16)
        WCre = kb.sb("WCre", [128, 8, 128], BF16)
        WCimn = kb.sb("WCimn", [128, 8, 128], BF16)
        Ecos = kb.sb("Ecos", [128, 8, T], F32)
        Esin = kb.sb("Esin", [128, 8, T], F32)
        Mmul = kb.sb("Mmul", [128, 8, T], F32)
        rho = kb.sb("rho", [128, 8], F32)
        d_s5w = Dep()
        d_tab = Dep()
        PB_G = [0, 1, 2, 7]
        PB_ACC = [3, 4, 5, 6]
        rr = {"g": 0}

        def gbank():
            i = rr["g"]
            rr["g"] = (i + 1) % len(PB_G)
            return PB_G[i]

        m1 = kb.mark()
        sv = kb.sb("sv", [128, 24, 8], F32)
        d_sv = Dep()
        svi = kb.sb("svi", [128, 8], I32)
        LR, LI, DT, LDR, LDI, COS, SIN, AR, AI, DEN, NR, QR, QI, TMP, TMP2, NQI, ARG, KF = range(18)

        def V(i):
            return sv[:, i, :]

        def dve(fn):
            kb.op("dve", fn, reads=[d_sv], writes=[d_sv])

        def act(fn):
            kb.op("act", fn, reads=[d_sv], writes=[d_sv])
        kb.dma("sp", V(LR), lamre[l], writes=[d_sv])
        kb.dma("sp", V(LI), lamim[l], writes=[d_sv])
        kb.dma("sp", V(DT), logdt[l], writes=[d_sv])
        dve(lambda: nc.vector.tensor_scalar(out=V(LR), in0=V(LR), scalar1=-1e-4, scalar2=None, op0=ALU.min))
        act(lambda: nc.scalar.activation(out=V(DT), in_=V(DT), func=AF.Exp))
        dve(lambda: nc.vector.tensor_tensor(out=V(LDR), in0=V(LR), in1=V(DT), op=ALU.mult))
        dve(lambda: nc.vector.tensor_tensor(out=V(LDI), in0=V(LI), in1=V(DT), op=ALU.mult))
        act(lambda: nc.scalar.activation(out=rho[:], in_=V(LDR), func=AF.Exp))

        def sin_of(dst, shift):
            dve(lambda: nc.vector.tensor_scalar(out=V(ARG), in0=V(LDI), scalar1=float(shift), scalar2=None, op0=ALU.add))
            dve(lambda: nc.vector.tensor_scalar(out=V(KF), in0=V(ARG), scalar1=1.0 / TWO_PI, scalar2=None, op0=ALU.mult))
            dve(lambda: nc.vector.tensor_copy(out=svi[:], in_=V(KF)))
            dve(lambda: nc.vector.tensor_copy(out=V(KF), in_=svi[:]))
            dve(lambda: nc.vector.scalar_tensor_tensor(out=V(ARG), in0=V(KF), scalar=-TWO_PI, in1=V(ARG), op0=ALU.mult, op1=ALU.add))
            dve(lambda: nc.vector.tensor_scalar(out=V(KF), in0=V(ARG), scalar1=float(np.pi), scalar2=-TWO_PI, op0=ALU.is_gt, op1=ALU.mult))
            dve(lambda: nc.vector.tensor_tensor(out=V(ARG), in0=V(ARG), in1=V(KF), op=ALU.add))
            dve(lambda: nc.vector.tensor_scalar(out=V(KF), in0=V(ARG), scalar1=float(-np.pi), scalar2=TWO_PI, op0=ALU.is_lt, op1=ALU.mult))
            dve(lambda: nc.vector.tensor_tensor(out=V(ARG), in0=V(ARG), in1=V(KF), op=ALU.add))
            act(lambda: nc.scalar.activation(out=V(dst), in_=V(ARG), func=AF.Sin))
        sin_of(SIN, 0.0)
        sin_of(COS, np.pi / 2)
        dve(lambda: nc.vector.tensor_tensor(out=V(AR), in0=rho[:], in1=V(COS), op=ALU.mult))
        dve(lambda: nc.vector.tensor_tensor(out=V(AI), in0=rho[:], in1=V(SIN), op=ALU.mult))
        dve(lambda: nc.vector.tensor_tensor(out=V(DEN), in0=V(LR), in1=V(LR), op=ALU.mult))
        dve(lambda: nc.vector.tensor_tensor(out=V(TMP), in0=V(LI), in1=V(LI), op=ALU.mult))
        dve(lambda: nc.vector.tensor_tensor(out=V(DEN), in0=V(DEN), in1=V(TMP), op=ALU.add))
        dve(lambda: nc.vector.reciprocal(out=V(DEN), in_=V(DEN)))
        dve(lambda: nc.vector.tensor_scalar(out=V(NR), in0=V(AR), scalar1=-1.0, scalar2=None, op0=ALU.add))
        dve(lambda: nc.vector.tensor_tensor(out=V(TMP), in0=V(NR), in1=V(LR), op=ALU.mult))
        dve(lambda: nc.vector.tensor_tensor(out=V(TMP2), in0=V(AI), in1=V(LI), op=ALU.mult))
        dve(lambda: nc.vector.tensor_tensor(out=V(QR), in0=V(TMP), in1=V(TMP2), op=ALU.add))
        dve(lambda: nc.vector.tensor_tensor(out=V(QR), in0=V(QR), in1=V(DEN), op=ALU.mult))
        dve(lambda: nc.vector.tensor_tensor(out=V(TMP), in0=V(AI), in1=V(LR), op=ALU.mult))
        dve(lambda: nc.vector.tensor_tensor(out=V(TMP2), in0=V(NR), in1=V(LI), op=ALU.mult))
        dve(lambda: nc.vector.tensor_tensor(out=V(QI), in0=V(TMP), in1=V(TMP2), op=ALU.subtract))
        dve(lambda: nc.vector.tensor_tensor(out=V(QI), in0=V(QI), in1=V(DEN), op=ALU.mult))
        Bre = kb.sb("Bre", [128, 8, 128], F32)
        Bim = kb.sb("Bim", [128, 8, 128], F32)
        Bb = kb.sb("Bb", [128, 8, 128], F32)
        Bt = kb.sb("Bt", [128, 8, 128], F32)
        Cld = kb.sb("Cld", [128, 8, 128], F32)
        d_B = Dep()
        d_Bb = Dep()
        d_C = Dep()
        kb.dma("sp", Bre[:], bre_pad[l], writes=[d_B])
        kb.dma("sp", Bim[:], bim_pad[l], writes=[d_B])

        def bc8(i):
            return V(i).unsqueeze(2).broadcast_to([128, 8, 128])
        for comp in range(2):
            if comp == 0:
                kb.op("dve", lambda: nc.vector.tensor_tensor(out=Bb[:], in0=Bre[:], in1=bc8(QR), op=ALU.mult), reads=[d_B, d_sv], writes=[d_Bb])
                kb.op("dve", lambda: nc.vector.tensor_tensor(out=Bt[:], in0=Bim[:], in1=bc8(QI), op=ALU.mult), reads=[d_B, d_sv], writes=[d_Bb])
                kb.op("dve", lambda: nc.vector.tensor_tensor(out=Bb[:], in0=Bb[:], in1=Bt[:], op=ALU.subtract), reads=[d_Bb], writes=[d_Bb])
                dstW = WBre
            else:
                kb.op("dve", lambda: nc.vector.tensor_tensor(out=Bb[:], in0=Bim[:], in1=bc8(QR), op=ALU.mult), reads=[d_B, d_sv], writes=[d_Bb])
                kb.op("dve", lambda: nc.vector.tensor_tensor(out=Bt[:], in0=Bre[:], in1=bc8(QI), op=ALU.mult), reads=[d_B, d_sv], writes=[d_Bb])
                kb.op("dve", lambda: nc.vector.tensor_tensor(out=Bb[:], in0=Bb[:], in1=Bt[:], op=ALU.add), reads=[d_Bb], writes=[d_Bb])
                dstW = WBim
            for half in range(2):
                b = gbank()
                for jj in range(4):
                    j = half * 4 + jj
                    kb.op("pe", lambda j=j, jj=jj: nc.tensor.transpose(out=pbank[b][:, jj * 128:(jj + 1) * 128], in_=Bb[:, j, :],
                                                                       identity=ident_f[:]),
                          reads=[d_Bb, d_const], writes=[pdep[b]])
                kb.op("act", lambda half=half, dstW=dstW: nc.scalar.copy(
                    out=dstW[:, half * 4:(half + 1) * 4, :], in_=pbank[b][:, :].rearrange("p (j c) -> p j c", j=4)),
                    reads=[pdep[b]], writes=[d_s5w])
        kb.dma("sp", Cld[:], cre_pad[l], writes=[d_C])
        kb.op("act", lambda: nc.scalar.copy(out=WCre[:], in_=Cld[:]), reads=[d_C], writes=[d_s5w])
        kb.dma("sp", Cld[:], cim_pad[l], reads=[d_C], writes=[d_C])
        kb.op("act", lambda: nc.scalar.mul(out=WCimn[:], in_=Cld[:], mul=-1.0), reads=[d_C], writes=[d_s5w])
        kb.op("dve", lambda: nc.vector.tensor_copy(out=Ecos[:, :, 0], in_=V(COS)), reads=[d_sv], writes=[d_tab])
        kb.op("dve", lambda: nc.vector.tensor_copy(out=Esin[:, :, 0], in_=V(SIN)), reads=[d_sv], writes=[d_tab])
        tt = kb.sb("tt", [128, 4, 8, T // 2], F32)
        n = 1
        while n < T:
            cb = Ecos[:, :, n - 1:n].broadcast_to([128, 8, n])
            sbb = Esin[:, :, n - 1:n].broadcast_to([128, 8, n])
            kb.op("dve", lambda n=n, cb=cb: nc.vector.tensor_tensor(out=tt[:, 0, :, 0:n], in0=Ecos[:, :, 0:n], in1=cb, op=ALU.mult), reads=[d_tab], writes=[d_tab])
            kb.op("dve", lambda n=n, sbb=sbb: nc.vector.tensor_tensor(out=tt[:, 1, :, 0:n], in0=Esin[:, :, 0:n], in1=sbb, op=ALU.mult), reads=[d_tab], writes=[d_tab])
            kb.op("dve", lambda n=n, sbb=sbb: nc.vector.tensor_tensor(out=tt[:, 2, :, 0:n], in0=Ecos[:, :, 0:n], in1=sbb, op=ALU.mult), reads=[d_tab], writes=[d_tab])
            kb.op("dve", lambda n=n, cb=cb: nc.vector.tensor_tensor(out=tt[:, 3, :, 0:n], in0=Esin[:, :, 0:n], in1=cb, op=ALU.mult), reads=[d_tab], writes=[d_tab])
            kb.op("dve", lambda n=n: nc.vector.tensor_tensor(out=Ecos[:, :, n:2 * n], in0=tt[:, 0, :, 0:n], in1=tt[:, 1, :, 0:n], op=ALU.subtract), reads=[d_tab], writes=[d_tab])
            kb.op("dve", lambda n=n: nc.vector.tensor_tensor(out=Esin[:, :, n:2 * n], in0=tt[:, 2, :, 0:n], in1=tt[:, 3, :, 0:n], op=ALU.add), reads=[d_tab], writes=[d_tab])
            n *= 2
        kb.op("dve", lambda: nc.vector.tensor_copy(out=Mmul[:], in_=rho[:].unsqueeze(2).broadcast_to([128, 8, T])), reads=[d_sv], writes=[d_tab])
        kb.op("dve", lambda: nc.vector.memset(Mmul[:, :, 0:1], 0.0), reads=[d_tab], writes=[d_tab])
        kb.barrier()
        kb.release(m1)

        feat_r = Rot([kb.sb(f"featb{i}", [128, 8, T], F32) for i in range(2)])
        u_bf = kb.sb("u_bf", [128, 2, T], BF16)
        d_ubf = Dep()
        tmp = [kb.sb(f"s5t{i}", [128, 4, T], F32) for i in range(8)]
        d_tmp = [Dep() for _ in range(8)]
        hr_bf = kb.sb("hr_bf", [128, 4, T], BF16)
        hi_bf = kb.sb("hi_bf", [128, 4, T], BF16)
        d_hbf = Dep()
        hin = kb.sb("hin", [128, 2, 8], F32)
        d_hin = Dep()
        hsm = kb.sb("hsm", [128, 8, 8], F32)
        d_hsm = Dep()
        yv = [kb.sb(f"yv{i}", [128, 2, T], F32) for i in range(10)]
        d_yv = [Dep() for _ in range(10)]
        yY_r = Rot([yv[0], yv[7]])
        g_bf = kb.sb("g_bf", [128, 2, T], BF16)
        d_gbf = Dep()
        zp = kb.sb("zp", [128, 2, T + 2], F32)
        d_zp = Dep()
        rstd_r = Rot([kb.sb(f"rstdb{i}", [128, T], F32) for i in range(2)])
        att_r = Rot([kb.sb(f"attl{i}", [128, ATT], F32) for i in range(2)])
        attn_bf = kb.sb("attn_bf", [128, ATT], BF16)
        d_attn = Dep()
        mixT = kb.sb("mixT", [128, 8, T], BF16)
        d_mixT = Dep()
        xmid2 = [kb.sb(f"xmid{i}", [128, 4, D], F32) for i in range(2)]
        d_xmid2 = [[Dep() for _ in range(4)] for _ in range(2)]
        sqs = kb.sb("sqsb", [128, D], F32)
        d_sqs = Dep()
        stat_r = Rot([kb.sb(f"statb{i}", [128, 8], F32) for i in range(4)])
        h2 = kb.sb("h2", [128, D], BF16)
        d_h2 = Dep()
        h2T2 = [kb.sb(f"h2T{i}", [128, 8, 512], BF16) for i in range(2)]
        d_h2T2 = [[Dep() for _ in range(4)] for _ in range(2)]
        f2T = kb.sb("f2T", [128, 32, 512], BF16)
        d_f2T = [Dep() for _ in range(8)]
        wup_r = Rot([kb.sb(f"wupb{i}", [128, 8, 256], BF16) for i in range(2)])
        wdn_r = Rot([kb.sb(f"wdnb{i}", [128, 4, 512], BF16) for i in range(2)])
        relu_r = Rot([kb.sb(f"relub{i}", [128, 512], F32) for i in range(2)])
        oe_r = Rot([kb.sb(f"oeb{i}", [128, 512], F32) for i in range(2)])
        cst_out = kb.sb("cst_out", [128, 2, 2], F32)
        d_cst_out = Dep()

        def s5_head(tg, tok0, Tc, first, ctx):
            ft, d_ft = feat_r.next()
            ctx["ft"] = (ft, d_ft)
            yY, d_yY = yY_r.next()
            ctx["y"] = (yY, d_yY)
            kb.dma("sp", ft[:, :, 0:Tc], featT[l].rearrange("(f p) t -> p f t", p=128)[:, :, tok0:tok0 + Tc],
                   reads=[d_feat[l][tg]], writes=[d_ft])
            kb.op("act", lambda: nc.scalar.copy(out=u_bf[:, :, 0:Tc], in_=ft[:, 0:2, 0:Tc]), reads=[d_ft], writes=[d_ubf])
            for half in range(2):
                j0 = half * 4
                bre, bim = gbank(), gbank()
                for jj in range(4):
                    j = j0 + jj
                    kb.op("pe", lambda j=j, jj=jj: nc.tensor.matmul(pbank[bre][:, jj * T:jj * T + Tc], lhsT=WBre[:, j, :],
                                                                   rhs=u_bf[:, half, 0:Tc], start=True, stop=True),
                          reads=[d_s5w, d_ubf], writes=[pdep[bre]])
                for jj in range(4):
                    j = j0 + jj
                    kb.op("pe", lambda j=j, jj=jj: nc.tensor.matmul(pbank[bim][:, jj * T:jj * T + Tc], lhsT=WBim[:, j, :],
                                                                   rhs=u_bf[:, half, 0:Tc], start=True, stop=True),
                          reads=[d_s5w, d_ubf], writes=[pdep[bim]])
                pre = pbank[bre][:, :].rearrange("p (j t) -> p j t", j=4)[:, :, 0:Tc]
                pim = pbank[bim][:, :].rearrange("p (j t) -> p j t", j=4)[:, :, 0:Tc]
                ec = Ecos[:, j0:j0 + 4, 0:Tc]
                es = Esin[:, j0:j0 + 4, 0:Tc]
                t = [x[:, :, 0:Tc] for x in tmp]
                kb.op("dve", lambda: nc.vector.tensor_tensor(out=t[0], in0=pre, in1=ec, op=ALU.mult), reads=[pdep[bre], d_tab], writes=[d_tmp[0]])
                kb.op("dve", lambda: nc.vector.tensor_tensor(out=t[1], in0=pim, in1=es, op=ALU.mult), reads=[pdep[bim], d_tab], writes=[d_tmp[1]])
                kb.op("dve", lambda: nc.vector.tensor_tensor(out=t[2], in0=pim, in1=ec, op=ALU.mult), reads=[pdep[bim], d_tab], writes=[d_tmp[2]])
                kb.op("dve", lambda: nc.vector.tensor_tensor(out=t[3], in0=pre, in1=es, op=ALU.mult), reads=[pdep[bre], d_tab], writes=[d_tmp[3]])
                kb.op("dve", lambda: nc.vector.tensor_tensor(out=t[4], in0=t[0], in1=t[1], op=ALU.add), reads=[d_tmp[0], d_tmp[1]], writes=[d_tmp[4]])
                kb.op("dve", lambda: nc.vector.tensor_tensor(out=t[5], in0=t[2], in1=t[3], op=ALU.subtract), reads=[d_tmp[2], d_tmp[3]], writes=[d_tmp[5]])
                if not first:
                    kb.op("dve", lambda: nc.vector.tensor_tensor(out=hsm[:, 0, 0:4], in0=hin[:, 0, j0:j0 + 4], in1=rho[:, j0:j0 + 4], op=ALU.mult),
                          reads=[d_hin, d_tab], writes=[d_hsm])
                    kb.op("dve", lambda: nc.vector.tensor_tensor(out=hsm[:, 1, 0:4], in0=hin[:, 1, j0:j0 + 4], in1=rho[:, j0:j0 + 4], op=ALU.mult),
                          reads=[d_hin, d_tab], writes=[d_hsm])
                    kb.op("dve", lambda: nc.vector.tensor_tensor(out=tmp[4][:, :, 0], in0=tmp[4][:, :, 0], in1=hsm[:, 0, 0:4], op=ALU.add),
                          reads=[d_hsm, d_tmp[4]], writes=[d_tmp[4]])
                    kb.op("dve", lambda: nc.vector.tensor_tensor(out=tmp[5][:, :, 0], in0=tmp[5][:, :, 0], in1=hsm[:, 1, 0:4], op=ALU.add),
                          reads=[d_hsm, d_tmp[5]], writes=[d_tmp[5]])
                if Tc == T:
                    for (src, dst) in ((4, 6), (5, 7)):
                        kb.op("dve", lambda src=src, dst=dst: nc.vector.tensor_tensor_scan(
                            out=tmp[dst][:].rearrange("p j t -> p (j t)"), data0=Mmul[:, j0:j0 + 4, :].rearrange("p j t -> p (j t)"),
                            data1=tmp[src][:].rearrange("p j t -> p (j t)"), initial=0.0, op0=ALU.mult, op1=ALU.add),
                            reads=[d_tmp[src], d_tab], writes=[d_tmp[dst]])
                else:
                    for (src, dst) in ((4, 6), (5, 7)):
                        for jj in range(4):
                            kb.op("dve", lambda src=src, dst=dst, jj=jj: nc.vector.tensor_tensor_scan(
                                out=tmp[dst][:, jj, 0:Tc], data0=Mmul[:, j0 + jj, 0:Tc], data1=tmp[src][:, jj, 0:Tc],
                                initial=0.0, op0=ALU.mult, op1=ALU.add),
                                reads=[d_tmp[src], d_tab], writes=[d_tmp[dst]])
                kb.op("dve", lambda: nc.vector.tensor_tensor(out=t[0], in0=t[6], in1=ec, op=ALU.mult), reads=[d_tmp[6], d_tab], writes=[d_tmp[0]])
                kb.op("dve", lambda: nc.vector.tensor_tensor(out=t[1], in0=t[7], in1=es, op=ALU.mult), reads=[d_tmp[7], d_tab], writes=[d_tmp[1]])
                kb.op("dve", lambda: nc.vector.tensor_tensor(out=t[2], in0=t[7], in1=ec, op=ALU.mult), reads=[d_tmp[7], d_tab], writes=[d_tmp[2]])
                kb.op("dve", lambda: nc.vector.tensor_tensor(out=t[3], in0=t[6], in1=es, op=ALU.mult), reads=[d_tmp[6], d_tab], writes=[d_tmp[3]])
                kb.op("dve", lambda: nc.vector.tensor_tensor(out=hr_bf[:, :, 0:Tc], in0=t[0], in1=t[1], op=ALU.subtract),
                      reads=[d_tmp[0], d_tmp[1]], writes=[d_hbf])
                kb.op("dve", lambda: nc.vector.tensor_tensor(out=hi_bf[:, :, 0:Tc], in0=t[2], in1=t[3], op=ALU.add),
                      reads=[d_tmp[2], d_tmp[3]], writes=[d_hbf])
                kb.op("dve", lambda: nc.vector.tensor_tensor(out=hin[:, 0, j0:j0 + 4], in0=tmp[0][:, :, Tc - 1], in1=tmp[1][:, :, Tc - 1], op=ALU.subtract),
                      reads=[d_tmp[0], d_tmp[1], d_hsm], writes=[d_hin])
                kb.op("dve", lambda: nc.vector.tensor_tensor(out=hin[:, 1, j0:j0 + 4], in0=tmp[2][:, :, Tc - 1], in1=tmp[3][:, :, Tc - 1], op=ALU.add),
                      reads=[d_tmp[2], d_tmp[3], d_hsm], writes=[d_hin])
                yield
                by = gbank()
                for jj in range(4):
                    j = j0 + jj
                    kb.op("pe", lambda j=j, jj=jj: nc.tensor.matmul(pbank[by][:, 0:Tc], lhsT=WCre[:, j, :], rhs=hr_bf[:, jj, 0:Tc],
                                                                   start=(jj == 0), stop=False),
                          reads=[d_s5w, d_hbf], writes=[pdep[by]])
                    kb.op("pe", lambda j=j, jj=jj: nc.tensor.matmul(pbank[by][:, 0:Tc], lhsT=WCimn[:, j, :], rhs=hi_bf[:, jj, 0:Tc],
                                                                   start=False, stop=(jj == 3)),
                          reads=[d_s5w, d_hbf], writes=[pdep[by]])
                kb.op("dve", lambda: nc.vector.scalar_tensor_tensor(out=yY[:, half, 0:Tc], in0=ft[:, half, 0:Tc], scalar=dv[:, half:half + 1],
                                                                    in1=pbank[by][:, 0:Tc], op0=ALU.mult, op1=ALU.add),
                      reads=[d_ft, d_vec, pdep[by]], writes=[d_yY])

        def s5_tail(tg, tok0, Tc, ctx):
            c0 = tok0 - tg * 128
            ft, d_ft = ctx["ft"]
            yv[0], d_yv[0] = ctx["y"]
            Y, X2, Z, SG, G, GATE, SSMV, ZC, YC, CV = range(10)

            def yy(i):
                return yv[i][:, :, 0:Tc]
            kb.op("dve", lambda: nc.vector.tensor_tensor(out=yy(X2), in0=yy(Y), in1=yy(Y), op=ALU.mult), reads=[d_yv[Y]], writes=[d_yv[X2]])
            kb.op("dve", lambda: nc.vector.tensor_scalar(out=yy(X2), in0=yy(X2), scalar1=0.044715, scalar2=1.0, op0=ALU.mult, op1=ALU.add),
                  reads=[d_yv[X2]], writes=[d_yv[X2]])
            kb.op("dve", lambda: nc.vector.tensor_tensor(out=yy(Z), in0=yy(X2), in1=yy(Y), op=ALU.mult), reads=[d_yv[X2], d_yv[Y]], writes=[d_yv[Z]])
            kb.op("act", lambda: nc.scalar.activation(out=yy(SG), in_=yy(Z), func=AF.Sigmoid, scale=1.5957691216057308),
                  reads=[d_yv[Z]], writes=[d_yv[SG]])
            kb.op("dve", lambda: nc.vector.tensor_tensor(out=yy(G), in0=yy(SG), in1=yy(Y), op=ALU.mult), reads=[d_yv[SG], d_yv[Y]], writes=[d_yv[G]])
            kb.op("act", lambda: nc.scalar.copy(out=g_bf[:, :, 0:Tc], in_=yy(G)), reads=[d_yv[G]], writes=[d_gbf])
            yield
            for mo in range(2):
                bgt = gbank()
                for mi in range(2):
                    kb.op("pe", lambda mi=mi, mo=mo: nc.tensor.matmul(pbank[bgt][:, 0:Tc], lhsT=w_glu_sb[:, mi, mo * 128:(mo + 1) * 128],
                                                                     rhs=g_bf[:, mi, 0:Tc], start=(mi == 0), stop=(mi == 1)),
                          reads=[d_wres, d_gbf], writes=[pdep[bgt]])
                kb.op("act", lambda mo=mo: nc.scalar.activation(out=yv[GATE][:, mo, 0:Tc], in_=pbank[bgt][:, 0:Tc], func=AF.Sigmoid,
                                                                bias=bg[:, mo:mo + 1]),
                      reads=[pdep[bgt], d_vec], writes=[d_yv[GATE]])
            kb.op("dve", lambda: nc.vector.tensor_tensor(out=yy(SSMV), in0=yy(G), in1=yy(GATE), op=ALU.mult), reads=[d_yv[G], d_yv[GATE]], writes=[d_yv[SSMV]])
            kb.op("dve", lambda: nc.vector.tensor_tensor(out=zp[:, :, 2:2 + Tc], in0=ft[:, 6:8, 0:Tc], in1=ft[:, 2:4, 0:Tc], op=ALU.mult),
                  reads=[d_ft, d_zp], writes=[d_zp])
            for m in range(2):
                kb.op("dve", lambda m=m: nc.vector.tensor_scalar(out=yv[YC][:, m, 0:Tc], in0=zp[:, m, 0:Tc], scalar1=cw[:, m, 0:1], scalar2=None,
                                                                 op0=ALU.mult), reads=[d_zp, d_vec], writes=[d_yv[YC]])
                kb.op("dve", lambda m=m: nc.vector.scalar_tensor_tensor(out=yv[YC][:, m, 0:Tc], in0=zp[:, m, 1:1 + Tc], scalar=cw[:, m, 1:2],
                                                                        in1=yv[YC][:, m, 0:Tc], op0=ALU.mult, op1=ALU.add),
                      reads=[d_zp, d_vec, d_yv[YC]], writes=[d_yv[YC]])
                kb.op("dve", lambda m=m: nc.vector.scalar_tensor_tensor(out=yv[YC][:, m, 0:Tc], in0=zp[:, m, 2:2 + Tc], scalar=cw[:, m, 2:3],
                                                                        in1=yv[YC][:, m, 0:Tc], op0=ALU.mult, op1=ALU.add),
                      reads=[d_zp, d_vec, d_yv[YC]], writes=[d_yv[YC]])
            kb.op("dve", lambda: nc.vector.tensor_tensor(out=yy(CV), in0=yy(YC), in1=ft[:, 4:6, 0:Tc], op=ALU.mult), reads=[d_yv[YC], d_ft], writes=[d_yv[CV]])
            kb.op("dve", lambda: nc.vector.tensor_copy(out=hsm[:, 4:6, 0:2], in_=zp[:, :, Tc:Tc + 2]), reads=[d_zp], writes=[d_hsm])
            kb.op("dve", lambda: nc.vector.tensor_copy(out=zp[:, :, 0:2], in_=hsm[:, 4:6, 0:2]), reads=[d_hsm, d_zp], writes=[d_zp])
            for (src, bnv, k0) in ((SSMV, bns, 4), (CV, bnc, 6)):
                kb.op("act", lambda src=src: nc.scalar.activation(out=yy(X2), in_=yy(src), func=AF.Square), reads=[d_yv[src]], writes=[d_yv[X2]])
                yield
                bss = gbank()
                for m in range(2):
                    kb.op("pe", lambda m=m: nc.tensor.matmul(pbank[bss][:, 0:Tc], lhsT=ones_f[:], rhs=yv[X2][:, m, 0:Tc], start=(m == 0), stop=(m == 1)),
                          reads=[d_yv[X2], d_const], writes=[pdep[bss]])
                rs, d_rs = rstd_r.next()
                kb.op("act", lambda: nc.scalar.activation(out=rs[:, 0:Tc], in_=pbank[bss][:, 0:Tc], func=AF.Ln, scale=1.0 / SSM, bias=EPS),
                      reads=[pdep[bss]], writes=[d_rs])
                kb.op("act", lambda: nc.scalar.activation(out=rs[:, 0:Tc], in_=rs[:, 0:Tc], func=AF.Exp, scale=-0.5), reads=[d_rs], writes=[d_rs])
                for m in range(2):
                    kb.op("dve", lambda m=m, src=src, bnv=bnv, k0=k0: nc.vector.scalar_tensor_tensor(
                        out=mixT[:, k0 + m, c0:c0 + Tc], in0=yv[src][:, m, 0:Tc], scalar=bnv[:, m:m + 1], in1=rs[:, 0:Tc],
                        op0=ALU.mult, op1=ALU.mult),
                        reads=[d_yv[src], d_vec, d_rs], writes=[d_mixT])

        def token_part(tg, sub, par):
            xmid, d_xmid, h2T, d_h2T = xmid2[par], d_xmid2[par], h2T2[par], d_h2T2[par]
            at, d_at = att_r.next()
            kb.dma("sp", at[:], att_s[l][tg * 128:(tg + 1) * 128, :], reads=[d_att[l][tg]], writes=[d_at])
            kb.dma("sp", xmid[:, sub, :], x_src[tg * 128:(tg + 1) * 128, :], reads=[d_xsrc[tg]], writes=[d_xmid[sub]])
            st, d_st = stat_r.next()
            kb.op("act", lambda: nc.scalar.activation(out=sqs[:, 0:ATT], in_=at[:], func=AF.Square, accum_out=st[:, 0:1]),
                  reads=[d_at], writes=[d_sqs, d_st])
            kb.op("act", lambda: nc.scalar.activation(out=st[:, 1:2], in_=st[:, 0:1], func=AF.Ln, scale=1.0 / ATT, bias=EPS), reads=[d_st], writes=[d_st])
            kb.op("act", lambda: nc.scalar.activation(out=st[:, 2:3], in_=st[:, 1:2], func=AF.Exp, scale=-0.5), reads=[d_st], writes=[d_st])
            kb.op("dve", lambda: nc.vector.scalar_tensor_tensor(out=attn_bf[:], in0=at[:], scalar=st[:, 2:3], in1=bna[:], op0=ALU.mult, op1=ALU.mult),
                  reads=[d_at, d_st, d_vec], writes=[d_attn])
            yield
            b = gbank()
            pb16 = pbank[b][:].bitcast(BF16)
            for c in range(4):
                kb.op("pe", lambda c=c: nc.tensor.transpose(out=pb16[:, c * 128:(c + 1) * 128], in_=attn_bf[:, c * 128:(c + 1) * 128], identity=ident_b[:]),
                      reads=[d_attn, d_const], writes=[pdep[b]])
            kb.op("act", lambda: nc.scalar.copy(out=mixT[:, 0:4, :], in_=pb16[:, 0:512].rearrange("p (c t) -> p c t", c=4)),
                  reads=[pdep[b]], writes=[d_mixT])
            yield
            for nch in range(2):
                bo = gbank()
                for kt in range(8):
                    kb.op("pe", lambda kt=kt: nc.tensor.matmul(pbank[bo][:, :], lhsT=mixT[:, kt, :], rhs=w_out_sb[:, kt, nch * 512:(nch + 1) * 512],
                                                              start=(kt == 0), stop=(kt == 7)),
                          reads=[d_mixT, d_wres], writes=[pdep[bo]])
                kb.op("dve", lambda: nc.vector.tensor_tensor(out=xmid[:, sub, nch * 512:(nch + 1) * 512], in0=pbank[bo][:, :],
                                                             in1=xmid[:, sub, nch * 512:(nch + 1) * 512], op=ALU.add),
                      reads=[pdep[bo], d_xmid[sub]], writes=[d_xmid[sub]])
            st, d_st = stat_r.next()
            kb.op("act", lambda: nc.scalar.activation(out=sqs[:], in_=xmid[:, sub, :], func=AF.Square, accum_out=st[:, 0:1]),
                  reads=[d_xmid[sub]], writes=[d_sqs, d_st])
            kb.op("act", lambda: nc.scalar.activation(out=st[:, 1:2], in_=st[:, 0:1], func=AF.Ln, scale=1.0 / D, bias=EPS), reads=[d_st], writes=[d_st])
            kb.op("act", lambda: nc.scalar.activation(out=st[:, 2:3], in_=st[:, 1:2], func=AF.Exp, scale=-0.5), reads=[d_st], writes=[d_st])
            kb.op("dve", lambda: nc.vector.scalar_tensor_tensor(out=h2[:], in0=xmid[:, sub, :], scalar=st[:, 2:3], in1=ln2[:], op0=ALU.mult, op1=ALU.mult),
                  reads=[d_xmid[sub], d_st, d_vec], writes=[d_h2])
            yield
            b = gbank()
            pb16 = pbank[b][:].bitcast(BF16)
            for c in range(8):
                kb.op("pe", lambda c=c: nc.tensor.transpose(out=pb16[:, c * 128:(c + 1) * 128], in_=h2[:, c * 128:(c + 1) * 128], identity=ident_b[:]),
                      reads=[d_h2, d_const], writes=[pdep[b]])
            kb.op("act", lambda: nc.scalar.copy(out=h2T[:, :, sub * 128:(sub + 1) * 128], in_=pb16[:, 0:1024].rearrange("p (c t) -> p c t", c=8)),
                  reads=[pdep[b]], writes=[d_h2T[sub]])

        def ffn(tg0, nsub, par):
            xmid, d_xmid, h2T, d_h2T = xmid2[par], d_xmid2[par], h2T2[par], d_h2T2[par]
            W = nsub * 128
            subs = list(range(nsub))
            for fc in range(16):
                wu, d_wu = wup_r.next()
                kb.dma("pool", wu[:], wup_bf[l, fc], reads=[d_wup_bf[l]], writes=[d_wu])
                for i in range(2):
                    bu = gbank()
                    for kt in range(8):
                        kb.op("pe", lambda kt=kt, i=i: nc.tensor.matmul(pbank[bu][:, 0:W], lhsT=wu[:, kt, i * 128:(i + 1) * 128], rhs=h2T[:, kt, 0:W],
                                                                       start=(kt == 0), stop=(kt == 7)),
                              reads=[d_wu] + [d_h2T[s_] for s_ in subs], writes=[pdep[bu]])
                    rl, d_rl = relu_r.next()
                    kb.op("act", lambda: nc.scalar.activation(out=rl[:, 0:W], in_=pbank[bu][:, 0:W], func=AF.Relu), reads=[pdep[bu]], writes=[d_rl])
                    kb.op("act", lambda fc=fc, i=i: nc.scalar.activation(out=f2T[:, fc * 2 + i, 0:W], in_=rl[:, 0:W], func=AF.Square),
                          reads=[d_rl], writes=[d_f2T[fc // 2]])
                    yield
            for c in range(2):
                for ig in range(8):
                    wd, d_wd = wdn_r.next()
                    kb.dma("pool", wd[:], wdn_bf[l, c, ig], reads=[d_wdn_bf[l]], writes=[d_wd])
                    for ii in range(4):
                        for s_ in subs:
                            kb.op("pe", lambda ii=ii, s_=s_, ig=ig: nc.tensor.matmul(
                                pbank[PB_ACC[s_]][:, :], lhsT=f2T[:, ig * 4 + ii, s_ * 128:(s_ + 1) * 128], rhs=wd[:, ii, :],
                                start=(ig == 0 and ii == 0), stop=(ig == 7 and ii == 3)),
                                reads=[d_wd, d_f2T[ig]], writes=[pdep[PB_ACC[s_]]])
                        if ii % 2 == 1 and not (ig == 7 and ii == 3):
                            yield
                for s_ in subs:
                    oe, d_oe = oe_r.next()
                    kb.op("dve", lambda s_=s_: nc.vector.tensor_tensor(out=oe[:], in0=pbank[PB_ACC[s_]][:, :], in1=xmid[:, s_, c * 512:(c + 1) * 512], op=ALU.add),
                          reads=[pdep[PB_ACC[s_]], d_xmid[s_]], writes=[d_oe])
                    tg = tg0 + s_
                    kb.dma("sp", x_dst[tg * 128:(tg + 1) * 128, c * 512:(c + 1) * 512], oe[:], reads=[d_oe], writes=[d_xdst[tg]])
                yield

        def store_hin(idx):
            kb.dma("sp", sre_out[l, idx], hin[:, 0, :], reads=[d_hin])
            kb.dma("sp", sim_out[l, idx], hin[:, 1, :], reads=[d_hin])

        def store_conv(idx):
            kb.op("dve", lambda: nc.vector.tensor_copy(out=cst_out[:], in_=zp[:, :, 0:2]), reads=[d_zp], writes=[d_cst_out])
            kb.dma("sp", conv_out[l, idx], cst_out[:], reads=[d_cst_out])

        kb.op("dve", lambda: nc.vector.memset(zp[:, :, 0:2], 0.0), writes=[d_zp])

        def chunk_list(Qm):
            out = []
            if Qm < NQ:
                par = Qm % 2
                for sub in range(4):
                    tg = 4 * Qm + sub
                    ctx = {}

                    def head(tg=tg, ctx=ctx):
                        yield from s5_head(tg, tg * 128, T, tg == 0, ctx)
                        if tg == NTP - 1:
                            store_hin(0)

                    def tail(tg=tg, ctx=ctx, sub=sub):
                        yield from s5_tail(tg, tg * 128, T, ctx)
                        if tg == NTP - 1:
                            store_conv(0)
                        yield from token_part(tg, sub, par)
                    out.append((head, tail))
            else:
                par = NQ % 2
                for s in range(2):
                    ctx = {}

                    def head(s=s, ctx=ctx):
                        kb.dma("sp", hin[:, 0, :], sre[l, s], reads=[d_hin], writes=[d_hin])
                        kb.dma("sp", hin[:, 1, :], sim[l, s], reads=[d_hin], writes=[d_hin])
                        yield from s5_head(NTP, NTP * 128 + s * 64, 64, False, ctx)
                        store_hin(1 + s)

                    def tail(s=s, ctx=ctx):
                        kb.dma("sp", zp[:, :, 0:2], sconv[l, s], reads=[d_zp], writes=[d_zp])
                        yield from s5_tail(NTP, NTP * 128 + s * 64, 64, ctx)
                        store_conv(1 + s)
                        if s == 1:
                            yield from token_part(NTP, 0, par)
                    out.append((head, tail))
            return out

        def cosched(chunks, gffn, K=1, TS=1):
            n = len(chunks)
            hc, tc = 0, 0
            gh = chunks[0][0]() if n else None
            gt = None
            head_done = [False] * n
            while hc < n or tc < n or gffn is not None:
                for _ in range(K):
                    if gffn is not None:
                        try:
                            next(gffn)
                        except StopIteration:
                            gffn = None
                if hc < n and hc <= tc + 1:
                    try:
                        next(gh)
                    except StopIteration:
                        head_done[hc] = True
                        hc += 1
                        gh = chunks[hc][0]() if hc < n else None
                for _ in range(TS):
                    if tc < n and head_done[tc]:
                        if gt is None:
                            gt = chunks[tc][1]()
                        try:
                            next(gt)
                        except StopIteration:
                            tc += 1
                            gt = None

        cosched(chunk_list(0), None)
        for Qm in range(NQ):
            cosched(chunk_list(Qm + 1), ffn(4 * Qm, 4, Qm % 2), K=(4 if Qm + 1 < NQ else 6), TS=2)
        cosched([], ffn(NTP, 1, NQ % 2))
        kb.barrier()
        kb.release(m0)
    d_xin = [Dep() for _ in range(NT)]
    d_y = [Dep() for _ in range(NT)]
    try:
        for l in range(nlayers):
            x_src, d_src = (x_all, d_xin) if l == 0 else (x1, d_x1)
            x_dst, d_dst = (y_out, d_y) if l == nlayers - 1 else (x1, d_x1)
            phase_a(l, x_src, d_src)
            stg(100 + 10 * l)
            phase_b(l, x_src, d_src, x_dst, d_dst)
            stg(101 + 10 * l)
    except StopBuild:
        pass
    kb.finish()
    print("instructions:", kb.ninst)
    return nc


def host_constants():
    k = np.arange(128)
    c = {}
    c["c_ident"] = np.eye(128, dtype=np.float32)
    c["c_utri"] = (k[:, None] <= k[None, :]).astype(np.float32)
    blk = (k[:, None] // 64) == (k[None, :] // 64)
    c["c_utri2"] = ((k[:, None] <= k[None, :]) & blk).astype(np.float32)
    c["c_ones"] = np.ones((128, 128), np.float32)
    c["c_lstrict"] = (k[:, None] > k[None, :]).astype(np.float32)
    c["c_maskT"] = np.where(k[:, None] <= k[None, :], 0.0, NEG).astype(np.float32)
    q = np.arange(64)
    m2 = np.full((2, 128, 64), NEG, np.float32)
    for s in range(2):
        kk = np.arange(64)
        m2[s, s * 64:(s + 1) * 64, :] = np.where(kk[:, None] <= q[None, :], 0.0, NEG)
    c["c_mask2"] = m2
    return c


def _state_layout(a):
    nl, S = a.shape[0], a.shape[1]
    return np.ascontiguousarray(a.reshape(nl, S, 8, 2, 64).transpose(0, 1, 3, 4, 2).reshape(nl, S, 128, 8))


def _state_unlayout(a):
    nl, S = a.shape[0], a.shape[1]
    return np.ascontiguousarray(a.reshape(nl, S, 2, 64, 8).transpose(0, 1, 4, 2, 3).reshape(nl, S, 16, 64))


def _fm2(a):
    nl = a.shape[0]
    return np.ascontiguousarray(a.reshape(nl, 2, 128).transpose(0, 2, 1))


def prep_shared(inp):
    f = np.float32
    nl = inp["w_in"].shape[0]
    d = {}
    for k in ["w_in", "w_out", "w_up", "w_down", "w_glu"]:
        d[k] = np.ascontiguousarray(inp[k], dtype=f)
    d["ln1b"] = np.ascontiguousarray(np.broadcast_to(inp["ln1_w"][:, None, :], (nl, 128, D)), dtype=f)
    d["ln2b"] = np.ascontiguousarray(np.broadcast_to(inp["ln2_w"][:, None, :], (nl, 128, D)), dtype=f)
    bn = np.asarray(inp["branch_norm_w"], dtype=f)
    d["bnatt"] = np.ascontiguousarray(np.broadcast_to(bn[:, None, :ATT], (nl, 128, ATT)), dtype=f)
    d["qnb"] = np.ascontiguousarray(np.broadcast_to(np.tile(inp["q_norm_w"], (1, NH))[:, None, :], (nl, 128, ATT)), dtype=f)
    d["knb"] = np.ascontiguousarray(np.broadcast_to(np.tile(inp["k_norm_w"], (1, NH))[:, None, :], (nl, 128, ATT)), dtype=f)
    d["bfb"] = np.ascontiguousarray(np.broadcast_to(inp["b_forget"][:, None, :], (nl, 128, NH)), dtype=f)
    cw = np.asarray(inp["conv_w"], dtype=f)
    d["convw"] = np.ascontiguousarray(cw.reshape(nl, 3, 2, 128).transpose(0, 3, 2, 1))
    d["dvec"] = _fm2(np.asarray(inp["ssm_d"], dtype=f))
    d["bglu"] = _fm2(np.asarray(inp["b_glu"], dtype=f))
    d["bnssm"] = _fm2(bn[:, ATT:ATT + SSM])
    d["bnconv"] = _fm2(bn[:, ATT + SSM:])
    d["lamre"] = _state_layout(np.asarray(inp["ssm_lam_re"], dtype=f)[:, None])[:, 0]
    d["lamim"] = _state_layout(np.asarray(inp["ssm_lam_im"], dtype=f)[:, None])[:, 0]
    ldt = np.broadcast_to(np.asarray(inp["ssm_log_dt"], dtype=f)[:, :, None], (nl, 16, 64))
    d["logdt"] = _state_layout(np.ascontiguousarray(ldt)[:, None])[:, 0]

    def pad_gpc(a):
        out = np.zeros((nl, 128, 8, 128), f)
        for g in range(16):
            out[:, (g % 2) * 64:(g % 2) * 64 + 64, g // 2, (g % 8) * 16:(g % 8) * 16 + 16] = a[:, g]
        return out
    d["bre_pad"] = pad_gpc(np.asarray(inp["ssm_b_re"], dtype=f))
    d["bim_pad"] = pad_gpc(np.asarray(inp["ssm_b_im"], dtype=f))
    d["cre_pad"] = pad_gpc(np.asarray(inp["ssm_c_re"], dtype=f).transpose(0, 1, 3, 2))
    d["cim_pad"] = pad_gpc(np.asarray(inp["ssm_c_im"], dtype=f).transpose(0, 1, 3, 2))
    d.update(host_constants())
    return d


def prep_core(inp, c, L, P):
    f = np.float32
    nl = inp["w_in"].shape[0]
    d = {}
    xs = np.asarray(inp["x_sample"][2 * c:2 * c + 2], dtype=f).reshape(128, D)
    d["x_all"] = np.ascontiguousarray(np.concatenate([np.asarray(inp["x_prompt"][c, :L], dtype=f), xs], axis=0))
    d["ck"] = np.ascontiguousarray(np.asarray(inp["cache_k"][:, 2 * c:2 * c + 2, :P], dtype=f).reshape(nl, 2, P, ATT))
    d["cv"] = np.ascontiguousarray(np.asarray(inp["cache_v"][:, 2 * c:2 * c + 2, :P], dtype=f).reshape(nl, 2, P, ATT))
    d["clf"] = np.ascontiguousarray(np.asarray(inp["cache_logf"][:, 2 * c:2 * c + 2, :P], dtype=f))
    d["sre"] = _state_layout(np.asarray(inp["state_ssm_re"][:, 2 * c:2 * c + 2], dtype=f))
    d["sim"] = _state_layout(np.asarray(inp["state_ssm_im"][:, 2 * c:2 * c + 2], dtype=f))
    sc = np.asarray(inp["state_conv"][:, 2 * c:2 * c + 2], dtype=f)
    d["sconv"] = np.ascontiguousarray(sc.reshape(nl, 2, 2, 2, 128).transpose(0, 1, 4, 3, 2))
    return d


def assemble(results, L):
    n = len(results)
    nl = results[0]["k_out"].shape[0]
    f = np.float32
    y = np.stack([r["y_out"][:L] for r in results]).astype(f)
    ys = np.concatenate([r["y_out"][L:].reshape(2, 64, D) for r in results]).astype(f)

    def tok(name, w):
        p = np.stack([r[name][:, :L] for r in results], axis=1)
        s_ = np.concatenate([r[name][:, L:].reshape(nl, 2, 64, w) for r in results], axis=1)
        return p, s_
    kp, ks = tok("k_out", ATT)
    vp, vs = tok("v_out", ATT)
    lp, ls = tok("lf_out", NH)
    kp = kp.reshape(nl, n, L, NH, DH); ks = ks.reshape(nl, 2 * n, 64, NH, DH)
    vp = vp.reshape(nl, n, L, NH, DH); vs = vs.reshape(nl, 2 * n, 64, NH, DH)

    def st(name):
        a = np.stack([r[name] for r in results], axis=1)
        p = _state_unlayout(a[:, :, 0])
        s_ = _state_unlayout(a[:, :, 1:3].reshape(nl, 2 * n, 128, 8))
        return p, s_
    rp, rs = st("sre_out")
    ip, is_ = st("sim_out")
    c = np.stack([r["conv_out"] for r in results], axis=1)
    c = c.transpose(0, 1, 2, 5, 4, 3).reshape(nl, n, 3, 2, 256)
    cp = c[:, :, 0]
    cs = c[:, :, 1:3].reshape(nl, 2 * n, 2, 256)
    return tuple(np.ascontiguousarray(a, dtype=f) for a in
                 (y, ys, kp, vp, lp, rp, ip, cp, ks, vs, ls, rs, is_, cs))


_PROG = {}


def kernel(**inputs):
    L, P = 4096, 4096
    inp = {k: np.asarray(v) for k, v in inputs.items()}
    if "nc" not in _PROG:
        _PROG["nc"] = build_program(L, P)
    nc = _PROG["nc"]
    shared = prep_shared(inp)
    in_maps = []
    for c in range(8):
        m = dict(shared)
        m.update(prep_core(inp, c, L, P))
        in_maps.append(m)
    res = run_bass_kernel_spmd(nc, in_maps, core_ids=list(range(8)))
    return assemble(res.results, L)
```

```python
import numpy as np
import concourse.bass as bass
import concourse.mybir as mybir
from concourse.bass_utils import run_bass_kernel_spmd

F32 = mybir.dt.float32
BF16 = mybir.dt.bfloat16
I32 = mybir.dt.int32
AF = mybir.ActivationFunctionType
ALU = mybir.AluOpType
AX = mybir.AxisListType

D = 1024
NH = 8
DH = 64
ATT = 512
SSM = 256
CONV = 256
PROJ = 2568
DFF = 4096
EPS = 1e-6
NEG = -1e30
C_FG = 3 * ATT
C_U = C_FG + NH
VW = DH + 2


class Dep:
    __slots__ = ("w", "r", "excl")

    def __init__(self, excl=False):
        self.w = None
        self.r = []
        self.excl = excl


class KB:
    def __init__(self, nc, ndma=20):
        self.nc = nc
        self.eng = {"pe": nc.tensor, "dve": nc.vector, "act": nc.scalar, "pool": nc.gpsimd, "sp": nc.sync}
        self.sems = {}
        self.cnt = {}
        for e in ["pe", "dve", "act", "pool"]:
            self.sems[e] = nc.semaphore("sem_" + e).__enter__()
            self.cnt[e] = 0
        self.dma_sems = {}
        self.dma_cnt = {}
        self.dma_rr = {}
        for q in ["sp", "pool"]:
            self.dma_sems[q] = [nc.semaphore(f"dsem_{q}{i}").__enter__() for i in range(ndma)]
            self.dma_cnt[q] = [0] * ndma
            self.dma_rr[q] = 0
        self.waited = {e: {} for e in self.eng}
        self.ninst = 0
        self.sb_bytes = 0
        self.stack = []

    def sb(self, name, shape, dt):
        self.sb_bytes += 1
        t = self.nc.sbuf_tensor(f"{name}_{self.sb_bytes}", list(shape), dt)
        h = t.__enter__()
        self.stack.append(t)
        return h

    def ps(self, name, shape, dt):
        t = self.nc.psum_tensor(name, list(shape), dt)
        h = t.__enter__()
        self.stack.append(t)
        return h

    def mark(self):
        return len(self.stack)

    def release(self, mark):
        while len(self.stack) > mark:
            t = self.stack.pop()
            t.__exit__(None, None, None)

    def _semobj(self, key):
        if isinstance(key, str):
            return self.sems[key]
        q, i = key
        return self.dma_sems[q][i]

    def _wait(self, e, key, val):
        w = self.waited[e]
        if w.get(key, 0) >= val:
            return
        w[key] = val
        self.eng[e].wait_ge(self._semobj(key), val)

    def _deps(self, e, reads, writes):
        need = {}

        def add(kv):
            if kv is None:
                return
            k, v = kv
            if k == "pe" and e == "pe":
                return
            if need.get(k, 0) < v:
                need[k] = v
        for d in reads:
            add(d.w)
        for d in writes:
            add(d.w)
            for r in d.r:
                add(r)
        for k, v in need.items():
            self._wait(e, k, v)

    def _record(self, tag, reads, writes):
        for d in reads:
            d.r.append(tag)
            if len(d.r) > 48:
                m = {}
                for k, v in d.r:
                    if m.get(k, 0) < v:
                        m[k] = v
                d.r = list(m.items())
        for d in writes:
            d.w = tag
            d.r = []

    def op(self, e, fn, reads=(), writes=()):
        ex = [d for d in reads if d.excl]
        if ex:
            reads = [d for d in reads if not d.excl]
            writes = list(writes) + [d for d in ex if d not in writes]
        self._deps(e, reads, writes)
        ins = fn()
        self.cnt[e] += 1
        ins.then_inc(self.sems[e], 1)
        self._record((e, self.cnt[e]), reads, writes)
        self.ninst += 1
        return ins

    def dma(self, q, out, in_, reads=(), writes=(), **kw):
        i = self.dma_rr[q]
        self.dma_rr[q] = (i + 1) % len(self.dma_sems[q])
        key = (q, i)
        if self.dma_cnt[q][i] > 0:
            self._wait(q, key, self.dma_cnt[q][i])
        self._deps(q, reads, writes)
        ins = self.eng[q].dma_start(out=out, in_=in_, **kw)
        self.dma_cnt[q][i] += 16
        ins.then_inc(self.dma_sems[q][i], 16)
        self._record((key, self.dma_cnt[q][i]), reads, writes)
        self.ninst += 1
        return ins

    def barrier(self):
        for e in ["pe", "dve", "act", "pool", "sp"]:
            for q in self.dma_sems:
                for i, c in enumerate(self.dma_cnt[q]):
                    if c > 0:
                        self._wait(e, (q, i), c)
            for e2 in ["pe", "dve", "act", "pool"]:
                if self.cnt[e2] > 0 and e2 != e:
                    self._wait(e, e2, self.cnt[e2])

    def finish(self):
        for q in self.dma_sems:
            for i, c in enumerate(self.dma_cnt[q]):
                if c > 0:
                    self._wait("sp", (q, i), c)
        for e in ["pe", "dve", "act", "pool"]:
            if self.cnt[e] > 0:
                self._wait("sp", e, self.cnt[e])


class Rot:
    def __init__(self, bufs):
        self.bufs = bufs
        self.deps = [Dep() for _ in bufs]
        self.i = 0

    def next(self):
        b, d = self.bufs[self.i], self.deps[self.i]
        self.i = (self.i + 1) % len(self.bufs)
        return b, d


class StopBuild(Exception):
    pass


def build_program(L, P, nlayers=2, debug=False, stage=99):
    nc = bass.Bass("TRN2", target_bir_lowering=False)
    kb = KB(nc)

    def stg(n):
        if stage == n:
            raise StopBuild()
    NTP = L // 128
    NT = NTP + 1
    NTOK = L + 128
    NQ = L // 512
    NKC = P // 128

    def din(name, shape, dt=F32):
        return nc.dram_tensor(name, list(shape), dt, kind="ExternalInput").ap()

    def dout(name, shape, dt=F32):
        return nc.dram_tensor(name, list(shape), dt, kind="ExternalOutput").ap()

    def dscr(name, shape, dt=F32):
        return nc.dram_tensor(name, list(shape), dt, kind="Internal").ap()

    x_all = din("x_all", [NTOK, D])
    ck = din("ck", [nlayers, 2, P, ATT])
    cv = din("cv", [nlayers, 2, P, ATT])
    clf = din("clf", [nlayers, 2, P, NH])
    sre = din("sre", [nlayers, 2, 128, 8])
    sim = din("sim", [nlayers, 2, 128, 8])
    sconv = din("sconv", [nlayers, 2, 128, 2, 2])
    w_in = din("w_in", [nlayers, D, PROJ])
    w_out = din("w_out", [nlayers, D, D])
    w_up = din("w_up", [nlayers, D, DFF])
    w_down = din("w_down", [nlayers, DFF, D])
    w_glu = din("w_glu", [nlayers, SSM, SSM])
    ln1b = din("ln1b", [nlayers, 128, D])
    ln2b = din("ln2b", [nlayers, 128, D])
    bnatt = din("bnatt", [nlayers, 128, ATT])
    qnb = din("qnb", [nlayers, 128, ATT])
    knb = din("knb", [nlayers, 128, ATT])
    bfb = din("bfb", [nlayers, 128, NH])
    convw = din("convw", [nlayers, 128, 2, 3])
    dvec = din("dvec", [nlayers, 128, 2])
    bglu = din("bglu", [nlayers, 128, 2])
    bnssm = din("bnssm", [nlayers, 128, 2])
    bnconv = din("bnconv", [nlayers, 128, 2])
    lamre = din("lamre", [nlayers, 128, 8])
    lamim = din("lamim", [nlayers, 128, 8])
    logdt = din("logdt", [nlayers, 128, 8])
    bre_pad = din("bre_pad", [nlayers, 128, 8, 128])
    bim_pad = din("bim_pad", [nlayers, 128, 8, 128])
    cre_pad = din("cre_pad", [nlayers, 128, 8, 128])
    cim_pad = din("cim_pad", [nlayers, 128, 8, 128])
    c_ident = din("c_ident", [128, 128])
    c_utri = din("c_utri", [128, 128])
    c_utri2 = din("c_utri2", [128, 128])
    c_ones = din("c_ones", [128, 128])
    c_lstrict = din("c_lstrict", [128, 128])
    c_maskT = din("c_maskT", [128, 128])
    c_mask2 = din("c_mask2", [2, 128, 64])

    y_out = dout("y_out", [NTOK, D])
    k_out = dout("k_out", [nlayers, NTOK, ATT])
    v_out = dout("v_out", [nlayers, NTOK, ATT])
    lf_out = dout("lf_out", [nlayers, NTOK, NH])
    sre_out = dout("sre_out", [nlayers, 3, 128, 8])
    sim_out = dout("sim_out", [nlayers, 3, 128, 8])
    conv_out = dout("conv_out", [nlayers, 3, 128, 2, 2])
    if debug:
        att_dbg = dout("att_dbg", [nlayers, NTOK, ATT])
        feat_dbg = dout("feat_dbg", [nlayers, D, NTOK])
        att_s = [att_dbg[l] for l in range(nlayers)]
        featT = [feat_dbg[l] for l in range(nlayers)]
    else:
        att_sc = dscr("att_sc", [nlayers, NTOK, ATT])
        feat_sc = dscr("feat_sc", [nlayers, D, NTOK])
        att_s = [att_sc[l] for l in range(nlayers)]
        featT = [feat_sc[l] for l in range(nlayers)]
    x1 = dscr("x1", [NTOK, D])
    wup_bf = dscr("wup_bf", [nlayers, 16, 128, 8, 256], BF16)
    wdn_bf = dscr("wdn_bf", [nlayers, 2, 8, 128, 4, 512], BF16)
    d_wup_bf = [Dep() for _ in range(nlayers)]
    d_wdn_bf = [Dep() for _ in range(nlayers)]
    d_att = [[Dep() for _ in range(NT)] for _ in range(nlayers)]
    d_feat = [[Dep() for _ in range(NT)] for _ in range(nlayers)]
    d_x1 = [Dep() for _ in range(NT)]

    pbank = [kb.ps(f"pb{i}", [128, 512], F32) for i in range(8)]
    pdep = [Dep(excl=True) for _ in range(8)]

    ident_f = kb.sb("ident_f", [128, 128], F32)
    ident_b = kb.sb("ident_b", [128, 128], BF16)
    utri = kb.sb("utri", [128, 128], F32)
    utri2 = kb.sb("utri2", [128, 128], F32)
    ones_f = kb.sb("ones_f", [128, 128], F32)
    lstrict = kb.sb("lstrict", [128, 128], F32)
    maskT = kb.sb("maskT", [128, 128], F32)
    mask2 = kb.sb("mask2", [128, 2, 64], F32)
    d_const = Dep()
    kb.dma("sp", ident_f[:], c_ident[:, :], writes=[d_const])
    kb.dma("pool", ident_b[:], c_ident[:, :], writes=[d_const])
    kb.dma("sp", utri[:], c_utri[:, :], writes=[d_const])
    kb.dma("sp", utri2[:], c_utri2[:, :], writes=[d_const])
    kb.dma("sp", ones_f[:], c_ones[:, :], writes=[d_const])
    kb.dma("sp", lstrict[:], c_lstrict[:, :], writes=[d_const])
    kb.dma("sp", maskT[:], c_maskT[:, :], writes=[d_const])
    kb.dma("sp", mask2[:], c_mask2.rearrange("s p q -> p s q"), writes=[d_const])

    CAST_KW = dict(max_dma_last_dim=2048)
    stg(1)

    def phase_a(l, x_src, d_xsrc):
        m0 = kb.mark()
        w_in_sb = kb.sb("w_in_sb", [128, 8, PROJ], BF16)
        WIN_CH = [(0, 512), (512, 1024), (1024, 1536), (1536, 2048), (2048, 2560), (2560, 2568)]
        d_win_ch = [Dep() for _ in WIN_CH]
        for (c0_, c1_), dd in zip(WIN_CH, d_win_ch):
            kb.dma("pool", w_in_sb[:, :, c0_:c1_], w_in[l].rearrange("(kt p) n -> p kt n", p=128)[:, :, c0_:c1_], writes=[dd], **CAST_KW)

        def d_win_for(c0_, n_):
            return [dd for (a, b), dd in zip(WIN_CH, d_win_ch) if a < c0_ + n_ and c0_ < b]
        for fc in range(16):
            kb.dma("pool", wup_bf[l, fc], w_up[l].rearrange("(kt p) n -> p kt n", p=128)[:, :, fc * 256:(fc + 1) * 256],
                   writes=[d_wup_bf[l]], **CAST_KW)
        for c in range(2):
            for ig in range(8):
                kb.dma("pool", wdn_bf[l, c, ig],
                       w_down[l].rearrange("(i p) n -> p i n", p=128)[:, ig * 4:(ig + 1) * 4, c * 512:(c + 1) * 512],
                       writes=[d_wdn_bf[l]], **CAST_KW)
        ln1 = kb.sb("ln1", [128, D], F32)
        qn = kb.sb("qn", [128, ATT], F32)
        kn = kb.sb("kn", [128, ATT], F32)
        bf = kb.sb("bf", [128, NH], F32)
        d_vec = Dep()
        kb.dma("sp", ln1[:], ln1b[l], writes=[d_vec])
        kb.dma("sp", qn[:], qnb[l], writes=[d_vec])
        kb.dma("sp", kn[:], knb[l], writes=[d_vec])
        kb.dma("sp", bf[:], bfb[l], writes=[d_vec])
        KT = kb.sb("KT", [128, 4, L], BF16)
        d_KT = [Dep() for _ in range(NTP)]
        VA = kb.sb("VA", [128, NTP, NH, VW], BF16)
        d_VA = [Dep() for _ in range(NTP)]
        d_VA1 = Dep()
        kb.op("dve", lambda: nc.vector.memset(VA[:, :, :, DH:DH + 1], 1.0), writes=[d_VA1])
        cst = kb.sb("cst", [128, NTP, NH], F32)
        d_cst = [Dep() for _ in range(NTP)]
        carry = kb.sb("carry", [128, NTP + 1, NH], F32)
        d_carry = [Dep() for _ in range(NTP + 1)]
        kb.op("dve", lambda: nc.vector.memset(carry[:, 0, :], 0.0), writes=[d_carry[0]])

        xt_r = Rot([kb.sb(f"xt{i}", [128, D], F32) for i in range(2)])
        sqs = kb.sb("sqs", [128, D], F32)
        d_sqs = Dep()
        hb_r = Rot([kb.sb(f"hb{i}", [128, D], BF16) for i in range(2)])
        hT = kb.sb("hT", [128, 8, 512], BF16)
        d_hT = [Dep() for _ in range(4)]
        QT2 = [kb.sb(f"QT{i}", [128, 4, 2, 512], BF16) for i in range(2)]
        d_QT2 = [[Dep() for _ in range(4)] for _ in range(2)]
        d_QTz = Dep()
        for QTb in QT2:
            kb.op("dve", lambda QTb=QTb: nc.vector.memset(QTb[:], 0.0), writes=[d_QTz])
        stat_r = Rot([kb.sb(f"stat{i}", [128, 40], F32) for i in range(10)])
        qk_r = Rot([kb.sb(f"qkb{i}", [128, ATT], BF16) for i in range(4)])
        kf_r = Rot([kb.sb(f"kf{i}", [128, ATT], F32) for i in range(2)])
        vf_r = Rot([kb.sb(f"vf{i}", [128, ATT], F32) for i in range(2)])
        lf_r = Rot([kb.sb(f"lf{i}", [128, NH], F32) for i in range(2)])
        fe_r = Rot([kb.sb(f"fe{i}", [128, 512], F32) for i in range(2)])
        rc_r = Rot([kb.sb(f"rc{i}", [128, 1], F32) for i in range(4)])
        ptw_r = Rot([kb.sb(f"ptw{i}", [128, 512], BF16) for i in range(4)])
        biasF = kb.sb("biasF", [128, NTP, NH], F32)
        d_biasF = Dep()
        biasN = kb.sb("biasN", [128, 4, 4, NH], F32)
        d_biasN = Dep()
        fac = kb.sb("fac", [128, 4, NH], F32)
        d_fac = Dep()
        osum = kb.sb("osum", [128, 4, DH + 1], F32)
        d_osum = Dep()
        rc4 = kb.sb("rc4", [128, 4], F32)
        d_rc4 = Dep()
        att4 = kb.sb("att4", [128, 4, ATT], F32)
        d_att4 = Dep()
        FAR_BANKS = [3, 5]
        NEAR_BANKS = [4, 6]
        PB_PROJ = [0, 1]
        PB_S = [2, 3, 4]
        PB_O = [5, 6]
        PB_T = 7
        rr = {"proj": 0, "s": 0, "o": 0, "sw": 0}

        def nextbank(kind, lst):
            i = rr[kind]
            rr[kind] = (i + 1) % len(lst)
            return lst[i]

        def rms_h(xt, d_x, nrm_w, hb, d_hb):
            st, d_st = stat_r.next()
            kb.op("act", lambda: nc.scalar.activation(out=sqs[:], in_=xt[:], func=AF.Square, accum_out=st[:, 0:1]),
                  reads=[d_x], writes=[d_sqs, d_st])
            kb.op("act", lambda: nc.scalar.activation(out=st[:, 1:2], in_=st[:, 0:1], func=AF.Ln, scale=1.0 / D, bias=EPS),
                  reads=[d_st], writes=[d_st])
            kb.op("act", lambda: nc.scalar.activation(out=st[:, 2:3], in_=st[:, 1:2], func=AF.Exp, scale=-0.5),
                  reads=[d_st], writes=[d_st])
            kb.op("dve", lambda: nc.vector.scalar_tensor_tensor(out=hb[:], in0=xt[:], scalar=st[:, 2:3], in1=nrm_w[:],
                                                                op0=ALU.mult, op1=ALU.mult),
                  reads=[d_x, d_st, d_vec], writes=[d_hb])

        def transpose_to(hb, d_hb, dst_fn, d_dst, ncols=8):
            b = PB_T
            pb16 = pbank[b][:].bitcast(BF16)
            for c in range(ncols):
                kb.op("pe", lambda c=c: nc.tensor.transpose(out=pb16[:, c * 128:(c + 1) * 128],
                                                           in_=hb[:, c * 128:(c + 1) * 128], identity=ident_b[:]),
                      reads=[d_hb, d_const], writes=[pdep[b]])
            return pb16, b

        def token_tile(tg, j, qpar, sample=False):
            QT, d_QT = QT2[qpar], d_QT2[qpar]
            xt, d_x = xt_r.next()
            kb.dma("sp", xt[:], x_src[tg * 128:(tg + 1) * 128, :], reads=[d_xsrc[tg]], writes=[d_x])
            hb, d_hb = hb_r.next()
            rms_h(xt, d_x, ln1, hb, d_hb)
            pb16, b = transpose_to(hb, d_hb, None, None)
            kb.op("dve", lambda: nc.vector.tensor_copy(out=hT[:, :, j * 128:(j + 1) * 128],
                                                       in_=pb16[:, 0:1024].rearrange("p (c t) -> p c t", c=8)),
                  reads=[pdep[b]], writes=[d_hT[j]])
            yield
            stg(2)
            st, d_st = stat_r.next()

            def proj(c0, n):
                pbi = nextbank("proj", PB_PROJ)
                for kt in range(8):
                    kb.op("pe", lambda kt=kt: nc.tensor.matmul(pbank[pbi][:, 0:n], lhsT=hT[:, kt, j * 128:(j + 1) * 128],
                                                              rhs=w_in_sb[:, kt, c0:c0 + n], start=(kt == 0), stop=(kt == 7)),
                          reads=[d_hT[j]] + d_win_for(c0, n), writes=[pdep[pbi]])
                return pbi

            def qk_norm(pbi, w_t, off, dst_b, d_dstb, dst_f=None, d_dstf=None):
                kb.op("act", lambda: nc.scalar.activation(out=sqs[:, 0:ATT], in_=pbank[pbi][:, 0:ATT], func=AF.Square),
                      reads=[pdep[pbi]], writes=[d_sqs])
                kb.op("dve", lambda: nc.vector.tensor_reduce(out=st[:, off:off + 8],
                                                             in_=sqs[:, 0:ATT].rearrange("p (h d) -> p h d", h=NH),
                                                             axis=AX.X, op=ALU.add),
                      reads=[d_sqs], writes=[d_st])
                kb.op("act", lambda: nc.scalar.activation(out=st[:, off:off + 8], in_=st[:, off:off + 8], func=AF.Ln,
                                                          scale=1.0 / DH, bias=EPS), reads=[d_st], writes=[d_st])
                kb.op("act", lambda: nc.scalar.activation(out=st[:, off + 16:off + 24], in_=st[:, off:off + 8], func=AF.Exp,
                                                          scale=-0.5), reads=[d_st], writes=[d_st])
                kb.op("dve", lambda: nc.vector.tensor_tensor(
                    out=sqs[:, 0:ATT].rearrange("p (h d) -> p h d", h=NH),
                    in0=pbank[pbi][:, 0:ATT].rearrange("p (h d) -> p h d", h=NH),
                    in1=st[:, off + 16:off + 24].unsqueeze(2).broadcast_to([128, NH, DH]), op=ALU.mult),
                    reads=[pdep[pbi], d_st], writes=[d_sqs])
                if dst_f is not None:
                    kb.op("dve", lambda: nc.vector.tensor_tensor(out=dst_f[:], in0=sqs[:, 0:ATT], in1=w_t[:], op=ALU.mult),
                          reads=[d_sqs, d_vec], writes=[d_dstf])
                    kb.op("act", lambda: nc.scalar.copy(out=dst_b[:], in_=dst_f[:]), reads=[d_dstf], writes=[d_dstb])
                else:
                    kb.op("dve", lambda: nc.vector.tensor_tensor(out=dst_b[:], in0=sqs[:, 0:ATT], in1=w_t[:], op=ALU.mult),
                          reads=[d_sqs, d_vec], writes=[d_dstb])

            pq = proj(0, ATT)
            qb, d_qb = qk_r.next()
            qk_norm(pq, qn, 0, qb, d_qb)
            pb16, b = transpose_to(qb, d_qb, None, None, ncols=4)
            kb.op("dve", lambda: nc.vector.tensor_copy(out=QT[0:64, :, 0, j * 128:(j + 1) * 128],
                                                       in_=pb16[0:64, 0:512].rearrange("p (c t) -> p c t", c=4)),
                  reads=[pdep[b], d_QTz], writes=[d_QT[j]])
            kb.op("dve", lambda: nc.vector.tensor_copy(out=QT[64:128, :, 1, j * 128:(j + 1) * 128],
                                                       in_=pb16[64:128, 0:512].rearrange("p (c t) -> p c t", c=4)),
                  reads=[pdep[b], d_QTz], writes=[d_QT[j]])
            yield
            stg(21)
            pk = proj(ATT, ATT)
            kbb, d_kbb = qk_r.next()
            kf, d_kf = kf_r.next()
            qk_norm(pk, kn, 8, kbb, d_kbb, kf, d_kf)
            kb.dma("sp", k_out[l, tg * 128:(tg + 1) * 128, :], kf[:], reads=[d_kf])
            pb16, b = transpose_to(kbb, d_kbb, None, None, ncols=4)
            if not sample:
                kb.op("dve", lambda: nc.vector.tensor_copy(out=KT[:, :, tg * 128:(tg + 1) * 128],
                                                           in_=pb16[:, 0:512].rearrange("p (c t) -> p c t", c=4)),
                      reads=[pdep[b]], writes=[d_KT[tg]])
            else:
                kb.op("dve", lambda: nc.vector.tensor_copy(out=KTs[:, :, :],
                                                           in_=pb16[:, 0:512].rearrange("p (c t) -> p c t", c=4)),
                      reads=[pdep[b]], writes=[d_KTs])
            yield
            stg(22)
            pv = proj(2 * ATT, ATT)
            stg(221)
            vf, d_vf = vf_r.next()
            kb.op("dve", lambda: nc.vector.tensor_copy(out=vf[:], in_=pbank[pv][:, 0:ATT]), reads=[pdep[pv]], writes=[d_vf])
            stg(222)
            kb.dma("sp", v_out[l, tg * 128:(tg + 1) * 128, :], vf[:], reads=[d_vf])
            stg(223)
            if not sample:
                kb.op("dve", lambda: nc.vector.tensor_copy(out=VA[:, tg, :, 0:DH],
                                                           in_=pbank[pv][:, 0:ATT].rearrange("p (h d) -> p h d", h=NH)),
                      reads=[pdep[pv], d_VA1], writes=[d_VA[tg]])
            else:
                kb.op("dve", lambda: nc.vector.tensor_copy(out=VAs[:, :, 0:DH],
                                                           in_=pbank[pv][:, 0:ATT].rearrange("p (h d) -> p h d", h=NH)),
                      reads=[pdep[pv]], writes=[d_VAs])
            stg(23)
            yield
            pf = proj(C_FG, NH)
            lf, d_lf = lf_r.next()
            kb.op("dve", lambda: nc.vector.tensor_tensor(out=st[:, 32:40], in0=pbank[pf][:, 0:NH], in1=bf[:], op=ALU.add),
                  reads=[pdep[pf], d_vec], writes=[d_st])
            kb.op("act", lambda: nc.scalar.activation(out=st[:, 32:40], in_=st[:, 32:40], func=AF.Exp, scale=-1.0),
                  reads=[d_st], writes=[d_st])
            kb.op("act", lambda: nc.scalar.activation(out=st[:, 32:40], in_=st[:, 32:40], func=AF.Ln, bias=1.0),
                  reads=[d_st], writes=[d_st])
            kb.op("dve", lambda: nc.vector.tensor_scalar(out=lf[:], in0=st[:, 32:40], scalar1=-1.0, scalar2=None, op0=ALU.mult),
                  reads=[d_st], writes=[d_lf])
            kb.dma("sp", lf_out[l, tg * 128:(tg + 1) * 128, :], lf[:], reads=[d_lf])
            stg(24)
            pc = nextbank("proj", PB_PROJ)
            tri = utri2 if sample else utri
            kb.op("pe", lambda: nc.tensor.matmul(pbank[pc][:, 0:NH], lhsT=tri[:], rhs=lf[:], start=True, stop=True),
                  reads=[d_lf, d_const], writes=[pdep[pc]])
            stg(25)
            if not sample:
                kb.op("pe", lambda: nc.tensor.matmul(pbank[pc][:, 8:8 + NH], lhsT=ones_f[:], rhs=lf[:], start=True, stop=True),
                      reads=[d_lf, d_const], writes=[pdep[pc]])
                stg(26)
                kb.op("dve", lambda: nc.vector.tensor_tensor(out=cst[:, tg, :], in0=pbank[pc][:, 0:NH], in1=carry[:, tg, :],
                                                             op=ALU.add),
                      reads=[pdep[pc], d_carry[tg]], writes=[d_cst[tg]])
                stg(27)
                kb.op("dve", lambda: nc.vector.tensor_tensor(out=carry[:, tg + 1, :], in0=pbank[pc][:, 8:8 + NH],
                                                             in1=carry[:, tg, :], op=ALU.add),
                      reads=[pdep[pc], d_carry[tg]], writes=[d_carry[tg + 1]])
            else:
                kb.op("dve", lambda: nc.vector.tensor_scalar(out=negc_s[:], in0=pbank[pc][:, 0:NH], scalar1=-1.0, scalar2=None,
                                                             op0=ALU.mult),
                      reads=[pdep[pc]], writes=[d_negc_s])

        def feature_proj(tg0, ntok, js):
            for ft in range(8):
                pbi = nextbank("proj", PB_PROJ)
                c0 = C_U + ft * 128
                for kt in range(8):
                    kb.op("pe", lambda kt=kt: nc.tensor.matmul(pbank[pbi][:, 0:ntok], lhsT=w_in_sb[:, kt, c0:c0 + 128],
                                                              rhs=hT[:, kt, 0:ntok], start=(kt == 0), stop=(kt == 7)),
                          reads=[d_hT[jj] for jj in js] + d_win_for(c0, 128), writes=[pdep[pbi]])
                fe, d_fe = fe_r.next()
                kb.op("dve", lambda: nc.vector.tensor_copy(out=fe[:, 0:ntok], in_=pbank[pbi][:, 0:ntok]), reads=[pdep[pbi]], writes=[d_fe])
                kb.dma("sp", featT[l][ft * 128:(ft + 1) * 128, tg0 * 128:tg0 * 128 + ntok], fe[:, 0:ntok], reads=[d_fe],
                       writes=[d_feat[l][tg0 + jj] for jj in js])
                yield

        def finish_att(ob, h, att, d_att_t, ncolq=128):
            rc, d_rc = rc_r.next()
            kb.op("dve", lambda: nc.vector.reciprocal(out=rc[0:ncolq, :], in_=pbank[ob][0:ncolq, DH:DH + 1]),
                  reads=[pdep[ob]], writes=[d_rc])
            kb.op("dve", lambda: nc.vector.tensor_scalar(out=att[0:ncolq, h * DH:(h + 1) * DH], in0=pbank[ob][0:ncolq, 0:DH],
                                                         scalar1=rc[0:ncolq, 0:1], scalar2=None, op0=ALU.mult),
                  reads=[pdep[ob], d_rc], writes=[d_att_t])

        def attention_prompt(Qm):
            QT, d_QT = QT2[Qm % 2], d_QT2[Qm % 2]
            LA = 2
            S_BANKS = [0, 1, 2, 7]
            t0 = 4 * Qm
            nfar = t0
            if nfar > 0:
                kb.op("dve", lambda: nc.vector.tensor_tensor(
                    out=biasF[:, 0:nfar, :], in0=carry[:, t0:t0 + 1, :].broadcast_to([128, nfar, NH]), in1=cst[:, 0:nfar, :],
                    op=ALU.subtract), reads=[d_carry[t0]] + d_cst[0:nfar], writes=[d_biasF])
                kb.op("dve", lambda: nc.vector.tensor_tensor(
                    out=fac[:], in0=carry[:, t0:t0 + 4, :], in1=carry[:, t0:t0 + 1, :].broadcast_to([128, 4, NH]), op=ALU.subtract),
                    reads=d_carry[t0:t0 + 4], writes=[d_fac])
                kb.op("act", lambda: nc.scalar.activation(out=fac[:], in_=fac[:], func=AF.Exp), reads=[d_fac], writes=[d_fac])
            for j in range(4):
                kb.op("dve", lambda j=j: nc.vector.tensor_tensor(
                    out=biasN[:, 0:j + 1, j, :], in0=carry[:, t0 + j:t0 + j + 1, :].broadcast_to([128, j + 1, NH]),
                    in1=cst[:, t0:t0 + j + 1, :], op=ALU.subtract),
                    reads=[d_carry[t0 + j]] + d_cst[t0:t0 + j + 1], writes=[d_biasN])
            blocks = []
            for h in range(NH):
                for kt in range(nfar):
                    blocks.append(dict(h=h, kt=kt, far=True, q0=0, i=None, first=(kt == 0), last=False))
                for i in range(4):
                    blocks.append(dict(h=h, kt=t0 + i, far=False, q0=i * 128, i=i, first=(i == 0), last=(i == 3)))
            pend = []
            cur = {}

            def emit_pvs(bk):
                h = bk["h"]
                kt = bk["kt"]
                pt, d_pt = bk["pt"]
                if bk["far"]:
                    if bk["first"]:
                        cur["far"] = FAR_BANKS[h % 2]
                    ob = cur["far"]
                    for j in range(4):
                        kb.op("pe", lambda j=j: nc.tensor.matmul(
                            pbank[ob][:, j * (DH + 1):(j + 1) * (DH + 1)], lhsT=pt[:, j * 128:(j + 1) * 128],
                            rhs=VA[:, kt, h, 0:DH + 1], start=(bk["first"] and j == 0), stop=(kt == nfar - 1), skip_group_check=True),
                            reads=[d_pt, d_VA[kt], d_VA1], writes=[pdep[ob]])
                else:
                    i = bk["i"]
                    if bk["first"]:
                        cur["near"] = NEAR_BANKS[h % 2]
                    ob = cur["near"]
                    for j in range(i, 4):
                        c0 = (j - i) * 128
                        kb.op("pe", lambda j=j, c0=c0: nc.tensor.matmul(
                            pbank[ob][:, j * (DH + 1):(j + 1) * (DH + 1)], lhsT=pt[:, c0:c0 + 128],
                            rhs=VA[:, kt, h, 0:DH + 1], start=(i == 0 and j == 0), stop=(i == j), skip_group_check=True),
                            reads=[d_pt, d_VA[kt], d_VA1], writes=[pdep[ob]])
                    if bk["last"]:
                        combine(h)

            def combine(h):
                nb = cur["near"]
                nview = pbank[nb][:, 0:4 * (DH + 1)].rearrange("p (j c) -> p j c", j=4)
                if nfar > 0:
                    fb = cur["far"]
                    fview = pbank[fb][:, 0:4 * (DH + 1)].rearrange("p (j c) -> p j c", j=4)
                    kb.op("dve", lambda: nc.vector.tensor_tensor(out=osum[:], in0=fview,
                                                                 in1=fac[:, :, h:h + 1].broadcast_to([128, 4, DH + 1]), op=ALU.mult),
                          reads=[pdep[fb], d_fac], writes=[d_osum])
                    kb.op("dve", lambda: nc.vector.tensor_tensor(out=osum[:], in0=nview, in1=osum[:], op=ALU.add),
                          reads=[pdep[nb], d_osum], writes=[d_osum])
                    src, d_src = osum[:], d_osum
                else:
                    kb.op("dve", lambda: nc.vector.tensor_copy(out=osum[:], in_=nview), reads=[pdep[nb]], writes=[d_osum])
                    src, d_src = osum[:], d_osum
                kb.op("dve", lambda: nc.vector.reciprocal(out=rc4[:], in_=osum[:, :, DH]), reads=[d_osum], writes=[d_rc4])
                kb.op("dve", lambda: nc.vector.tensor_tensor(out=att4[:, :, h * DH:(h + 1) * DH], in0=osum[:, :, 0:DH],
                                                             in1=rc4[:].unsqueeze(2).broadcast_to([128, 4, DH]), op=ALU.mult),
                      reads=[d_osum, d_rc4], writes=[d_att4])
                if h == NH - 1:
                    kb.dma("sp", att_s[l][t0 * 128:(t0 + 4) * 128, :].rearrange("(j p) c -> p j c", p=128), att4[:],
                           reads=[d_att4], writes=[d_att[l][t0 + j] for j in range(4)])

            for bk in blocks:
                h, kt = bk["h"], bk["kt"]
                pr, po = h // 2, (h % 2) * 64
                sbk = nextbank("sw", S_BANKS)
                q0 = bk["q0"]
                wq = 512 - q0
                kb.op("pe", lambda: nc.tensor.matmul(
                    pbank[sbk][:, 0:wq], lhsT=KT[:, pr, kt * 128:(kt + 1) * 128],
                    rhs=QT[:, pr, h % 2, q0:512], start=True, stop=True),
                    reads=[d_KT[kt]] + d_QT, writes=[pdep[sbk]])
                pt, d_pt = ptw_r.next()
                bk["pt"] = (pt, d_pt)
                if bk["far"]:
                    kb.op("act", lambda: nc.scalar.activation(out=pt[:, 0:512], in_=pbank[sbk][:, 0:512], func=AF.Exp,
                                                              scale=DH ** -0.5, bias=biasF[:, kt, h:h + 1]),
                          reads=[pdep[sbk], d_biasF], writes=[d_pt])
                else:
                    i = bk["i"]
                    kb.op("dve", lambda: nc.vector.tensor_tensor(out=pbank[sbk][:, 0:128], in0=pbank[sbk][:, 0:128],
                                                                 in1=maskT[:], op=ALU.add),
                          reads=[pdep[sbk], d_const], writes=[pdep[sbk]])
                    for j in range(i, 4):
                        c0 = (j - i) * 128
                        kb.op("act", lambda j=j, c0=c0: nc.scalar.activation(
                            out=pt[:, c0:c0 + 128], in_=pbank[sbk][:, c0:c0 + 128], func=AF.Exp, scale=DH ** -0.5,
                            bias=biasN[:, i, j, h:h + 1]),
                            reads=[pdep[sbk], d_biasN], writes=[d_pt])
                pend.append(bk)
                if len(pend) > LA:
                    emit_pvs(pend.pop(0))
                yield
            while pend:
                emit_pvs(pend.pop(0))

        def proj_macro(Qm):
            gens = []
            started = 0
            while started < 4 or gens:
                if started < 4:
                    gens.append(token_tile(4 * Qm + started, started, Qm % 2))
                    started += 1
                for g in list(gens):
                    try:
                        next(g)
                    except StopIteration:
                        gens.remove(g)
                yield
            yield from feature_proj(4 * Qm, 512, [0, 1, 2, 3])

        def proj_sample():
            yield from token_tile(NTP, 0, NQ % 2, sample=True)
            yield from feature_proj(NTP, 128, [0])

        def cosched2(ga, gb_, K=1):
            while ga is not None or gb_ is not None:
                for _ in range(K):
                    if ga is not None:
                        try:
                            next(ga)
                        except StopIteration:
                            ga = None
                if gb_ is not None:
                    try:
                        next(gb_)
                    except StopIteration:
                        gb_ = None

        KTs = kb.sb("KTs", [128, 4, 128], BF16)
        d_KTs = Dep()
        VAs = kb.sb("VAs", [128, NH, VW], BF16)
        d_VAs = Dep()
        kb.op("dve", lambda: nc.vector.memset(VAs[:, :, DH:DH + 1], 1.0), writes=[d_VAs])
        negc_s = kb.sb("negc_s", [128, NH], F32)
        d_negc_s = Dep()
        cosched2(proj_macro(0), None)
        for Qm in range(NQ):
            nblk = NH * (4 * Qm + 4)
            cosched2(attention_prompt(Qm), proj_macro(Qm + 1) if Qm + 1 < NQ else proj_sample(),
                     K=max(1, int(round(nblk / (16.0 if Qm + 1 < NQ else 14.0)))))
        QT, d_QT = QT2[NQ % 2], d_QT2[NQ % 2]
        stg(6)
        clf_sb = kb.sb("clf_sb", [128, NKC, NH], F32)
        d_clf = Dep()
        suf = kb.sb("suf", [128, NKC, NH], F32)
        d_suf = Dep()
        carr_s = kb.sb("carr_s", [128, NH], F32)
        d_carr_s = Dep()
        ckb_r = Rot([kb.sb(f"ckb{i}", [128, ATT], BF16) for i in range(2)])
        ckT_r = Rot([kb.sb(f"ckT{i}", [128, 4, 128], BF16) for i in range(3)])
        cva_r = Rot([kb.sb(f"cva{i}", [128, NH, VW], BF16) for i in range(3)])
        for cva_b, cva_d in zip(cva_r.bufs, cva_r.deps):
            kb.op("dve", lambda cva_b=cva_b: nc.vector.memset(cva_b[:, :, DH:DH + 1], 1.0), writes=[cva_d])
        pts_r = Rot([kb.sb(f"pts{i}", [128, NH, 64], BF16) for i in range(3)])
        att_smp = att4[:, 0, :]
        d_att_smp = d_att4
        for s in range(2):
            kb.dma("sp", clf_sb[:], clf[l, s].rearrange("(kt p) h -> p kt h", p=128), reads=[d_suf], writes=[d_clf])
            pc = nextbank("proj", PB_PROJ)
            n8 = NKC * NH
            kb.op("pe", lambda: nc.tensor.matmul(pbank[pc][:, 0:n8], lhsT=lstrict[:], rhs=clf_sb[:].rearrange("p k h -> p (k h)"),
                                                 start=True, stop=True), reads=[d_clf, d_const], writes=[pdep[pc]])
            pc2 = nextbank("proj", PB_PROJ)
            kb.op("pe", lambda: nc.tensor.matmul(pbank[pc2][:, 0:n8], lhsT=ones_f[:], rhs=clf_sb[:].rearrange("p k h -> p (k h)"),
                                                 start=True, stop=True), reads=[d_clf, d_const], writes=[pdep[pc2]])
            kb.op("dve", lambda: nc.vector.memset(carr_s[:], 0.0), writes=[d_carr_s])
            for kt in range(NKC - 1, -1, -1):
                kb.op("dve", lambda kt=kt: nc.vector.tensor_tensor(out=suf[:, kt, :], in0=pbank[pc][:, kt * NH:(kt + 1) * NH],
                                                                   in1=carr_s[:], op=ALU.add),
                      reads=[pdep[pc], d_carr_s], writes=[d_suf])
                if kt > 0:
                    kb.op("dve", lambda kt=kt: nc.vector.tensor_tensor(out=carr_s[:], in0=pbank[pc2][:, kt * NH:(kt + 1) * NH],
                                                                       in1=carr_s[:], op=ALU.add),
                          reads=[pdep[pc2], d_carr_s], writes=[d_carr_s])
            stg(61)
            obs = PB_O
            pend_pv = []

            def emit_pv_s(item):
                kt_, last_, pts_, d_pts_, va_, d_va_ = item
                for h in range(NH):
                    ob = obs[h // 4]
                    c0 = (h % 4) * (DH + 1)
                    first = (kt_ == 0 and h % 4 == 0)
                    kb.op("pe", lambda h=h, ob=ob, c0=c0, first=first: nc.tensor.matmul(
                        pbank[ob][0:64, c0:c0 + DH + 1], lhsT=pts_[:, h, :], rhs=va_[:, h, 0:DH + 1], start=first, stop=last_,
                        skip_group_check=True),
                        reads=[d_pts_, d_va_], writes=[pdep[ob]])

            for kt in range(NKC + 1):
                last = (kt == NKC)
                if not last:
                    ckb, d_ckb = ckb_r.next()
                    kb.dma("pool", ckb[:], ck[l, s, kt * 128:(kt + 1) * 128, :], writes=[d_ckb], **CAST_KW)
                    cva, d_cva = cva_r.next()
                    kb.dma("pool", cva[:, :, 0:DH], cv[l, s, kt * 128:(kt + 1) * 128, :].rearrange("p (h d) -> p h d", h=NH),
                           writes=[d_cva], **CAST_KW)
                    pb16, b = transpose_to(ckb, d_ckb, None, None, ncols=4)
                    ckT, d_ckT = ckT_r.next()
                    kb.op("dve", lambda: nc.vector.tensor_copy(out=ckT[:], in_=pb16[:, 0:512].rearrange("p (c t) -> p c t", c=4)),
                          reads=[pdep[b]], writes=[d_ckT])
                    kT_t, d_kT_t, va_t, d_va_t = ckT, d_ckT, cva, d_cva
                else:
                    kT_t, d_kT_t, va_t, d_va_t = KTs, d_KTs, VAs, d_VAs
                sbk = nextbank("s", PB_S)
                for h in range(NH):
                    pr = h // 2
                    kb.op("pe", lambda h=h, pr=pr: nc.tensor.matmul(
                        pbank[sbk][:, h * 64:(h + 1) * 64], lhsT=kT_t[:, pr, :],
                        rhs=QT[:, pr, h % 2, s * 64:(s + 1) * 64], start=True, stop=True),
                        reads=[d_kT_t, d_QT[0]], writes=[pdep[sbk]])
                if last:
                    kb.op("dve", lambda: nc.vector.tensor_tensor(
                        out=pbank[sbk][:, :].rearrange("p (h q) -> p h q", h=NH),
                        in0=pbank[sbk][:, :].rearrange("p (h q) -> p h q", h=NH),
                        in1=mask2[:, s:s + 1, :].broadcast_to([128, NH, 64]), op=ALU.add),
                        reads=[pdep[sbk], d_const], writes=[pdep[sbk]])
                pts, d_pts = pts_r.next()
                for h in range(NH):
                    bias_ap = negc_s[:, h:h + 1] if last else suf[:, kt, h:h + 1]
                    kb.op("act", lambda h=h, bias_ap=bias_ap: nc.scalar.activation(
                        out=pts[:, h, :], in_=pbank[sbk][:, h * 64:(h + 1) * 64], func=AF.Exp, scale=DH ** -0.5, bias=bias_ap),
                        reads=[pdep[sbk], d_negc_s if last else d_suf], writes=[d_pts])
                pend_pv.append((kt, last, pts, d_pts, va_t, d_va_t))
                if len(pend_pv) > 1:
                    emit_pv_s(pend_pv.pop(0))
            while pend_pv:
                emit_pv_s(pend_pv.pop(0))
            stg(64)
            for h in range(NH):
                ob = obs[h // 4]
                c0 = (h % 4) * (DH + 1)
                rc, d_rc = rc_r.next()
                kb.op("dve", lambda: nc.vector.reciprocal(out=rc[0:64, :], in_=pbank[ob][0:64, c0 + DH:c0 + DH + 1]),
                      reads=[pdep[ob]], writes=[d_rc])
                kb.op("dve", lambda h=h: nc.vector.tensor_scalar(out=att_smp[s * 64:(s + 1) * 64, h * DH:(h + 1) * DH],
                                                                 in0=pbank[ob][0:64, c0:c0 + DH], scalar1=rc[0:64, 0:1],
                                                                 scalar2=None, op0=ALU.mult),
                      reads=[pdep[ob], d_rc], writes=[d_att_smp])
        kb.dma("sp", att_s[l][NTP * 128:(NTP + 1) * 128, :], att_smp, reads=[d_att_smp], writes=[d_att[l][NTP]])
        kb.barrier()
        kb.release(m0)


    def phase_b(l, x_src, d_xsrc, x_dst, d_xdst):
        m0 = kb.mark()
        T = 128
        TWO_PI = float(2 * np.pi)
        w_out_sb = kb.sb("w_out_sb", [128, 8, D], BF16)
        w_glu_sb = kb.sb("w_glu_sb", [128, 2, SSM], BF16)
        d_wres = Dep()
        for kt in range(8):
            kb.dma("pool", w_out_sb[:, kt, :], w_out[l, kt * 128:(kt + 1) * 128, :], writes=[d_wres], **CAST_KW)
        kb.dma("pool", w_glu_sb[:], w_glu[l].rearrange("(kt p) n -> p kt n", p=128), writes=[d_wres], **CAST_KW)
        ln2 = kb.sb("ln2", [128, D], F32)
        bna = kb.sb("bna", [128, ATT], F32)
        cw = kb.sb("cw", [128, 2, 3], F32)
        dv = kb.sb("dv", [128, 2], F32)
        bg = kb.sb("bg", [128, 2], F32)
        bns = kb.sb("bns", [128, 2], F32)
        bnc = kb.sb("bnc", [128, 2], F32)
        d_vec = Dep()
        kb.dma("sp", ln2[:], ln2b[l], writes=[d_vec])
        kb.dma("sp", bna[:], bnatt[l], writes=[d_vec])
        kb.dma("sp", cw[:], convw[l], writes=[d_vec])
        kb.dma("sp", dv[:], dvec[l], writes=[d_vec])
        kb.dma("sp", bg[:], bglu[l], writes=[d_vec])
        kb.dma("sp", bns[:], bnssm[l], writes=[d_vec])
        kb.dma("sp", bnc[:], bnconv[l], writes=[d_vec])
        WBre = kb.sb("WBre", [128, 8, 128], BF16)
        WBim = kb.sb("WBim", [128, 8, 128], BF16)
        WCre = kb.sb("WCre", [128, 8, 128], BF16)
        WCimn = kb.sb("WCimn", [128, 8, 128], BF16)
        Ecos = kb.sb("Ecos", [128, 8, T], F32)
        Esin = kb.sb("Esin", [128, 8, T], F32)
        Mmul = kb.sb("Mmul", [128, 8, T], F32)
        rho = kb.sb("rho", [128, 8], F32)
        d_s5w = Dep()
        d_tab = Dep()
        PB_G = [0, 1, 2, 7]
        PB_ACC = [3, 4, 5, 6]
        rr = {"g": 0}

        def gbank():
            i = rr["g"]
            rr["g"] = (i + 1) % len(PB_G)
            return PB_G[i]

        m1 = kb.mark()
        sv = kb.sb("sv", [128, 24, 8], F32)
        d_sv = Dep()
        svi = kb.sb("svi", [128, 8], I32)
        LR, LI, DT, LDR, LDI, COS, SIN, AR, AI, DEN, NR, QR, QI, TMP, TMP2, NQI, ARG, KF = range(18)

        def V(i):
            return sv[:, i, :]

        def dve(fn):
            kb.op("dve", fn, reads=[d_sv], writes=[d_sv])

        def act(fn):
            kb.op("act", fn, reads=[d_sv], writes=[d_sv])
        kb.dma("sp", V(LR), lamre[l], writes=[d_sv])
        kb.dma("sp", V(LI), lamim[l], writes=[d_sv])
        kb.dma("sp", V(DT), logdt[l], writes=[d_sv])
        dve(lambda: nc.vector.tensor_scalar(out=V(LR), in0=V(LR), scalar1=-1e-4, scalar2=None, op0=ALU.min))
        act(lambda: nc.scalar.activation(out=V(DT), in_=V(DT), func=AF.Exp))
        dve(lambda: nc.vector.tensor_tensor(out=V(LDR), in0=V(LR), in1=V(DT), op=ALU.mult))
        dve(lambda: nc.vector.tensor_tensor(out=V(LDI), in0=V(LI), in1=V(DT), op=ALU.mult))
        act(lambda: nc.scalar.activation(out=rho[:], in_=V(LDR), func=AF.Exp))

        def sin_of(dst, shift):
            dve(lambda: nc.vector.tensor_scalar(out=V(ARG), in0=V(LDI), scalar1=float(shift), scalar2=None, op0=ALU.add))
            dve(lambda: nc.vector.tensor_scalar(out=V(KF), in0=V(ARG), scalar1=1.0 / TWO_PI, scalar2=None, op0=ALU.mult))
            dve(lambda: nc.vector.tensor_copy(out=svi[:], in_=V(KF)))
            dve(lambda: nc.vector.tensor_copy(out=V(KF), in_=svi[:]))
            dve(lambda: nc.vector.scalar_tensor_tensor(out=V(ARG), in0=V(KF), scalar=-TWO_PI, in1=V(ARG), op0=ALU.mult, op1=ALU.add))
            dve(lambda: nc.vector.tensor_scalar(out=V(KF), in0=V(ARG), scalar1=float(np.pi), scalar2=-TWO_PI, op0=ALU.is_gt, op1=ALU.mult))
            dve(lambda: nc.vector.tensor_tensor(out=V(ARG), in0=V(ARG), in1=V(KF), op=ALU.add))
            dve(lambda: nc.vector.tensor_scalar(out=V(KF), in0=V(ARG), scalar1=float(-np.pi), scalar2=TWO_PI, op0=ALU.is_lt, op1=ALU.mult))
            dve(lambda: nc.vector.tensor_tensor(out=V(ARG), in0=V(ARG), in1=V(KF), op=ALU.add))
            act(lambda: nc.scalar.activation(out=V(dst), in_=V(ARG), func=AF.Sin))
        sin_of(SIN, 0.0)
        sin_of(COS, np.pi / 2)
        dve(lambda: nc.vector.tensor_tensor(out=V(AR), in0=rho[:], in1=V(COS), op=ALU.mult))
        dve(lambda: nc.vector.tensor_tensor(out=V(AI), in0=rho[:], in1=V(SIN), op=ALU.mult))
        dve(lambda: nc.vector.tensor_tensor(out=V(DEN), in0=V(LR), in1=V(LR), op=ALU.mult))
        dve(lambda: nc.vector.tensor_tensor(out=V(TMP), in0=V(LI), in1=V(LI), op=ALU.mult))
        dve(lambda: nc.vector.tensor_tensor(out=V(DEN), in0=V(DEN), in1=V(TMP), op=ALU.add))
        dve(lambda: nc.vector.reciprocal(out=V(DEN), in_=V(DEN)))
        dve(lambda: nc.vector.tensor_scalar(out=V(NR), in0=V(AR), scalar1=-1.0, scalar2=None, op0=ALU.add))
        dve(lambda: nc.vector.tensor_tensor(out=V(TMP), in0=V(NR), in1=V(LR), op=ALU.mult))
        dve(lambda: nc.vector.tensor_tensor(out=V(TMP2), in0=V(AI), in1=V(LI), op=ALU.mult))
        dve(lambda: nc.vector.tensor_tensor(out=V(QR), in0=V(TMP), in1=V(TMP2), op=ALU.add))
        dve(lambda: nc.vector.tensor_tensor(out=V(QR), in0=V(QR), in1=V(DEN), op=ALU.mult))
        dve(lambda: nc.vector.tensor_tensor(out=V(TMP), in0=V(AI), in1=V(LR), op=ALU.mult))
        dve(lambda: nc.vector.tensor_tensor(out=V(TMP2), in0=V(NR), in1=V(LI), op=ALU.mult))
        dve(lambda: nc.vector.tensor_tensor(out=V(QI), in0=V(TMP), in1=V(TMP2), op=ALU.subtract))
        dve(lambda: nc.vector.tensor_tensor(out=V(QI), in0=V(QI), in1=V(DEN), op=ALU.mult))
        Bre = kb.sb("Bre", [128, 8, 128], F32)
        Bim = kb.sb("Bim", [128, 8, 128], F32)
        Bb = kb.sb("Bb", [128, 8, 128], F32)
        Bt = kb.sb("Bt", [128, 8, 128], F32)
        Cld = kb.sb("Cld", [128, 8, 128], F32)
        d_B = Dep()
        d_Bb = Dep()
        d_C = Dep()
        kb.dma("sp", Bre[:], bre_pad[l], writes=[d_B])
        kb.dma("sp", Bim[:], bim_pad[l], writes=[d_B])

        def bc8(i):
            return V(i).unsqueeze(2).broadcast_to([128, 8, 128])
        for comp in range(2):
            if comp == 0:
                kb.op("dve", lambda: nc.vector.tensor_tensor(out=Bb[:], in0=Bre[:], in1=bc8(QR), op=ALU.mult), reads=[d_B, d_sv], writes=[d_Bb])
                kb.op("dve", lambda: nc.vector.tensor_tensor(out=Bt[:], in0=Bim[:], in1=bc8(QI), op=ALU.mult), reads=[d_B, d_sv], writes=[d_Bb])
                kb.op("dve", lambda: nc.vector.tensor_tensor(out=Bb[:], in0=Bb[:], in1=Bt[:], op=ALU.subtract), reads=[d_Bb], writes=[d_Bb])
                dstW = WBre
            else:
                kb.op("dve", lambda: nc.vector.tensor_tensor(out=Bb[:], in0=Bim[:], in1=bc8(QR), op=ALU.mult), reads=[d_B, d_sv], writes=[d_Bb])
                kb.op("dve", lambda: nc.vector.tensor_tensor(out=Bt[:], in0=Bre[:], in1=bc8(QI), op=ALU.mult), reads=[d_B, d_sv], writes=[d_Bb])
                kb.op("dve", lambda: nc.vector.tensor_tensor(out=Bb[:], in0=Bb[:], in1=Bt[:], op=ALU.add), reads=[d_Bb], writes=[d_Bb])
                dstW = WBim
            for half in range(2):
                b = gbank()
                for jj in range(4):
                    j = half * 4 + jj
                    kb.op("pe", lambda j=j, jj=jj: nc.tensor.transpose(out=pbank[b][:, jj * 128:(jj + 1) * 128], in_=Bb[:, j, :],
                                                                       identity=ident_f[:]),
                          reads=[d_Bb, d_const], writes=[pdep[b]])
                kb.op("act", lambda half=half, dstW=dstW: nc.scalar.copy(
                    out=dstW[:, half * 4:(half + 1) * 4, :], in_=pbank[b][:, :].rearrange("p (j c) -> p j c", j=4)),
                    reads=[pdep[b]], writes=[d_s5w])
        kb.dma("sp", Cld[:], cre_pad[l], writes=[d_C])
        kb.op("act", lambda: nc.scalar.copy(out=WCre[:], in_=Cld[:]), reads=[d_C], writes=[d_s5w])
        kb.dma("sp", Cld[:], cim_pad[l], reads=[d_C], writes=[d_C])
        kb.op("act", lambda: nc.scalar.mul(out=WCimn[:], in_=Cld[:], mul=-1.0), reads=[d_C], writes=[d_s5w])
        kb.op("dve", lambda: nc.vector.tensor_copy(out=Ecos[:, :, 0], in_=V(COS)), reads=[d_sv], writes=[d_tab])
        kb.op("dve", lambda: nc.vector.tensor_copy(out=Esin[:, :, 0], in_=V(SIN)), reads=[d_sv], writes=[d_tab])
        tt = kb.sb("tt", [128, 4, 8, T // 2], F32)
        n = 1
        while n < T:
            cb = Ecos[:, :, n - 1:n].broadcast_to([128, 8, n])
            sbb = Esin[:, :, n - 1:n].broadcast_to([128, 8, n])
            kb.op("dve", lambda n=n, cb=cb: nc.vector.tensor_tensor(out=tt[:, 0, :, 0:n], in0=Ecos[:, :, 0:n], in1=cb, op=ALU.mult), reads=[d_tab], writes=[d_tab])
            kb.op("dve", lambda n=n, sbb=sbb: nc.vector.tensor_tensor(out=tt[:, 1, :, 0:n], in0=Esin[:, :, 0:n], in1=sbb, op=ALU.mult), reads=[d_tab], writes=[d_tab])
            kb.op("dve", lambda n=n, sbb=sbb: nc.vector.tensor_tensor(out=tt[:, 2, :, 0:n], in0=Ecos[:, :, 0:n], in1=sbb, op=ALU.mult), reads=[d_tab], writes=[d_tab])
            kb.op("dve", lambda n=n, cb=cb: nc.vector.tensor_tensor(out=tt[:, 3, :, 0:n], in0=Esin[:, :, 0:n], in1=cb, op=ALU.mult), reads=[d_tab], writes=[d_tab])
            kb.op("dve", lambda n=n: nc.vector.tensor_tensor(out=Ecos[:, :, n:2 * n], in0=tt[:, 0, :, 0:n], in1=tt[:, 1, :, 0:n], op=ALU.subtract), reads=[d_tab], writes=[d_tab])
            kb.op("dve", lambda n=n: nc.vector.tensor_tensor(out=Esin[:, :, n:2 * n], in0=tt[:, 2, :, 0:n], in1=tt[:, 3, :, 0:n], op=ALU.add), reads=[d_tab], writes=[d_tab])
            n *= 2
        kb.op("dve", lambda: nc.vector.tensor_copy(out=Mmul[:], in_=rho[:].unsqueeze(2).broadcast_to([128, 8, T])), reads=[d_sv], writes=[d_tab])
        kb.op("dve", lambda: nc.vector.memset(Mmul[:, :, 0:1], 0.0), reads=[d_tab], writes=[d_tab])
        kb.barrier()
        kb.release(m1)

        feat_r = Rot([kb.sb(f"featb{i}", [128, 8, T], F32) for i in range(2)])
        u_bf = kb.sb("u_bf", [128, 2, T], BF16)
        d_ubf = Dep()
        tmp = [kb.sb(f"s5t{i}", [128, 4, T], F32) for i in range(8)]
        d_tmp = [Dep() for _ in range(8)]
        hr_bf = kb.sb("hr_bf", [128, 4, T], BF16)
        hi_bf = kb.sb("hi_bf", [128, 4, T], BF16)
        d_hbf = Dep()
        hin = kb.sb("hin", [128, 2, 8], F32)
        d_hin = Dep()
        hsm = kb.sb("hsm", [128, 8, 8], F32)
        d_hsm = Dep()
        yv = [kb.sb(f"yv{i}", [128, 2, T], F32) for i in range(10)]
        d_yv = [Dep() for _ in range(10)]
        yY_r = Rot([yv[0], yv[7]])
        g_bf = kb.sb("g_bf", [128, 2, T], BF16)
        d_gbf = Dep()
        zp = kb.sb("zp", [128, 2, T + 2], F32)
        d_zp = Dep()
        rstd_r = Rot([kb.sb(f"rstdb{i}", [128, T], F32) for i in range(2)])
        att_r = Rot([kb.sb(f"attl{i}", [128, ATT], F32) for i in range(2)])
        attn_bf = kb.sb("attn_bf", [128, ATT], BF16)
        d_attn = Dep()
        mixT = kb.sb("mixT", [128, 8, T], BF16)
        d_mixT = Dep()
        xmid2 = [kb.sb(f"xmid{i}", [128, 4, D], F32) for i in range(2)]
        d_xmid2 = [[Dep() for _ in range(4)] for _ in range(2)]
        sqs = kb.sb("sqsb", [128, D], F32)
        d_sqs = Dep()
        stat_r = Rot([kb.sb(f"statb{i}", [128, 8], F32) for i in range(4)])
        h2 = kb.sb("h2", [128, D], BF16)
        d_h2 = Dep()
        h2T2 = [kb.sb(f"h2T{i}", [128, 8, 512], BF16) for i in range(2)]
        d_h2T2 = [[Dep() for _ in range(4)] for _ in range(2)]
        f2T = kb.sb("f2T", [128, 32, 512], BF16)
        d_f2T = [Dep() for _ in range(8)]
        wup_r = Rot([kb.sb(f"wupb{i}", [128, 8, 256], BF16) for i in range(2)])
        wdn_r = Rot([kb.sb(f"wdnb{i}", [128, 4, 512], BF16) for i in range(2)])
        relu_r = Rot([kb.sb(f"relub{i}", [128, 512], F32) for i in range(2)])
        oe_r = Rot([kb.sb(f"oeb{i}", [128, 512], F32) for i in range(2)])
        cst_out = kb.sb("cst_out", [128, 2, 2], F32)
        d_cst_out = Dep()

        def s5_head(tg, tok0, Tc, first, ctx):
            ft, d_ft = feat_r.next()
            ctx["ft"] = (ft, d_ft)
            yY, d_yY = yY_r.next()
            ctx["y"] = (yY, d_yY)
            kb.dma("sp", ft[:, :, 0:Tc], featT[l].rearrange("(f p) t -> p f t", p=128)[:, :, tok0:tok0 + Tc],
                   reads=[d_feat[l][tg]], writes=[d_ft])
            kb.op("act", lambda: nc.scalar.copy(out=u_bf[:, :, 0:Tc], in_=ft[:, 0:2, 0:Tc]), reads=[d_ft], writes=[d_ubf])
            for half in range(2):
                j0 = half * 4
                bre, bim = gbank(), gbank()
                for jj in range(4):
                    j = j0 + jj
                    kb.op("pe", lambda j=j, jj=jj: nc.tensor.matmul(pbank[bre][:, jj * T:jj * T + Tc], lhsT=WBre[:, j, :],
                                                                   rhs=u_bf[:, half, 0:Tc], start=True, stop=True),
                          reads=[d_s5w, d_ubf], writes=[pdep[bre]])
                for jj in range(4):
                    j = j0 + jj
                    kb.op("pe", lambda j=j, jj=jj: nc.tensor.matmul(pbank[bim][:, jj * T:jj * T + Tc], lhsT=WBim[:, j, :],
                                                                   rhs=u_bf[:, half, 0:Tc], start=True, stop=True),
                          reads=[d_s5w, d_ubf], writes=[pdep[bim]])
                pre = pbank[bre][:, :].rearrange("p (j t) -> p j t", j=4)[:, :, 0:Tc]
                pim = pbank[bim][:, :].rearrange("p (j t) -> p j t", j=4)[:, :, 0:Tc]
                ec = Ecos[:, j0:j0 + 4, 0:Tc]
                es = Esin[:, j0:j0 + 4, 0:Tc]
                t = [x[:, :, 0:Tc] for x in tmp]
                kb.op("dve", lambda: nc.vector.tensor_tensor(out=t[0], in0=pre, in1=ec, op=ALU.mult), reads=[pdep[bre], d_tab], writes=[d_tmp[0]])
                kb.op("dve", lambda: nc.vector.tensor_tensor(out=t[1], in0=pim, in1=es, op=ALU.mult), reads=[pdep[bim], d_tab], writes=[d_tmp[1]])
                kb.op("dve", lambda: nc.vector.tensor_tensor(out=t[2], in0=pim, in1=ec, op=ALU.mult), reads=[pdep[bim], d_tab], writes=[d_tmp[2]])
                kb.op("dve", lambda: nc.vector.tensor_tensor(out=t[3], in0=pre, in1=es, op=ALU.mult), reads=[pdep[bre], d_tab], writes=[d_tmp[3]])
                kb.op("dve", lambda: nc.vector.tensor_tensor(out=t[4], in0=t[0], in1=t[1], op=ALU.add), reads=[d_tmp[0], d_tmp[1]], writes=[d_tmp[4]])
                kb.op("dve", lambda: nc.vector.tensor_tensor(out=t[5], in0=t[2], in1=t[3], op=ALU.subtract), reads=[d_tmp[2], d_tmp[3]], writes=[d_tmp[5]])
                if not first:
                    kb.op("dve", lambda: nc.vector.tensor_tensor(out=hsm[:, 0, 0:4], in0=hin[:, 0, j0:j0 + 4], in1=rho[:, j0:j0 + 4], op=ALU.mult),
                          reads=[d_hin, d_tab], writes=[d_hsm])
                    kb.op("dve", lambda: nc.vector.tensor_tensor(out=hsm[:, 1, 0:4], in0=hin[:, 1, j0:j0 + 4], in1=rho[:, j0:j0 + 4], op=ALU.mult),
                          reads=[d_hin, d_tab], writes=[d_hsm])
                    kb.op("dve", lambda: nc.vector.tensor_tensor(out=tmp[4][:, :, 0], in0=tmp[4][:, :, 0], in1=hsm[:, 0, 0:4], op=ALU.add),
                          reads=[d_hsm, d_tmp[4]], writes=[d_tmp[4]])
                    kb.op("dve", lambda: nc.vector.tensor_tensor(out=tmp[5][:, :, 0], in0=tmp[5][:, :, 0], in1=hsm[:, 1, 0:4], op=ALU.add),
                          reads=[d_hsm, d_tmp[5]], writes=[d_tmp[5]])
                if Tc == T:
                    for (src, dst) in ((4, 6), (5, 7)):
                        kb.op("dve", lambda src=src, dst=dst: nc.vector.tensor_tensor_scan(
                            out=tmp[dst][:].rearrange("p j t -> p (j t)"), data0=Mmul[:, j0:j0 + 4, :].rearrange("p j t -> p (j t)"),
                            data1=tmp[src][:].rearrange("p j t -> p (j t)"), initial=0.0, op0=ALU.mult, op1=ALU.add),
                            reads=[d_tmp[src], d_tab], writes=[d_tmp[dst]])
                else:
                    for (src, dst) in ((4, 6), (5, 7)):
                        for jj in range(4):
                            kb.op("dve", lambda src=src, dst=dst, jj=jj: nc.vector.tensor_tensor_scan(
                                out=tmp[dst][:, jj, 0:Tc], data0=Mmul[:, j0 + jj, 0:Tc], data1=tmp[src][:, jj, 0:Tc],
                                initial=0.0, op0=ALU.mult, op1=ALU.add),
                                reads=[d_tmp[src], d_tab], writes=[d_tmp[dst]])
                kb.op("dve", lambda: nc.vector.tensor_tensor(out=t[0], in0=t[6], in1=ec, op=ALU.mult), reads=[d_tmp[6], d_tab], writes=[d_tmp[0]])
                kb.op("dve", lambda: nc.vector.tensor_tensor(out=t[1], in0=t[7], in1=es, op=ALU.mult), reads=[d_tmp[7], d_tab], writes=[d_tmp[1]])
                kb.op("dve", lambda: nc.vector.tensor_tensor(out=t[2], in0=t[7], in1=ec, op=ALU.mult), reads=[d_tmp[7], d_tab], writes=[d_tmp[2]])
                kb.op("dve", lambda: nc.vector.tensor_tensor(out=t[3], in0=t[6], in1=es, op=ALU.mult), reads=[d_tmp[6], d_tab], writes=[d_tmp[3]])
                kb.op("dve", lambda: nc.vector.tensor_tensor(out=hr_bf[:, :, 0:Tc], in0=t[0], in1=t[1], op=ALU.subtract),
                      reads=[d_tmp[0], d_tmp[1]], writes=[d_hbf])
                kb.op("dve", lambda: nc.vector.tensor_tensor(out=hi_bf[:, :, 0:Tc], in0=t[2], in1=t[3], op=ALU.add),
                      reads=[d_tmp[2], d_tmp[3]], writes=[d_hbf])
                kb.op("dve", lambda: nc.vector.tensor_tensor(out=hin[:, 0, j0:j0 + 4], in0=tmp[0][:, :, Tc - 1], in1=tmp[1][:, :, Tc - 1], op=ALU.subtract),
                      reads=[d_tmp[0], d_tmp[1], d_hsm], writes=[d_hin])
                kb.op("dve", lambda: nc.vector.tensor_tensor(out=hin[:, 1, j0:j0 + 4], in0=tmp[2][:, :, Tc - 1], in1=tmp[3][:, :, Tc - 1], op=ALU.add),
                      reads=[d_tmp[2], d_tmp[3], d_hsm], writes=[d_hin])
                yield
                by = gbank()
                for jj in range(4):
                    j = j0 + jj
                    kb.op("pe", lambda j=j, jj=jj: nc.tensor.matmul(pbank[by][:, 0:Tc], lhsT=WCre[:, j, :], rhs=hr_bf[:, jj, 0:Tc],
                                                                   start=(jj == 0), stop=False),
                          reads=[d_s5w, d_hbf], writes=[pdep[by]])
                    kb.op("pe", lambda j=j, jj=jj: nc.tensor.matmul(pbank[by][:, 0:Tc], lhsT=WCimn[:, j, :], rhs=hi_bf[:, jj, 0:Tc],
                                                                   start=False, stop=(jj == 3)),
                          reads=[d_s5w, d_hbf], writes=[pdep[by]])
                kb.op("dve", lambda: nc.vector.scalar_tensor_tensor(out=yY[:, half, 0:Tc], in0=ft[:, half, 0:Tc], scalar=dv[:, half:half + 1],
                                                                    in1=pbank[by][:, 0:Tc], op0=ALU.mult, op1=ALU.add),
                      reads=[d_ft, d_vec, pdep[by]], writes=[d_yY])

        def s5_tail(tg, tok0, Tc, ctx):
            c0 = tok0 - tg * 128
            ft, d_ft = ctx["ft"]
            yv[0], d_yv[0] = ctx["y"]
            Y, X2, Z, SG, G, GATE, SSMV, ZC, YC, CV = range(10)

            def yy(i):
                return yv[i][:, :, 0:Tc]
            kb.op("dve", lambda: nc.vector.tensor_tensor(out=yy(X2), in0=yy(Y), in1=yy(Y), op=ALU.mult), reads=[d_yv[Y]], writes=[d_yv[X2]])
            kb.op("dve", lambda: nc.vector.tensor_scalar(out=yy(X2), in0=yy(X2), scalar1=0.044715, scalar2=1.0, op0=ALU.mult, op1=ALU.add),
                  reads=[d_yv[X2]], writes=[d_yv[X2]])
            kb.op("dve", lambda: nc.vector.tensor_tensor(out=yy(Z), in0=yy(X2), in1=yy(Y), op=ALU.mult), reads=[d_yv[X2], d_yv[Y]], writes=[d_yv[Z]])
            kb.op("act", lambda: nc.scalar.activation(out=yy(SG), in_=yy(Z), func=AF.Sigmoid, scale=1.5957691216057308),
                  reads=[d_yv[Z]], writes=[d_yv[SG]])
            kb.op("dve", lambda: nc.vector.tensor_tensor(out=yy(G), in0=yy(SG), in1=yy(Y), op=ALU.mult), reads=[d_yv[SG], d_yv[Y]], writes=[d_yv[G]])
            kb.op("act", lambda: nc.scalar.copy(out=g_bf[:, :, 0:Tc], in_=yy(G)), reads=[d_yv[G]], writes=[d_gbf])
            yield
            for mo in range(2):
                bgt = gbank()
                for mi in range(2):
                    kb.op("pe", lambda mi=mi, mo=mo: nc.tensor.matmul(pbank[bgt][:, 0:Tc], lhsT=w_glu_sb[:, mi, mo * 128:(mo + 1) * 128],
                                                                     rhs=g_bf[:, mi, 0:Tc], start=(mi == 0), stop=(mi == 1)),
                          reads=[d_wres, d_gbf], writes=[pdep[bgt]])
                kb.op("act", lambda mo=mo: nc.scalar.activation(out=yv[GATE][:, mo, 0:Tc], in_=pbank[bgt][:, 0:Tc], func=AF.Sigmoid,
                                                                bias=bg[:, mo:mo + 1]),
                      reads=[pdep[bgt], d_vec], writes=[d_yv[GATE]])
            kb.op("dve", lambda: nc.vector.tensor_tensor(out=yy(SSMV), in0=yy(G), in1=yy(GATE), op=ALU.mult), reads=[d_yv[G], d_yv[GATE]], writes=[d_yv[SSMV]])
            kb.op("dve", lambda: nc.vector.tensor_tensor(out=zp[:, :, 2:2 + Tc], in0=ft[:, 6:8, 0:Tc], in1=ft[:, 2:4, 0:Tc], op=ALU.mult),
                  reads=[d_ft, d_zp], writes=[d_zp])
            for m in range(2):
                kb.op("dve", lambda m=m: nc.vector.tensor_scalar(out=yv[YC][:, m, 0:Tc], in0=zp[:, m, 0:Tc], scalar1=cw[:, m, 0:1], scalar2=None,
                                                                 op0=ALU.mult), reads=[d_zp, d_vec], writes=[d_yv[YC]])
                kb.op("dve", lambda m=m: nc.vector.scalar_tensor_tensor(out=yv[YC][:, m, 0:Tc], in0=zp[:, m, 1:1 + Tc], scalar=cw[:, m, 1:2],
                                                                        in1=yv[YC][:, m, 0:Tc], op0=ALU.mult, op1=ALU.add),
                      reads=[d_zp, d_vec, d_yv[YC]], writes=[d_yv[YC]])
                kb.op("dve", lambda m=m: nc.vector.scalar_tensor_tensor(out=yv[YC][:, m, 0:Tc], in0=zp[:, m, 2:2 + Tc], scalar=cw[:, m, 2:3],
                                                                        in1=yv[YC][:, m, 0:Tc], op0=ALU.mult, op1=ALU.add),
                      reads=[d_zp, d_vec, d_yv[YC]], writes=[d_yv[YC]])
            kb.op("dve", lambda: nc.vector.tensor_tensor(out=yy(CV), in0=yy(YC), in1=ft[:, 4:6, 0:Tc], op=ALU.mult), reads=[d_yv[YC], d_ft], writes=[d_yv[CV]])
            kb.op("dve", lambda: nc.vector.tensor_copy(out=hsm[:, 4:6, 0:2], in_=zp[:, :, Tc:Tc + 2]), reads=[d_zp], writes=[d_hsm])
            kb.op("dve", lambda: nc.vector.tensor_copy(out=zp[:, :, 0:2], in_=hsm[:, 4:6, 0:2]), reads=[d_hsm, d_zp], writes=[d_zp])
            for (src, bnv, k0) in ((SSMV, bns, 4), (CV, bnc, 6)):
                kb.op("act", lambda src=src: nc.scalar.activation(out=yy(X2), in_=yy(src), func=AF.Square), reads=[d_yv[src]], writes=[d_yv[X2]])
                yield
                bss = gbank()
                for m in range(2):
                    kb.op("pe", lambda m=m: nc.tensor.matmul(pbank[bss][:, 0:Tc], lhsT=ones_f[:], rhs=yv[X2][:, m, 0:Tc], start=(m == 0), stop=(m == 1)),
                          reads=[d_yv[X2], d_const], writes=[pdep[bss]])
                rs, d_rs = rstd_r.next()
                kb.op("act", lambda: nc.scalar.activation(out=rs[:, 0:Tc], in_=pbank[bss][:, 0:Tc], func=AF.Ln, scale=1.0 / SSM, bias=EPS),
                      reads=[pdep[bss]], writes=[d_rs])
                kb.op("act", lambda: nc.scalar.activation(out=rs[:, 0:Tc], in_=rs[:, 0:Tc], func=AF.Exp, scale=-0.5), reads=[d_rs], writes=[d_rs])
                for m in range(2):
                    kb.op("dve", lambda m=m, src=src, bnv=bnv, k0=k0: nc.vector.scalar_tensor_tensor(
                        out=mixT[:, k0 + m, c0:c0 + Tc], in0=yv[src][:, m, 0:Tc], scalar=bnv[:, m:m + 1], in1=rs[:, 0:Tc],
                        op0=ALU.mult, op1=ALU.mult),
                        reads=[d_yv[src], d_vec, d_rs], writes=[d_mixT])

        def token_part(tg, sub, par):
            xmid, d_xmid, h2T, d_h2T = xmid2[par], d_xmid2[par], h2T2[par], d_h2T2[par]
            at, d_at = att_r.next()
            kb.dma("sp", at[:], att_s[l][tg * 128:(tg + 1) * 128, :], reads=[d_att[l][tg]], writes=[d_at])
            kb.dma("sp", xmid[:, sub, :], x_src[tg * 128:(tg + 1) * 128, :], reads=[d_xsrc[tg]], writes=[d_xmid[sub]])
            st, d_st = stat_r.next()
            kb.op("act", lambda: nc.scalar.activation(out=sqs[:, 0:ATT], in_=at[:], func=AF.Square, accum_out=st[:, 0:1]),
                  reads=[d_at], writes=[d_sqs, d_st])
            kb.op("act", lambda: nc.scalar.activation(out=st[:, 1:2], in_=st[:, 0:1], func=AF.Ln, scale=1.0 / ATT, bias=EPS), reads=[d_st], writes=[d_st])
            kb.op("act", lambda: nc.scalar.activation(out=st[:, 2:3], in_=st[:, 1:2], func=AF.Exp, scale=-0.5), reads=[d_st], writes=[d_st])
            kb.op("dve", lambda: nc.vector.scalar_tensor_tensor(out=attn_bf[:], in0=at[:], scalar=st[:, 2:3], in1=bna[:], op0=ALU.mult, op1=ALU.mult),
                  reads=[d_at, d_st, d_vec], writes=[d_attn])
            yield
            b = gbank()
            pb16 = pbank[b][:].bitcast(BF16)
            for c in range(4):
                kb.op("pe", lambda c=c: nc.tensor.transpose(out=pb16[:, c * 128:(c + 1) * 128], in_=attn_bf[:, c * 128:(c + 1) * 128], identity=ident_b[:]),
                      reads=[d_attn, d_const], writes=[pdep[b]])
            kb.op("act", lambda: nc.scalar.copy(out=mixT[:, 0:4, :], in_=pb16[:, 0:512].rearrange("p (c t) -> p c t", c=4)),
                  reads=[pdep[b]], writes=[d_mixT])
            yield
            for nch in range(2):
                bo = gbank()
                for kt in range(8):
                    kb.op("pe", lambda kt=kt: nc.tensor.matmul(pbank[bo][:, :], lhsT=mixT[:, kt, :], rhs=w_out_sb[:, kt, nch * 512:(nch + 1) * 512],
                                                              start=(kt == 0), stop=(kt == 7)),
                          reads=[d_mixT, d_wres], writes=[pdep[bo]])
                kb.op("dve", lambda: nc.vector.tensor_tensor(out=xmid[:, sub, nch * 512:(nch + 1) * 512], in0=pbank[bo][:, :],
                                                             in1=xmid[:, sub, nch * 512:(nch + 1) * 512], op=ALU.add),
                      reads=[pdep[bo], d_xmid[sub]], writes=[d_xmid[sub]])
            st, d_st = stat_r.next()
            kb.op("act", lambda: nc.scalar.activation(out=sqs[:], in_=xmid[:, sub, :], func=AF.Square, accum_out=st[:, 0:1]),
                  reads=[d_xmid[sub]], writes=[d_sqs, d_st])
            kb.op("act", lambda: nc.scalar.activation(out=st[:, 1:2], in_=st[:, 0:1], func=AF.Ln, scale=1.0 / D, bias=EPS), reads=[d_st], writes=[d_st])
            kb.op("act", lambda: nc.scalar.activation(out=st[:, 2:3], in_=st[:, 1:2], func=AF.Exp, scale=-0.5), reads=[d_st], writes=[d_st])
            kb.op("dve", lambda: nc.vector.scalar_tensor_tensor(out=h2[:], in0=xmid[:, sub, :], scalar=st[:, 2:3], in1=ln2[:], op0=ALU.mult, op1=ALU.mult),
                  reads=[d_xmid[sub], d_st, d_vec], writes=[d_h2])
            yield
            b = gbank()
            pb16 = pbank[b][:].bitcast(BF16)
            for c in range(8):
                kb.op("pe", lambda c=c: nc.tensor.transpose(out=pb16[:, c * 128:(c + 1) * 128], in_=h2[:, c * 128:(c + 1) * 128], identity=ident_b[:]),
                      reads=[d_h2, d_const], writes=[pdep[b]])
            kb.op("act", lambda: nc.scalar.copy(out=h2T[:, :, sub * 128:(sub + 1) * 128], in_=pb16[:, 0:1024].rearrange("p (c t) -> p c t", c=8)),
                  reads=[pdep[b]], writes=[d_h2T[sub]])

        def ffn(tg0, nsub, par):
            xmid, d_xmid, h2T, d_h2T = xmid2[par], d_xmid2[par], h2T2[par], d_h2T2[par]
            W = nsub * 128
            subs = list(range(nsub))
            for fc in range(16):
                wu, d_wu = wup_r.next()
                kb.dma("pool", wu[:], wup_bf[l, fc], reads=[d_wup_bf[l]], writes=[d_wu])
                for i in range(2):
                    bu = gbank()
                    for kt in range(8):
                        kb.op("pe", lambda kt=kt, i=i: nc.tensor.matmul(pbank[bu][:, 0:W], lhsT=wu[:, kt, i * 128:(i + 1) * 128], rhs=h2T[:, kt, 0:W],
                                                                       start=(kt == 0), stop=(kt == 7)),
                              reads=[d_wu] + [d_h2T[s_] for s_ in subs], writes=[pdep[bu]])
                    rl, d_rl = relu_r.next()
                    kb.op("act", lambda: nc.scalar.activation(out=rl[:, 0:W], in_=pbank[bu][:, 0:W], func=AF.Relu), reads=[pdep[bu]], writes=[d_rl])
                    kb.op("act", lambda fc=fc, i=i: nc.scalar.activation(out=f2T[:, fc * 2 + i, 0:W], in_=rl[:, 0:W], func=AF.Square),
                          reads=[d_rl], writes=[d_f2T[fc // 2]])
                    yield
            for c in range(2):
                for ig in range(8):
                    wd, d_wd = wdn_r.next()
                    kb.dma("pool", wd[:], wdn_bf[l, c, ig], reads=[d_wdn_bf[l]], writes=[d_wd])
                    for ii in range(4):
                        for s_ in subs:
                            kb.op("pe", lambda ii=ii, s_=s_, ig=ig: nc.tensor.matmul(
                                pbank[PB_ACC[s_]][:, :], lhsT=f2T[:, ig * 4 + ii, s_ * 128:(s_ + 1) * 128], rhs=wd[:, ii, :],
                                start=(ig == 0 and ii == 0), stop=(ig == 7 and ii == 3)),
                                reads=[d_wd, d_f2T[ig]], writes=[pdep[PB_ACC[s_]]])
                        if ii % 2 == 1 and not (ig == 7 and ii == 3):
                            yield
                for s_ in subs:
                    oe, d_oe = oe_r.next()
                    kb.op("dve", lambda s_=s_: nc.vector.tensor_tensor(out=oe[:], in0=pbank[PB_ACC[s_]][:, :], in1=xmid[:, s_, c * 512:(c + 1) * 512], op=ALU.add),
                          reads=[pdep[PB_ACC[s_]], d_xmid[s_]], writes=[d_oe])
                    tg = tg0 + s_
                    kb.dma("sp", x_dst[tg * 128:(tg + 1) * 128, c * 512:(c + 1) * 512], oe[:], reads=[d_oe], writes=[d_xdst[tg]])
                yield

        def store_hin(idx):
            kb.dma("sp", sre_out[l, idx], hin[:, 0, :], reads=[d_hin])
            kb.dma("sp", sim_out[l, idx], hin[:, 1, :], reads=[d_hin])

        def store_conv(idx):
            kb.op("dve", lambda: nc.vector.tensor_copy(out=cst_out[:], in_=zp[:, :, 0:2]), reads=[d_zp], writes=[d_cst_out])
            kb.dma("sp", conv_out[l, idx], cst_out[:], reads=[d_cst_out])

        kb.op("dve", lambda: nc.vector.memset(zp[:, :, 0:2], 0.0), writes=[d_zp])

        def chunk_list(Qm):
            out = []
            if Qm < NQ:
                par = Qm % 2
                for sub in range(4):
                    tg = 4 * Qm + sub
                    ctx = {}

                    def head(tg=tg, ctx=ctx):
                        yield from s5_head(tg, tg * 128, T, tg == 0, ctx)
                        if tg == NTP - 1:
                            store_hin(0)

                    def tail(tg=tg, ctx=ctx, sub=sub):
                        yield from s5_tail(tg, tg * 128, T, ctx)
                        if tg == NTP - 1:
                            store_conv(0)
                        yield from token_part(tg, sub, par)
                    out.append((head, tail))
            else:
                par = NQ % 2
                for s in range(2):
                    ctx = {}

                    def head(s=s, ctx=ctx):
                        kb.dma("sp", hin[:, 0, :], sre[l, s], reads=[d_hin], writes=[d_hin])
                        kb.dma("sp", hin[:, 1, :], sim[l, s], reads=[d_hin], writes=[d_hin])
                        yield from s5_head(NTP, NTP * 128 + s * 64, 64, False, ctx)
                        store_hin(1 + s)

                    def tail(s=s, ctx=ctx):
                        kb.dma("sp", zp[:, :, 0:2], sconv[l, s], reads=[d_zp], writes=[d_zp])
                        yield from s5_tail(NTP, NTP * 128 + s * 64, 64, ctx)
                        store_conv(1 + s)
                        if s == 1:
                            yield from token_part(NTP, 0, par)
                    out.append((head, tail))
            return out

        def cosched(chunks, gffn, K=1):
            n = len(chunks)
            hc, tc = 0, 0
            gh = chunks[0][0]() if n else None
            gt = None
            head_done = [False] * n
            while hc < n or tc < n or gffn is not None:
                for _ in range(K):
                    if gffn is not None:
                        try:
                            next(gffn)
                        except StopIteration:
                            gffn = None
                if hc < n and hc <= tc + 1:
                    try:
                        next(gh)
                    except StopIteration:
                        head_done[hc] = True
                        hc += 1
                        gh = chunks[hc][0]() if hc < n else None
                if tc < n and head_done[tc]:
                    if gt is None:
                        gt = chunks[tc][1]()
                    try:
                        next(gt)
                    except StopIteration:
                        tc += 1
                        gt = None

        cosched(chunk_list(0), None)
        for Qm in range(NQ):
            cosched(chunk_list(Qm + 1), ffn(4 * Qm, 4, Qm % 2), K=(2 if Qm + 1 < NQ else 4))
        cosched([], ffn(NTP, 1, NQ % 2))
        kb.barrier()
        kb.release(m0)
    d_xin = [Dep() for _ in range(NT)]
    d_y = [Dep() for _ in range(NT)]
    try:
        for l in range(nlayers):
            x_src, d_src = (x_all, d_xin) if l == 0 else (x1, d_x1)
            x_dst, d_dst = (y_out, d_y) if l == nlayers - 1 else (x1, d_x1)
            phase_a(l, x_src, d_src)
            stg(100 + 10 * l)
            phase_b(l, x_src, d_src, x_dst, d_dst)
            stg(101 + 10 * l)
    except StopBuild:
        pass
    kb.finish()
    print("instructions:", kb.ninst)
    return nc


def host_constants():
    k = np.arange(128)
    c = {}
    c["c_ident"] = np.eye(128, dtype=np.float32)
    c["c_utri"] = (k[:, None] <= k[None, :]).astype(np.float32)
    blk = (k[:, None] // 64) == (k[None, :] // 64)
    c["c_utri2"] = ((k[:, None] <= k[None, :]) & blk).astype(np.float32)
    c["c_ones"] = np.ones((128, 128), np.float32)
    c["c_lstrict"] = (k[:, None] > k[None, :]).astype(np.float32)
    c["c_maskT"] = np.where(k[:, None] <= k[None, :], 0.0, NEG).astype(np.float32)
    q = np.arange(64)
    m2 = np.full((2, 128, 64), NEG, np.float32)
    for s in range(2):
        kk = np.arange(64)
        m2[s, s * 64:(s + 1) * 64, :] = np.where(kk[:, None] <= q[None, :], 0.0, NEG)
    c["c_mask2"] = m2
    return c


def _state_layout(a):
    nl, S = a.shape[0], a.shape[1]
    return np.ascontiguousarray(a.reshape(nl, S, 8, 2, 64).transpose(0, 1, 3, 4, 2).reshape(nl, S, 128, 8))


def _state_unlayout(a):
    nl, S = a.shape[0], a.shape[1]
    return np.ascontiguousarray(a.reshape(nl, S, 2, 64, 8).transpose(0, 1, 4, 2, 3).reshape(nl, S, 16, 64))


def _fm2(a):
    nl = a.shape[0]
    return np.ascontiguousarray(a.reshape(nl, 2, 128).transpose(0, 2, 1))


def prep_shared(inp):
    f = np.float32
    nl = inp["w_in"].shape[0]
    d = {}
    for k in ["w_in", "w_out", "w_up", "w_down", "w_glu"]:
        d[k] = np.ascontiguousarray(inp[k], dtype=f)
    d["ln1b"] = np.ascontiguousarray(np.broadcast_to(inp["ln1_w"][:, None, :], (nl, 128, D)), dtype=f)
    d["ln2b"] = np.ascontiguousarray(np.broadcast_to(inp["ln2_w"][:, None, :], (nl, 128, D)), dtype=f)
    bn = np.asarray(inp["branch_norm_w"], dtype=f)
    d["bnatt"] = np.ascontiguousarray(np.broadcast_to(bn[:, None, :ATT], (nl, 128, ATT)), dtype=f)
    d["qnb"] = np.ascontiguousarray(np.broadcast_to(np.tile(inp["q_norm_w"], (1, NH))[:, None, :], (nl, 128, ATT)), dtype=f)
    d["knb"] = np.ascontiguousarray(np.broadcast_to(np.tile(inp["k_norm_w"], (1, NH))[:, None, :], (nl, 128, ATT)), dtype=f)
    d["bfb"] = np.ascontiguousarray(np.broadcast_to(inp["b_forget"][:, None, :], (nl, 128, NH)), dtype=f)
    cw = np.asarray(inp["conv_w"], dtype=f)
    d["convw"] = np.ascontiguousarray(cw.reshape(nl, 3, 2, 128).transpose(0, 3, 2, 1))
    d["dvec"] = _fm2(np.asarray(inp["ssm_d"], dtype=f))
    d["bglu"] = _fm2(np.asarray(inp["b_glu"], dtype=f))
    d["bnssm"] = _fm2(bn[:, ATT:ATT + SSM])
    d["bnconv"] = _fm2(bn[:, ATT + SSM:])
    d["lamre"] = _state_layout(np.asarray(inp["ssm_lam_re"], dtype=f)[:, None])[:, 0]
    d["lamim"] = _state_layout(np.asarray(inp["ssm_lam_im"], dtype=f)[:, None])[:, 0]
    ldt = np.broadcast_to(np.asarray(inp["ssm_log_dt"], dtype=f)[:, :, None], (nl, 16, 64))
    d["logdt"] = _state_layout(np.ascontiguousarray(ldt)[:, None])[:, 0]

    def pad_gpc(a):
        out = np.zeros((nl, 128, 8, 128), f)
        for g in range(16):
            out[:, (g % 2) * 64:(g % 2) * 64 + 64, g // 2, (g % 8) * 16:(g % 8) * 16 + 16] = a[:, g]
        return out
    d["bre_pad"] = pad_gpc(np.asarray(inp["ssm_b_re"], dtype=f))
    d["bim_pad"] = pad_gpc(np.asarray(inp["ssm_b_im"], dtype=f))
    d["cre_pad"] = pad_gpc(np.asarray(inp["ssm_c_re"], dtype=f).transpose(0, 1, 3, 2))
    d["cim_pad"] = pad_gpc(np.asarray(inp["ssm_c_im"], dtype=f).transpose(0, 1, 3, 2))
    d.update(host_constants())
    return d


def prep_core(inp, c, L, P):
    f = np.float32
    nl = inp["w_in"].shape[0]
    d = {}
    xs = np.asarray(inp["x_sample"][2 * c:2 * c + 2], dtype=f).reshape(128, D)
    d["x_all"] = np.ascontiguousarray(np.concatenate([np.asarray(inp["x_prompt"][c, :L], dtype=f), xs], axis=0))
    d["ck"] = np.ascontiguousarray(np.asarray(inp["cache_k"][:, 2 * c:2 * c + 2, :P], dtype=f).reshape(nl, 2, P, ATT))
    d["cv"] = np.ascontiguousarray(np.asarray(inp["cache_v"][:, 2 * c:2 * c + 2, :P], dtype=f).reshape(nl, 2, P, ATT))
    d["clf"] = np.ascontiguousarray(np.asarray(inp["cache_logf"][:, 2 * c:2 * c + 2, :P], dtype=f))
    d["sre"] = _state_layout(np.asarray(inp["state_ssm_re"][:, 2 * c:2 * c + 2], dtype=f))
    d["sim"] = _state_layout(np.asarray(inp["state_ssm_im"][:, 2 * c:2 * c + 2], dtype=f))
    sc = np.asarray(inp["state_conv"][:, 2 * c:2 * c + 2], dtype=f)
    d["sconv"] = np.ascontiguousarray(sc.reshape(nl, 2, 2, 2, 128).transpose(0, 1, 4, 3, 2))
    return d


def assemble(results, L):
    n = len(results)
    nl = results[0]["k_out"].shape[0]
    f = np.float32
    y = np.stack([r["y_out"][:L] for r in results]).astype(f)
    ys = np.concatenate([r["y_out"][L:].reshape(2, 64, D) for r in results]).astype(f)

    def tok(name, w):
        p = np.stack([r[name][:, :L] for r in results], axis=1)
        s_ = np.concatenate([r[name][:, L:].reshape(nl, 2, 64, w) for r in results], axis=1)
        return p, s_
    kp, ks = tok("k_out", ATT)
    vp, vs = tok("v_out", ATT)
    lp, ls = tok("lf_out", NH)
    kp = kp.reshape(nl, n, L, NH, DH); ks = ks.reshape(nl, 2 * n, 64, NH, DH)
    vp = vp.reshape(nl, n, L, NH, DH); vs = vs.reshape(nl, 2 * n, 64, NH, DH)

    def st(name):
        a = np.stack([r[name] for r in results], axis=1)
        p = _state_unlayout(a[:, :, 0])
        s_ = _state_unlayout(a[:, :, 1:3].reshape(nl, 2 * n, 128, 8))
        return p, s_
    rp, rs = st("sre_out")
    ip, is_ = st("sim_out")
    c = np.stack([r["conv_out"] for r in results], axis=1)
    c = c.transpose(0, 1, 2, 5, 4, 3).reshape(nl, n, 3, 2, 256)
    cp = c[:, :, 0]
    cs = c[:, :, 1:3].reshape(nl, 2 * n, 2, 256)
    return tuple(np.ascontiguousarray(a, dtype=f) for a in
                 (y, ys, kp, vp, lp, rp, ip, cp, ks, vs, ls, rs, is_, cs))


_PROG = {}


def kernel(**inputs):
    L, P = 4096, 4096
    inp = {k: np.asarray(v) for k, v in inputs.items()}
    if "nc" not in _PROG:
        _PROG["nc"] = build_program(L, P)
    nc = _PROG["nc"]
    shared = prep_shared(inp)
    in_maps = []
    for c in range(8):
        m = dict(shared)
        m.update(prep_core(inp, c, L, P))
        in_maps.append(m)
    res = run_bass_kernel_spmd(nc, in_maps, core_ids=list(range(8)))
    return assemble(res.results, L)
```
